# Optimizing a Trainium2 kernel written in Bass

```python
import functools
import numpy as np
import jax
import jax.numpy as jnp
from jax import lax

D_MODEL = 2048
BATCH = 8
SEQ = 2048
DEPTH = 1
DEC_BATCH = 32
DEC_SEQ = 4
PAST_LEN = 8192
PAGE_SIZE = 128

D_CONV = 1024
CONV_W = 3
N_HEADS = 16
N_KV_HEADS = 4
HEAD_DIM = 128
GROUP = N_HEADS // N_KV_HEADS
ROT_DIM = HEAD_DIM // 4
N_IDX_HEADS = 16
IDX_DIM = 64
IDX_ROT_DIM = IDX_DIM // 4
TOPK_MAX = 256
ROPE_THETA = 500000.0
D_FF = 5632
Q_BLOCK = 128
LN_EPS = 1e-5
DEEPNORM_ALPHA = (2.0 * DEPTH) ** 0.25
DEEPNORM_BETA = (8.0 * DEPTH) ** -0.25
D_Q = N_HEADS * HEAD_DIM
D_KV = N_KV_HEADS * HEAD_DIM
D_IQ = N_IDX_HEADS * IDX_DIM
SPLIT_SIZES = (D_CONV, D_CONV, D_CONV, D_Q, D_KV, D_KV, D_IQ, IDX_DIM, N_IDX_HEADS, D_MODEL, D_MODEL)
D_IN = sum(SPLIT_SIZES)

kernel_name = 'dsa_shortconv_parallel_hybrid_step'


def _layer_norm(x, g, b):
    xf = x.astype(jnp.float32)
    mu = jnp.mean(xf, axis=-1, keepdims=True)
    var = jnp.mean(jnp.square(xf - mu), axis=-1, keepdims=True)
    y = (xf - mu) * lax.rsqrt(var + LN_EPS) * g.astype(jnp.float32) + b.astype(jnp.float32)
    return y.astype(x.dtype)


def _partial_rope(x, pos, rot_dim):
    half = rot_dim // 2
    inv_freq = ROPE_THETA ** (-jnp.arange(half, dtype=jnp.float32) / half)
    ang = pos.astype(jnp.float32)[:, None] * inv_freq[None, :]
    cos = jnp.cos(ang)[None, :, None, :]
    sin = jnp.sin(ang)[None, :, None, :]
    xr = x[..., :rot_dim].astype(jnp.float32)
    x1, x2 = xr[..., :half], xr[..., half:]
    rot = jnp.concatenate([x1 * cos - x2 * sin, x2 * cos + x1 * sin], axis=-1).astype(x.dtype)
    return jnp.concatenate([rot, x[..., rot_dim:]], axis=-1)


def _causal_dwconv(u, hist, w):
    T = u.shape[1]
    ext = jnp.concatenate([hist, u], axis=1)
    y = ext[:, 0:T] * w[0]
    for j in range(1, CONV_W):
        y = y + ext[:, j:j + T] * w[j]
    return y, ext[:, -(CONV_W - 1):]


def _index_scores(iq, ik, iw):
    s = jnp.einsum('bthd,bsd->bths', iq, ik, preferred_element_type=jnp.float32)
    return jnp.einsum('bths,bth->bts', jax.nn.relu(s * IDX_DIM ** -0.5), iw.astype(jnp.float32))


def _gather_rows(rows, idx):
    return jax.vmap(lambda r, i: r[i])(rows, idx)


def _sparse_attend(q, k_sel, v_sel, valid):
    B, T = q.shape[:2]
    qg = q.reshape(B, T, N_KV_HEADS, GROUP, HEAD_DIM)
    logits = jnp.einsum('btkgd,btskd->btkgs', qg, k_sel, preferred_element_type=jnp.float32) * HEAD_DIM ** -0.5
    logits = jnp.where(valid[:, :, None, None, :], logits, -jnp.inf)
    p = jax.nn.softmax(logits, axis=-1).astype(v_sel.dtype)
    o = jnp.einsum('btkgs,btskd->btkgd', p, v_sel)
    return o.reshape(B, T, D_Q)


def _prompt_attention(q, k, v, iq, ik, iw):
    B, T = q.shape[:2]
    topk = min(TOPK_MAX, T // 4)
    key_pos = jnp.arange(T, dtype=jnp.int32)

    def block(i):
        start = i * Q_BLOCK
        qb = lax.dynamic_slice_in_dim(q, start, Q_BLOCK, axis=1)
        iqb = lax.dynamic_slice_in_dim(iq, start, Q_BLOCK, axis=1)
        iwb = lax.dynamic_slice_in_dim(iw, start, Q_BLOCK, axis=1)
        qpos = start + jnp.arange(Q_BLOCK, dtype=jnp.int32)
        scores = _index_scores(iqb, ik, iwb)
        scores = jnp.where((key_pos[None, :] <= qpos[:, None])[None], scores, -jnp.inf)
        _, idx = lax.top_k(scores, topk)
        valid = idx <= qpos[None, :, None]
        return _sparse_attend(qb, _gather_rows(k, idx), _gather_rows(v, idx), valid)

    out = lax.map(block, jnp.arange(T // Q_BLOCK, dtype=jnp.int32))
    return jnp.swapaxes(out, 0, 1).reshape(B, T, D_Q)


def _sample_attention(q, k, v, iq, ik, iw, cache_k, cache_v, cache_idx_k, page_table, layer):
    B, T = q.shape[:2]
    n_pages = PAST_LEN // PAGE_SIZE
    L = PAST_LEN + T
    topk = min(TOPK_MAX, L // 4)
    ik_past = cache_idx_k[layer, page_table].reshape(B, n_pages * PAGE_SIZE, IDX_DIM)
    ik_all = jnp.concatenate([ik_past, ik], axis=1)
    qpos = PAST_LEN + jnp.arange(T, dtype=jnp.int32)
    scores = _index_scores(iq, ik_all, iw)
    scores = jnp.where((jnp.arange(L, dtype=jnp.int32)[None, :] <= qpos[:, None])[None], scores, -jnp.inf)
    _, idx = lax.top_k(scores, topk)
    valid = idx <= qpos[None, :, None]
    in_past = (idx < PAST_LEN)[..., None, None]
    pidx = jnp.minimum(idx, PAST_LEN - 1)
    phys = jax.vmap(lambda pt, i: pt[i])(page_table, pidx // PAGE_SIZE)
    off = pidx % PAGE_SIZE
    nidx = jnp.clip(idx - PAST_LEN, 0, T - 1)
    k_sel = jnp.where(in_past, cache_k[layer, phys, off], _gather_rows(k, nidx))
    v_sel = jnp.where(in_past, cache_v[layer, phys, off], _gather_rows(v, nidx))
    return _sparse_attend(q, k_sel, v_sel, valid)


def _layer(x, pos, attend, conv_a_hist, ffn_hist, w_in, idx_k_norm_g, idx_k_norm_b, conv_a_w, w_a_out,
           w_attn_out, w_mix_out, ln1_g, ln1_b, w_up, w_gate, conv_ffn_w, conv_ffn_b, w_down, ln2_g, ln2_b):
    B, T, _ = x.shape
    z = jnp.einsum('btd,de->bte', x, w_in)
    points = [int(p) for p in np.cumsum(SPLIT_SIZES)[:-1]]
    cb, cc, ch, q, k, v, iq, ik, iw, ga, gb = jnp.split(z, points, axis=-1)
    y_conv, new_conv_a = _causal_dwconv(cc * ch, conv_a_hist, conv_a_w)
    y_a = jnp.einsum('btc,cd->btd', cb * y_conv, w_a_out)
    q = _partial_rope(q.reshape(B, T, N_HEADS, HEAD_DIM), pos, ROT_DIM)
    k = _partial_rope(k.reshape(B, T, N_KV_HEADS, HEAD_DIM), pos, ROT_DIM)
    v = v.reshape(B, T, N_KV_HEADS, HEAD_DIM)
    iq = _partial_rope(iq.reshape(B, T, N_IDX_HEADS, IDX_DIM), pos, IDX_ROT_DIM)
    ik = _partial_rope(_layer_norm(ik, idx_k_norm_g, idx_k_norm_b)[:, :, None, :], pos, IDX_ROT_DIM)[:, :, 0, :]
    iw = iw * N_IDX_HEADS ** -0.5
    y_b = jnp.einsum('bte,ed->btd', attend(q, k, v, iq, ik, iw), w_attn_out)
    m = jax.nn.sigmoid(ga) * y_a + jax.nn.sigmoid(gb) * y_b
    h = _layer_norm(DEEPNORM_ALPHA * x + jnp.einsum('btd,de->bte', m, w_mix_out), ln1_g, ln1_b)
    u = jnp.einsum('btd,df->btf', h, w_up)
    g = jnp.einsum('btd,df->btf', h, w_gate)
    uc, new_conv_ffn = _causal_dwconv(u, ffn_hist, conv_ffn_w)
    f = jnp.einsum('btf,fd->btd', jax.nn.gelu(uc + conv_ffn_b) * g, w_down)
    out = _layer_norm(DEEPNORM_ALPHA * h + f, ln2_g, ln2_b)
    return out, k, v, ik, new_conv_a, new_conv_ffn


def setup_inputs(seed: int = 0) -> dict:
    key = jax.random.key(seed)
    ks = jax.random.split(key, 26)
    n_pages = PAST_LEN // PAGE_SIZE
    n_phys = (DEC_BATCH * n_pages * 5) // 4

    def nrm(k, shape, scale):
        return jax.random.normal(k, shape, jnp.float32) * scale

    x_prompt = nrm(ks[0], (BATCH, SEQ, D_MODEL), 1.0)
    x_sample = nrm(ks[1], (DEC_BATCH, DEC_SEQ, D_MODEL), 1.0)
    cache_k = nrm(ks[2], (DEPTH, n_phys, PAGE_SIZE, N_KV_HEADS, HEAD_DIM), 1.0)
    cache_v = nrm(ks[3], (DEPTH, n_phys, PAGE_SIZE, N_KV_HEADS, HEAD_DIM), 1.0)
    cache_idx_k = nrm(ks[4], (DEPTH, n_phys, PAGE_SIZE, IDX_DIM), 1.0)
    state_conv_a = nrm(ks[5], (DEPTH, DEC_BATCH, CONV_W - 1, D_CONV), 1.0)
    state_conv_ffn = nrm(ks[6], (DEPTH, DEC_BATCH, CONV_W - 1, D_FF), 1.0)
    page_table = jax.random.permutation(ks[7], n_phys)[:DEC_BATCH * n_pages].reshape(DEC_BATCH, n_pages).astype(jnp.int32)
    col_scale = jnp.concatenate([
        jnp.ones((3 * D_CONV + D_Q + D_KV,), jnp.float32),
        jnp.full((D_KV,), DEEPNORM_BETA, dtype=jnp.float32),
        jnp.ones((D_IQ + IDX_DIM + N_IDX_HEADS + 2 * D_MODEL,), jnp.float32)])
    w_in = nrm(ks[8], (DEPTH, D_MODEL, D_IN), D_MODEL ** -0.5) * col_scale
    idx_k_norm_g = 1.0 + nrm(ks[9], (DEPTH, IDX_DIM), 0.02)
    idx_k_norm_b = nrm(ks[10], (DEPTH, IDX_DIM), 0.02)
    conv_a_w = nrm(ks[11], (DEPTH, CONV_W, D_CONV), CONV_W ** -0.5)
    w_a_out = nrm(ks[12], (DEPTH, D_CONV, D_MODEL), DEEPNORM_BETA * D_CONV ** -0.5)
    w_attn_out = nrm(ks[13], (DEPTH, D_Q, D_MODEL), DEEPNORM_BETA * D_Q ** -0.5)
    w_mix_out = nrm(ks[14], (DEPTH, D_MODEL, D_MODEL), DEEPNORM_BETA * D_MODEL ** -0.5)
    ln1_g = 1.0 + nrm(ks[15], (DEPTH, D_MODEL), 0.02)
    ln1_b = nrm(ks[16], (DEPTH, D_MODEL), 0.02)
    w_up = nrm(ks[17], (DEPTH, D_MODEL, D_FF), D_MODEL ** -0.5)
    w_gate = nrm(ks[18], (DEPTH, D_MODEL, D_FF), D_MODEL ** -0.5)
    conv_ffn_w = nrm(ks[19], (DEPTH, CONV_W, D_FF), CONV_W ** -0.5)
    conv_ffn_b = nrm(ks[20], (DEPTH, D_FF), 0.02)
    w_down = nrm(ks[21], (DEPTH, D_FF, D_MODEL), DEEPNORM_BETA * D_FF ** -0.5)
    ln2_g = 1.0 + nrm(ks[22], (DEPTH, D_MODEL), 0.02)
    ln2_b = nrm(ks[23], (DEPTH, D_MODEL), 0.02)
    return {'x_prompt': x_prompt, 'x_sample': x_sample, 'cache_k': cache_k, 'cache_v': cache_v,
            'cache_idx_k': cache_idx_k, 'state_conv_a': state_conv_a, 'state_conv_ffn': state_conv_ffn,
            'page_table': page_table, 'w_in': w_in, 'idx_k_norm_g': idx_k_norm_g, 'idx_k_norm_b': idx_k_norm_b,
            'conv_a_w': conv_a_w, 'w_a_out': w_a_out, 'w_attn_out': w_attn_out, 'w_mix_out': w_mix_out,
            'ln1_g': ln1_g, 'ln1_b': ln1_b, 'w_up': w_up, 'w_gate': w_gate, 'conv_ffn_w': conv_ffn_w,
            'conv_ffn_b': conv_ffn_b, 'w_down': w_down, 'ln2_g': ln2_g, 'ln2_b': ln2_b}


def reference(x_prompt, x_sample, cache_k, cache_v, cache_idx_k, state_conv_a, state_conv_ffn, page_table,
              w_in, idx_k_norm_g, idx_k_norm_b, conv_a_w, w_a_out, w_attn_out, w_mix_out, ln1_g, ln1_b,
              w_up, w_gate, conv_ffn_w, conv_ffn_b, w_down, ln2_g, ln2_b):
    bp, tp = x_prompt.shape[:2]
    bs, ts = x_sample.shape[:2]
    pos_p = jnp.arange(tp, dtype=jnp.int32)
    pos_s = PAST_LEN + jnp.arange(ts, dtype=jnp.int32)
    hp, hs = x_prompt, x_sample
    kp, vp, ikp, cap, cfp = [], [], [], [], []
    kss, vss, iks, cas, cfs = [], [], [], [], []
    for l in range(DEPTH):
        wts = (w_in[l], idx_k_norm_g[l], idx_k_norm_b[l], conv_a_w[l], w_a_out[l], w_attn_out[l], w_mix_out[l],
               ln1_g[l], ln1_b[l], w_up[l], w_gate[l], conv_ffn_w[l], conv_ffn_b[l], w_down[l], ln2_g[l], ln2_b[l])
        zero_a = jnp.zeros((bp, CONV_W - 1, D_CONV), hp.dtype)
        zero_f = jnp.zeros((bp, CONV_W - 1, D_FF), hp.dtype)
        hp, k1, v1, ik1, ca1, cf1 = _layer(hp, pos_p, _prompt_attention, zero_a, zero_f, *wts)
        kp.append(k1); vp.append(v1); ikp.append(ik1); cap.append(ca1); cfp.append(cf1)
        attend_s = functools.partial(_sample_attention, cache_k=cache_k, cache_v=cache_v,
                                     cache_idx_k=cache_idx_k, page_table=page_table, layer=l)
        hs, k2, v2, ik2, ca2, cf2 = _layer(hs, pos_s, attend_s, state_conv_a[l], state_conv_ffn[l], *wts)
        kss.append(k2); vss.append(v2); iks.append(ik2); cas.append(ca2); cfs.append(cf2)
    return (hp, hs, jnp.stack(kp), jnp.stack(vp), jnp.stack(ikp), jnp.stack(cap), jnp.stack(cfp),
            jnp.stack(kss), jnp.stack(vss), jnp.stack(iks), jnp.stack(cas), jnp.stack(cfs))
```

```python
import os
from contextlib import ExitStack

import numpy as np
import concourse.bass as bass
import concourse.mybir as mybir
from concourse.bass_utils import run_bass_kernel_spmd

F32 = mybir.dt.float32
BF16 = mybir.dt.bfloat16
I32 = mybir.dt.int32
AF = mybir.ActivationFunctionType
ALU = mybir.AluOpType
AX = mybir.AxisListType

D = 2048
KC = 16
T = 2048
TB = 512
NBLK = 4
NSEQ = 4
TS = 4
NS = NSEQ * TS
NTOT = T + NS
DCONV = 1024
DFF = 5632
NFC = 44
NH = 16
NKV = 4
HD = 128
NIH = 16
IDIM = 64
PAST = 8192
NPG = 64
NPHYS = 2560
TOPK = 256
ALPHA = 2.0 ** 0.25
EPS = 1e-5
NEG = -1.0e30
NEGBIG = -3.0e38
NBIS = 27
TOPK_BISECT = True

TI_CONV, TI_Q, TI_K, TI_V, TI_IQ, TI_GA, TI_GB = 0, 24, 40, 44, 48, 56, 72
N_WIN_TILES = 88
WIN_SPLIT = (24, 16, 8, 8, 16, 16)


class _Stop(Exception):
    pass


class Buf:
    __slots__ = ("name", "w", "r", "rd", "excl")

    def __init__(self, name, excl=False):
        self.name = name
        self.w = []
        self.r = {}
        self.rd = []
        self.excl = excl


class Ins:
    __slots__ = ("eng", "fn", "deps", "needs_inc", "val", "dma_key", "dma_val", "is_dma", "acc")

    def __init__(self, eng, fn, is_dma=False, dma_key=None):
        self.acc = False
        self.eng = eng
        self.fn = fn
        self.deps = []
        self.needs_inc = False
        self.val = 0
        self.is_dma = is_dma
        self.dma_key = dma_key
        self.dma_val = 0


class Prog:
    ENGS = ("pe", "act", "dve", "pool", "sp")

    def __init__(self, nc):
        self.nc = nc
        self.streams = {e: [] for e in self.ENGS}
        self.same = {"act", "dve", "pool"}
        self.dma_keys = {}
        self.last_dma = {}
        self.all = []

    def emit(self, eng, fn, reads=(), writes=(), dma_key=None, acc=False):
        is_dma = dma_key is not None
        ins = Ins(eng, fn, is_dma, dma_key)
        ins.acc = acc
        deps = []
        for b in reads:
            deps.extend(b.w)
            if b.excl:
                deps.extend(v for k, v in b.r.items() if k != eng)
        for b in writes:
            if acc:
                deps.extend(w for w in b.w if not w.acc)
            else:
                deps.extend(b.w)
            deps.extend(b.r.values())
            deps.extend(b.rd)
        seen = set()
        for d in deps:
            if id(d) in seen:
                continue
            seen.add(id(d))
            if (not d.is_dma) and d.eng == eng and eng not in self.same:
                continue
            ins.deps.append(d)
        for b in writes:
            if acc:
                b.w = [w for w in b.w if w.acc] + [ins]
            else:
                b.w = [ins]
                b.r = {}
                b.rd = []
        for b in reads:
            if b in writes:
                continue
            if is_dma:
                b.rd.append(ins)
            else:
                b.r[eng] = ins
        if is_dma:
            self.dma_keys[dma_key] = self.dma_keys.get(dma_key, 0) + 1
            ins.dma_val = 16 * self.dma_keys[dma_key]
            self.last_dma[dma_key] = ins
        self.streams[eng].append(ins)
        self.all.append(ins)
        return ins

    def barrier(self):
        lasts = []
        for e in self.ENGS:
            for ins in reversed(self.streams[e]):
                if not ins.is_dma and ins.fn is not None:
                    lasts.append(ins)
                    break
        lasts.extend(self.last_dma.values())
        for e in self.ENGS:
            ins = Ins(e, None)
            ins.deps = list(lasts)
            self.streams[e].append(ins)
            self.all.append(ins)

    def finalize(self, stack):
        nc = self.nc
        for ins in self.all:
            for d in ins.deps:
                if not d.is_dma:
                    d.needs_inc = True
        sems = {e: stack.enter_context(nc.semaphore(f"s_{e}")) for e in self.ENGS}
        ksems = {k: stack.enter_context(nc.semaphore(f"k_{i}")) for i, k in enumerate(self.dma_keys)}
        for e in self.ENGS:
            c = 0
            for ins in self.streams[e]:
                if ins.needs_inc and not ins.is_dma:
                    c += 1
                    ins.val = c
        final = [(ksems[k], 16 * n) for k, n in self.dma_keys.items()]
        block = stack.enter_context(nc.Block())
        streams = self.streams

        def run(e, h):
            seen = {}
            for ins in streams[e]:
                need = {}
                for d in ins.deps:
                    if d.is_dma:
                        s, v = ksems[d.dma_key], d.dma_val
                    else:
                        s, v = sems[d.eng], d.val
                    cur = need.get(id(s))
                    if cur is None or cur[1] < v:
                        need[id(s)] = (s, v)
                for sid, (s, v) in need.items():
                    if seen.get(sid, 0) >= v:
                        continue
                    seen[sid] = v
                    h.wait_ge(s, v)
                if ins.fn is None:
                    continue
                r = ins.fn(h)
                if ins.is_dma:
                    r.then_inc(ksems[ins.dma_key], 16)
                elif ins.needs_inc:
                    r.then_inc(sems[e], 1)
            if e == "sp":
                for s, v in final:
                    h.wait_ge(s, v)

        @block.tensor
        def _(h):
            run("pe", h)

        @block.scalar
        def _(h):
            run("act", h)

        @block.vector
        def _(h):
            run("dve", h)

        @block.gpsimd
        def _(h):
            run("pool", h)

        @block.sync
        def _(h):
            run("sp", h)


def build_program(nblk_run=NBLK, do_sample=True, nphys=NPHYS, stop=None):
    def ck(i):
        if stop is not None and stop == i:
            raise _Stop()

    nc = bass.Bass("TRN2", target_bir_lowering=False)
    st = ExitStack()

    def din(name, shape, dt=F32):
        return nc.dram_tensor(name, list(shape), dt, kind="ExternalInput").ap()

    def dout(name, shape, dt=F32):
        return nc.dram_tensor(name, list(shape), dt, kind="ExternalOutput").ap()

    xT_d = din("xT", [D, NTOT])
    xtok_d = din("xtok", [NTOT, D])
    w_in_parts = [din(f"w_in_t{i}", [n, 128, KC * 128]) for i, n in enumerate(WIN_SPLIT)]

    class _WIn:
        def __getitem__(self, idx):
            for i, n in enumerate(WIN_SPLIT):
                if idx < n:
                    return w_in_parts[i][idx]
                idx -= n
            raise IndexError
    w_in_d = _WIn()
    w_ikw_d = din("w_ikw", [128, KC * 80])
    w_a_d = din("w_a_t", [16, 128, 8 * 128])
    w_at_d = din("w_attn_t", [16, 128, KC * 128])
    w_mix_d = din("w_mix_t", [16, 128, KC * 128])
    w_up_parts = [din(f"w_up_t{i}", [22, 128, KC * 128]) for i in range(2)]
    w_gate_parts = [din(f"w_gate_t{i}", [22, 128, KC * 128]) for i in range(2)]
    w_dn_parts = [din(f"w_down_t{i}", [16, 128, 22 * 128]) for i in range(2)]

    class _Split2:
        def __init__(self, parts):
            self.parts = parts
        def __getitem__(self, idx):
            return self.parts[idx // 22][idx % 22]
    w_up_d = _Split2(w_up_parts)
    w_gate_d = _Split2(w_gate_parts)

    class _Dn:
        def __getitem__(self, jh):
            j, hf = jh
            return w_dn_parts[hf][j]
    w_dn_d = _Dn()
    cbf_d = din("cbf", [128, 4 * 128])
    cf_d = din("cf", [128, 3 * 128])
    small_d = din("small", [128, 8 * 3 + NFC * 3 + NFC])
    ikgb_d = din("ikgb", [2, IDIM])
    ln_d = din("lnp", [4, D])
    rope_fm_d = din("rope_fm", [128, 4, NTOT])
    rope_tok_d = din("rope_tok", [NTOT, 16])
    sca_d = din("sca", [128, 8 * NSEQ * 2])
    scf_d = din("scf", [128, NFC * NSEQ * 2])
    pt_d = din("pt", [NSEQ * NPG], I32)
    ckv_d = din("cache_kv", [nphys * 128, 2 * NKV * HD])
    cik_d = din("cache_ik", [nphys * 128, IDIM])

    y_o = dout("y_o", [NTOT, D])
    kT_o = dout("kT_o", [NKV, 128, NTOT])
    vT_o = dout("vT_o", [NKV, 128, NTOT])
    ik_o = dout("ik_o", [NTOT, IDIM])
    ca_o = dout("ca_o", [128, 8 * 2])
    cf_o = dout("cf_o", [128, NFC * 2])
    cas_o = dout("cas_o", [128, 8 * NSEQ * 2])
    cfs_o = dout("cfs_o", [128, NFC * NSEQ * 2])

    def sb(name, shape, dt):
        return st.enter_context(nc.sbuf_tensor(name, list(shape), dt))

    def ps(name, shape, dt):
        return st.enter_context(nc.psum_tensor(name, list(shape), dt))

    P = Prog(nc)
    E = P.emit

    KT = sb("KT", [128, NKV, NTOT], BF16)
    V = sb("V", [128, 16, NKV * HD], BF16)
    VS = sb("VS", [128, NSEQ, NKV * HD], BF16)
    IKT = sb("IKT", [128, NTOT], BF16)
    NWS = 3
    WSLOT = 22 * 128
    WR = sb("WR", [128, NWS, WSLOT], BF16)
    CB = sb("CB", [128, 4, 128], BF16)
    CF = sb("CF", [128, 3, 128], F32)
    SM = sb("SM", [128, 8 * 3 + NFC * 3 + NFC], F32)
    IKGB = sb("IKGB", [128, 2, IDIM], F32)
    UH = sb("UH", [128, NFC, 2], F32)
    AH = sb("AH", [128, 8, 2], F32)
    SCA = sb("SCA", [128, 8, NSEQ, 2], F32)
    SCF = sb("SCF", [128, NFC, NSEQ, 2], F32)
    CAS = sb("CAS", [128, 8, NSEQ, 2], F32)
    CFS = sb("CFS", [128, NFC, NSEQ, 2], F32)
    QS = sb("QS", [128, NSEQ, NH, TS], BF16)
    WT = sb("WT", [128, 5, NIH], F32)
    XT = sb("XT", [128, KC, TB + NS], BF16)
    AT = sb("AT", [128, 8, TB + NS], BF16)
    OT = sb("OT", [128, KC, TB + NS], BF16)
    NTB = TB + NS
    RA = sb("RA", [128, 10304], F32)
    RC = sb("RC", [128, 5808], F32)
    RD = sb("RD", [128, 4096], F32)
    RS = sb("RS", [128, 4096], F32) if do_sample else None

    identb = CB[:, 0, :]
    pmqk = CB[:, 1, :]
    pmiq = CB[:, 2, :]
    onesb = CB[:, 3, :]
    identf = CF[:, 0, :]
    cneg = CF[:, 1, :]
    misc = CF[:, 2, :]
    CAW = SM[:, 0:24].rearrange("p (c j) -> p c j", j=3)
    CFW = SM[:, 24:24 + NFC * 3].rearrange("p (c j) -> p c j", j=3)
    CFB = SM[:, 24 + NFC * 3:24 + NFC * 4]

    def carve(region, off, shape, dt):
        n = 1
        for s in shape[1:]:
            n *= s
        if dt == F32:
            v = region[:, off:off + n]
            words = n
        else:
            words = (n + 1) // 2
            v = region[:, off:off + words].bitcast(BF16)[:, 0:n]
        if len(shape) == 2:
            return v, off + words
        names = " ".join(f"d{i}" for i in range(len(shape) - 1))
        kw = {f"d{i}": shape[i + 1] for i in range(len(shape) - 1)}
        return v.rearrange(f"p ({names}) -> p {names}", **kw), off + words

    o = 0
    MASKT, o = carve(RA, o, [128, 16, TB], BF16)
    ISC, o = carve(RA, o, [128, 2048], F32)
    RTMP, o = carve(RA, o, [128, 2, 512], F32)
    MB, o = carve(RA, o, [128, 2048], BF16)
    ROPE, o = carve(RA, o, [128, 4, NTB], F32)
    assert o <= 10304, o
    TOK = RA[:, 0:5 * D].rearrange("p (t d) -> p t d", d=D)
    o = 0
    IQT, o = carve(RC, o, [128, 8, NTB], BF16)
    QB, o = carve(RC, o, [128, 2, NTB], BF16)
    PTT, o = carve(RC, o, [128, 4, 512], BF16)
    LNL, o = carve(RC, o, [128, 2, 512], F32)
    VTB, o = carve(RC, o, [128, NTB], BF16)
    IKD, o = carve(RC, o, [128, 2, 128], BF16)
    IKTOK, o = carve(RC, o, [128, 2, 96], F32)
    assert o <= 5808, o
    MT = RC[:, 0:(KC * NTB) // 2].bitcast(BF16).rearrange("p (k t) -> p k t", t=NTB)
    ACTT = RC[:, 0:(22 * NTB) // 2].bitcast(BF16).rearrange("p (k t) -> p k t", t=NTB)
    o = 0
    STG4, o = carve(RD, o, [128, 4, NTB], F32)
    TA, o = carve(RD, o, [128, 544], F32)
    TB1, o = carve(RD, o, [128, NTB], F32)
    TB2, o = carve(RD, o, [128, NTB], F32)
    SMALLT, o = carve(RD, o, [128, 384], F32)
    assert o <= 4096, o
    HB = STG4
    LNB = OT[:, :, :].rearrange("p k t -> p (k t)")[:, 0:8192].bitcast(F32).rearrange("p (a d) -> p a d", d=D)

    pb = [ps(f"pb{i}", [128, 512], F32) for i in range(6)]
    ptb = [ps(f"ptb{i}", [128, 512], F32) for i in range(2)]
    Bpb = [Buf(f"pb{i}", excl=True) for i in range(6)]
    Bptb = [Buf(f"ptb{i}", excl=True) for i in range(2)]

    class Ctr:
        bank = 0
        tb = 0
        ws = 0

    def next_bank():
        i = Ctr.bank % 5
        Ctr.bank += 1
        return pb[i], Bpb[i]

    def next_tb():
        i = Ctr.tb % 2
        Ctr.tb += 1
        return ptb[i], Bptb[i]

    Bws = [Buf(f"ws{i}") for i in range(NWS)]

    def wload(src_ap, nelem):
        i = Ctr.ws % NWS
        Ctr.ws += 1
        E("pool", lambda h, i=i, src_ap=src_ap, nelem=nelem: h.dma_start(out=WR[:, i, 0:nelem], in_=src_ap),
          writes=[Bws[i]], dma_key=f"w{i}")
        return WR[:, i, :], Bws[i]

    B = {}

    def bf(name):
        if name not in B:
            B[name] = Buf(name)
        return B[name]

    E("pool", lambda h: h.dma_start(out=CB[:, :, :].rearrange("p a b -> p (a b)"), in_=cbf_d), writes=[bf("CB")], dma_key="cb")
    E("sp", lambda h: h.dma_start(out=CF[:, :, :].rearrange("p a b -> p (a b)"), in_=cf_d), writes=[bf("CF")], dma_key="cf")
    E("sp", lambda h: h.dma_start(out=SM[:, :], in_=small_d), writes=[bf("SM")], dma_key="sm")
    E("sp", lambda h: h.dma_start(out=IKGB[:, :, :].rearrange("p a b -> p (a b)"),
                                  in_=ikgb_d.rearrange("a b -> (a b)").partition_broadcast(128)),
      writes=[bf("IKGB")], dma_key="ikgb")
    E("sp", lambda h: h.dma_start(out=SCA[:, :, :, :].rearrange("p a b c -> p (a b c)"), in_=sca_d), writes=[bf("SCA")], dma_key="sca")
    E("sp", lambda h: h.dma_start(out=SCF[:, :, :, :].rearrange("p a b c -> p (a b c)"), in_=scf_d), writes=[bf("SCF")], dma_key="scf")
    E("dve", lambda h: h.memset(UH[:, :, :], 0.0), writes=[bf("UH")])
    E("dve", lambda h: h.memset(AH[:, :, :], 0.0), writes=[bf("AH")])

    CONSTS = [bf("CB"), bf("CF")]

    def proj(w_src, kc, M, srcf, src_bufs, segs, epi, fixed_bank=None):
        wt, wb = wload(w_src, kc * M)
        for si, (c0, n) in enumerate(segs):
            bank, bb = next_bank() if fixed_bank is None else (pb[fixed_bank], Bpb[fixed_bank])
            for k in range(kc):
                E("pe", lambda h, bank=bank, wt=wt, k=k, c0=c0, n=n, M=M, kc=kc:
                  h.matmul(bank[0:M, 0:n], lhsT=wt[:, k * M:(k + 1) * M], rhs=srcf(k, c0, n),
                           start=(k == 0), stop=(k == kc - 1)),
                  reads=[wb] + src_bufs, writes=[bb])
            epi(si, c0, n, bank, bb)

    def rope_fm(bank, bb, n, xb_view, xb_buf, pm, ctab, stab, f32_out=None, f32_buf=None):
        lvl = int(os.environ.get("KDBG_ROPE", 9))
        E("act", lambda h: h.copy(out=xb_view, in_=bank[:, 0:n]), reads=[bb], writes=[xb_buf])
        if lvl < 1:
            return
        pbk, pbb = pb[5], Bpb[5]
        E("pe", lambda h: h.matmul(pbk[:, 0:n], lhsT=pm, rhs=xb_view, start=True, stop=True),
          reads=[xb_buf] + CONSTS, writes=[pbb])
        if lvl < 2:
            return
        var = os.environ.get("KDBG_VAR", "")
        if var == "A":
            E("dve", lambda h: h.tensor_tensor(out=TB1[:, 0:n], in0=ctab, in1=bank[:, 0:n], op=ALU.mult),
              reads=[bb, bf("ROPE")], writes=[bf("TB1")])
        elif var == "B":
            E("act", lambda h: h.copy(out=TB1[:, 0:n], in_=bank[:, 0:n]), reads=[bb], writes=[bf("TB1")])
            E("dve", lambda h: h.tensor_tensor(out=TB1[:, 0:n], in0=TB1[:, 0:n], in1=ctab, op=ALU.mult),
              reads=[bf("ROPE")], writes=[bf("TB1")])
        elif var == "C":
            E("dve", lambda h: h.tensor_copy(out=TB1[:, 0:n], in_=ctab), reads=[bf("ROPE")], writes=[bf("TB1")])
        else:
            E("dve", lambda h: h.tensor_tensor(out=TB1[:, 0:n], in0=bank[:, 0:n], in1=ctab, op=ALU.mult),
              reads=[bb, bf("ROPE")], writes=[bf("TB1")])
        if lvl < 3:
            return
        E("dve", lambda h: h.tensor_tensor(out=TB2[:, 0:n], in0=pbk[:, 0:n], in1=stab, op=ALU.mult),
          reads=[pbb, bf("ROPE")], writes=[bf("TB2")])
        if lvl < 4:
            return
        if f32_out is None:
            E("dve", lambda h: h.tensor_tensor(out=xb_view, in0=TB1[:, 0:n], in1=TB2[:, 0:n], op=ALU.add),
              reads=[bf("TB1"), bf("TB2")], writes=[xb_buf])
        else:
            E("dve", lambda h: h.tensor_tensor(out=f32_out, in0=TB1[:, 0:n], in1=TB2[:, 0:n], op=ALU.add),
              reads=[bf("TB1"), bf("TB2")], writes=[f32_buf])
            E("act", lambda h: h.copy(out=xb_view, in_=f32_out), reads=[f32_buf], writes=[xb_buf])

    def layer_norm_rows(x_view, ntp, ncol, xbuf, tag):
        nch = (ncol + 511) // 512
        STv = SMALLT[:, 0:nch * 6].rearrange("p (a s) -> p a s", s=6)
        MV = SMALLT[:, 32:34]
        RSv = SMALLT[:, 34:35]
        for a in range(nch):
            w = min(512, ncol - a * 512)
            E("dve", lambda h, a=a, w=w: h.bn_stats(out=STv[0:ntp, a, :], in_=x_view[0:ntp, a * 512:a * 512 + w]),
              reads=[xbuf], writes=[bf("ST" )])
        E("dve", lambda h: h.bn_aggr(out=MV[0:ntp, :], in_=STv[0:ntp, :, :]), reads=[bf("ST")], writes=[bf("MV")])
        E("dve", lambda h: h.tensor_scalar(out=RSv[0:ntp, :], in0=MV[0:ntp, 1:2], scalar1=EPS, scalar2=None, op0=ALU.add),
          reads=[bf("MV")], writes=[bf("RSv")])
        E("act", lambda h: h.activation(out=RSv[0:ntp, :], in_=RSv[0:ntp, :], func=AF.Sqrt), reads=[bf("RSv")], writes=[bf("RSv")])
        E("dve", lambda h: h.reciprocal(out=RSv[0:ntp, :], in_=RSv[0:ntp, :]), reads=[bf("RSv")], writes=[bf("RSv")])
        return MV, RSv

    def do_block(b):
      if True:
        last = (b == NBLK - 1)
        tok0 = b * TB
        ntok = TB + (NS if last else 0)
        segs = [(0, TB)] + ([(TB, NS)] if last else [])
        ntiles = 4 + (1 if last else 0)

        def gcols(c0, n):
            return (tok0 + c0, n) if c0 < TB else (T + (c0 - TB), n)

        def tile_rows(tt):
            return (tok0 + tt * 128, 128) if tt < 4 else (T, NS)

        xT_v = xT_d.rearrange("(k p) t -> p k t", p=128)
        E("pool", lambda h: h.dma_start(out=XT[:, :, 0:TB], in_=xT_v[:, :, tok0:tok0 + TB]), writes=[bf("XT")], dma_key="xt")
        E("sp", lambda h: h.dma_start(out=ROPE[:, :, 0:TB], in_=rope_fm_d[:, :, tok0:tok0 + TB]), writes=[bf("ROPE")], dma_key="rope")
        if last:
            E("pool", lambda h: h.dma_start(out=XT[:, :, TB:NTB], in_=xT_v[:, :, T:NTOT]), writes=[bf("XT")], dma_key="xt")
            E("sp", lambda h: h.dma_start(out=ROPE[:, :, TB:NTB], in_=rope_fm_d[:, :, T:NTOT]), writes=[bf("ROPE")], dma_key="rope")
        ROPT = SMALLT[:, 64:64 + 5 * 16].rearrange("p (t c) -> p t c", c=16)
        for tt in range(ntiles):
            r0, nr = tile_rows(tt)
            E("sp", lambda h, tt=tt, r0=r0, nr=nr: h.dma_start(out=ROPT[0:nr, tt, :], in_=rope_tok_d[r0:r0 + nr, :]),
              writes=[bf("ROPT")], dma_key="ropt")

        xsrc = lambda k, c0, n: XT[:, k, c0:c0 + n]
        XB = [bf("XT")]

        ck(1)
        PA = TA[:, 0:TB + 2]
        PAS = TA[:, 520:544].rearrange("p (s t) -> p s t", t=6)
        ACCS = TB1[:, TB:NTB].rearrange("p (s t) -> p s t", t=TS)
        for c in range(8):
            def epi_cc(si, c0, n, bank, bb):
                E("act", lambda h: h.copy(out=TB2[:, c0:c0 + n], in_=bank[:, 0:n]), reads=[bb], writes=[bf("TB2")])
            proj(w_in_d[TI_CONV + 3 * c + 0], KC, 128, xsrc, XB, segs, epi_cc)
            E("act", lambda h, c=c: h.copy(out=PA[:, 0:2], in_=AH[:, c, :]), reads=[bf("AH")], writes=[bf("TA")])
            if last:
                E("act", lambda h, c=c: h.copy(out=PAS[:, :, 0:2], in_=SCA[:, c, :, :]), reads=[bf("SCA")], writes=[bf("TA")])

            def epi_ch(si, c0, n, bank, bb):
                if si == 0:
                    E("dve", lambda h: h.tensor_tensor(out=PA[:, 2:2 + TB], in0=bank[:, 0:TB], in1=TB2[:, 0:TB], op=ALU.mult),
                      reads=[bb, bf("TB2")], writes=[bf("TA")])
                else:
                    E("dve", lambda h: h.tensor_tensor(out=PAS[:, :, 2:6],
                                                        in0=bank[:, 0:NS].rearrange("p (s t) -> p s t", t=TS),
                                                        in1=TB2[:, TB:NTB].rearrange("p (s t) -> p s t", t=TS), op=ALU.mult),
                      reads=[bb, bf("TB2")], writes=[bf("TA")])
            proj(w_in_d[TI_CONV + 3 * c + 1], KC, 128, xsrc, XB, segs, epi_ch)
            E("dve", lambda h, c=c: h.tensor_scalar(out=TB1[:, 0:TB], in0=PA[:, 2:2 + TB], scalar1=CAW[:, c, 2:3], scalar2=None, op0=ALU.mult),
              reads=[bf("TA"), bf("SM")], writes=[bf("TB1")])
            E("dve", lambda h, c=c: h.scalar_tensor_tensor(out=TB1[:, 0:TB], in0=PA[:, 1:1 + TB], scalar=CAW[:, c, 1:2], in1=TB1[:, 0:TB], op0=ALU.mult, op1=ALU.add),
              reads=[bf("TA"), bf("SM")], writes=[bf("TB1")])
            E("dve", lambda h, c=c: h.scalar_tensor_tensor(out=TB1[:, 0:TB], in0=PA[:, 0:TB], scalar=CAW[:, c, 0:1], in1=TB1[:, 0:TB], op0=ALU.mult, op1=ALU.add),
              reads=[bf("TA"), bf("SM")], writes=[bf("TB1")])
            if last:
                E("dve", lambda h, c=c: h.tensor_scalar(out=ACCS, in0=PAS[:, :, 2:6], scalar1=CAW[:, c, 2:3], scalar2=None, op0=ALU.mult),
                  reads=[bf("TA"), bf("SM")], writes=[bf("TB1")])
                E("dve", lambda h, c=c: h.scalar_tensor_tensor(out=ACCS, in0=PAS[:, :, 1:5], scalar=CAW[:, c, 1:2], in1=ACCS, op0=ALU.mult, op1=ALU.add),
                  reads=[bf("TA"), bf("SM")], writes=[bf("TB1")])
                E("dve", lambda h, c=c: h.scalar_tensor_tensor(out=ACCS, in0=PAS[:, :, 0:4], scalar=CAW[:, c, 0:1], in1=ACCS, op0=ALU.mult, op1=ALU.add),
                  reads=[bf("TA"), bf("SM")], writes=[bf("TB1")])
                E("act", lambda h, c=c: h.copy(out=CAS[:, c, :, :], in_=PAS[:, :, 4:6]), reads=[bf("TA")], writes=[bf("CAS")])
            E("act", lambda h, c=c: h.copy(out=AH[:, c, :], in_=PA[:, TB:TB + 2]), reads=[bf("TA")], writes=[bf("AH")])

            def epi_cb(si, c0, n, bank, bb, c=c):
                E("dve", lambda h: h.tensor_tensor(out=AT[:, c, c0:c0 + n], in0=bank[:, 0:n], in1=TB1[:, c0:c0 + n], op=ALU.mult),
                  reads=[bb, bf("TB1")], writes=[bf("AT")])
            proj(w_in_d[TI_CONV + 3 * c + 2], KC, 128, xsrc, XB, segs, epi_cb)

        ck(2)
        for g in range(int(os.environ.get('KDBG_NG', NKV))):
            def epi_k(si, c0, n, bank, bb, g=g):
                gc0, _ = gcols(c0, n)
                stg = STG4[:, g % 2, c0:c0 + n]
                sbuf = bf(f"STG{g % 2}")
                rope_fm(bank, bb, n, KT[:, g, gc0:gc0 + n], bf("KT"), pmqk, ROPE[:, 0, c0:c0 + n], ROPE[:, 1, c0:c0 + n],
                        f32_out=stg, f32_buf=sbuf)
                if not os.environ.get('KDBG_NOKDMA'):
                    E("sp", lambda h: h.dma_start(out=kT_o[g, :, gc0:gc0 + n], in_=stg), reads=[sbuf], dma_key=f"o_stg{g % 2}")
            proj(w_in_d[TI_K + g], KC, 128, xsrc, XB, segs, epi_k)

        ck(3)
        for g in range(NKV):
            def epi_v(si, c0, n, bank, bb, g=g):
                gc0, _ = gcols(c0, n)
                stg = STG4[:, 2 + g % 2, c0:c0 + n]
                sbuf = bf(f"STG{2 + g % 2}")
                E("act", lambda h: h.copy(out=stg, in_=bank[:, 0:n]), reads=[bb], writes=[sbuf])
                E("dve", lambda h: h.tensor_copy(out=VTB[:, c0:c0 + n], in_=bank[:, 0:n]), reads=[bb], writes=[bf("VTB")])
                E("sp", lambda h: h.dma_start(out=vT_o[g, :, gc0:gc0 + n], in_=stg), reads=[sbuf], dma_key=f"o_stg{2 + g % 2}")
                if si == 0:
                    tbk, tbb = next_tb()
                    for tt in range(4):
                        E("pe", lambda h, tt=tt: h.matmul(tbk[:, tt * 128:(tt + 1) * 128], lhsT=VTB[:, tt * 128:(tt + 1) * 128], rhs=identb, start=True, stop=True),
                          reads=[bf("VTB")] + CONSTS, writes=[tbb])
                    E("act", lambda h: h.copy(out=V[:, b * 4:b * 4 + 4, g * HD:(g + 1) * HD],
                                              in_=tbk[:, 0:512].rearrange("p (t d) -> p t d", d=128)),
                      reads=[tbb], writes=[bf("V")])
                else:
                    for s_ in range(NSEQ):
                        tbk, tbb = next_tb()
                        E("pe", lambda h, s_=s_, tbk=tbk: h.matmul(tbk[0:TS, 0:128], lhsT=VTB[:, TB + s_ * TS:TB + (s_ + 1) * TS], rhs=identb, start=True, stop=True),
                          reads=[bf("VTB")] + CONSTS, writes=[tbb])
                        E("act", lambda h, s_=s_, tbk=tbk: h.copy(out=VS[0:TS, s_, g * HD:(g + 1) * HD], in_=tbk[0:TS, 0:128]),
                          reads=[tbb], writes=[bf("VS")])
            proj(w_in_d[TI_V + g], KC, 128, xsrc, XB, segs, epi_v)

        ck(4)
        IKS = STG4[:, 0, :]
        def epi_ikw(si, c0, n, bank, bb):
            E("act", lambda h: h.copy(out=IKS[0:80, c0:c0 + n], in_=bank[0:80, 0:n]), reads=[bb], writes=[bf("STG0")])
        proj(w_ikw_d, KC, 80, xsrc, XB, segs, epi_ikw)
        for tt in range(ntiles):
            r0, nr = tile_rows(tt)
            lc0 = tt * 128
            pk, pkb = pb[5], Bpb[5]
            E("pe", lambda h, lc0=lc0, nr=nr: h.transpose(pk[0:nr, 0:80], IKS[0:80, lc0:lc0 + nr], identf[0:80, 0:80]),
              reads=[bf("STG0")] + CONSTS, writes=[pkb])
            MV, RSv = layer_norm_rows(pk, nr, IDIM, pkb, "ik")
            ikt = IKTOK[:, tt % 2, :]
            ikb_ = bf(f"IKTOK{tt % 2}")
            E("dve", lambda h, nr=nr, ikt=ikt: h.tensor_scalar(out=ikt[0:nr, 0:IDIM], in0=pk[0:nr, 0:IDIM], scalar1=MV[0:nr, 0:1], scalar2=RSv[0:nr, 0:1],
                                                             op0=ALU.subtract, op1=ALU.mult),
              reads=[pkb, bf("MV"), bf("RSv")], writes=[ikb_])
            E("dve", lambda h, nr=nr, ikt=ikt: h.tensor_tensor(out=ikt[0:nr, 0:IDIM], in0=ikt[0:nr, 0:IDIM], in1=IKGB[0:nr, 0, :], op=ALU.mult),
              reads=[bf("IKGB")], writes=[ikb_])
            E("dve", lambda h, nr=nr, ikt=ikt: h.tensor_tensor(out=ikt[0:nr, 0:IDIM], in0=ikt[0:nr, 0:IDIM], in1=IKGB[0:nr, 1, :], op=ALU.add),
              reads=[bf("IKGB")], writes=[ikb_])
            cs, sn = ROPT[:, tt, 0:8], ROPT[:, tt, 8:16]
            for (dst, a, tab) in ((64, 0, cs), (72, 8, sn), (80, 8, cs), (88, 0, sn)):
                E("dve", lambda h, nr=nr, ikt=ikt, dst=dst, a=a, tab=tab: h.tensor_tensor(out=ikt[0:nr, dst:dst + 8], in0=ikt[0:nr, a:a + 8], in1=tab[0:nr, :], op=ALU.mult),
                  reads=[bf("ROPT")], writes=[ikb_])
            E("dve", lambda h, nr=nr, ikt=ikt: h.tensor_tensor(out=ikt[0:nr, 0:8], in0=ikt[0:nr, 64:72], in1=ikt[0:nr, 72:80], op=ALU.subtract),
              writes=[ikb_])
            E("dve", lambda h, nr=nr, ikt=ikt: h.tensor_tensor(out=ikt[0:nr, 8:16], in0=ikt[0:nr, 80:88], in1=ikt[0:nr, 88:96], op=ALU.add),
              writes=[ikb_])
            E("act", lambda h, nr=nr, tt=tt: h.activation(out=WT[0:nr, tt, :], in_=pk[0:nr, 64:80], func=AF.Copy, scale=1.0 / 32.0),
              reads=[pkb], writes=[bf("WT")])
            E("sp", lambda h, nr=nr, r0=r0, ikt=ikt: h.dma_start(out=ik_o[r0:r0 + nr, :], in_=ikt[0:nr, 0:IDIM]), reads=[ikb_], dma_key=f"o_ik{tt % 2}")
            ikd = IKD[:, tt % 2, :]
            ikdb = bf(f"IKD{tt % 2}")
            E("act", lambda h, nr=nr, ikt=ikt, ikd=ikd: h.copy(out=ikd[0:nr, 0:IDIM], in_=ikt[0:nr, 0:IDIM]), reads=[ikb_], writes=[ikdb])
            E("act", lambda h, nr=nr, ikt=ikt, ikd=ikd: h.copy(out=ikd[0:nr, IDIM:128], in_=ikt[0:nr, 0:IDIM]), reads=[ikb_], writes=[ikdb])
            tbk, tbb = next_tb()
            E("pe", lambda h, nr=nr, ikd=ikd, tbk=tbk: h.matmul(tbk[:, 0:nr], lhsT=ikd[0:nr, :], rhs=identb[0:nr, 0:nr], start=True, stop=True),
              reads=[ikdb] + CONSTS, writes=[tbb])
            E("act", lambda h, nr=nr, r0=r0, tbk=tbk: h.copy(out=IKT[:, r0:r0 + nr], in_=tbk[:, 0:nr]), reads=[tbb], writes=[bf("IKT")])

        ck(5)
        for j in range(8):
            def epi_iq(si, c0, n, bank, bb, j=j):
                rope_fm(bank, bb, n, IQT[:, j, c0:c0 + n], bf("IQT"), pmiq, ROPE[:, 2, c0:c0 + n], ROPE[:, 3, c0:c0 + n])
            proj(w_in_d[TI_IQ + j], KC, 128, xsrc, XB, segs, epi_iq)

        ck(6)
        for qt in range(4):
            i_g = b * 4 + qt
            nk = (i_g + 1) * 128
            ngrp = (nk + 511) // 512
            for kg in range(ngrp):
                k0 = kg * 512
                kw = min(512, nk - k0)
                for hh in range(NIH):
                    j, par = hh // 2, hh % 2
                    bank, bb = next_bank()
                    E("pe", lambda h, bank=bank, j=j, par=par, k0=k0, kw=kw, qt=qt:
                      h.matmul(bank[:, 0:kw], lhsT=IQT[par * 64:(par + 1) * 64, j, qt * 128:(qt + 1) * 128],
                               rhs=IKT[par * 64:(par + 1) * 64, k0:k0 + kw], start=True, stop=True),
                      reads=[bf("IQT"), bf("IKT")], writes=[bb])
                    rt = RTMP[:, hh % 2, 0:kw]
                    rtb = bf(f"RTMP{hh % 2}")
                    E("act", lambda h, bank=bank, kw=kw, rt=rt: h.activation(out=rt, in_=bank[:, 0:kw], func=AF.Relu), reads=[bb], writes=[rtb])
                    if hh == 0:
                        E("dve", lambda h, rt=rt, k0=k0, kw=kw, qt=qt: h.tensor_scalar(out=ISC[:, k0:k0 + kw], in0=rt, scalar1=WT[:, qt, 0:1], scalar2=None, op0=ALU.mult),
                          reads=[rtb, bf("WT")], writes=[bf("ISC")])
                    else:
                        E("dve", lambda h, rt=rt, k0=k0, kw=kw, qt=qt, hh=hh: h.scalar_tensor_tensor(out=ISC[:, k0:k0 + kw], in0=rt, scalar=WT[:, qt, hh:hh + 1],
                                                                                                   in1=ISC[:, k0:k0 + kw], op0=ALU.mult, op1=ALU.add),
                          reads=[rtb, bf("WT")], writes=[bf("ISC")])
            NB = 26
            use_bis = (nk > TOPK) and TOPK_BISECT
            MNv = SMALLT[:, 36:37]
            MXv = SMALLT[:, 37:38]
            W0v = SMALLT[:, 38:39]
            MIDv = SMALLT[:, 39:40]
            CNTv = SMALLT[:, 40:41]
            G2v = SMALLT[:, 41:42]
            HWv = SMALLT[:, 224:256]
            if use_bis:
                E("dve", lambda h, nk=nk: h.tensor_reduce(out=MNv, in_=ISC[:, 0:nk], axis=AX.X, op=ALU.min), reads=[bf("ISC")], writes=[bf("MNv")])
                E("dve", lambda h, nk=nk: h.tensor_reduce(out=MXv, in_=ISC[:, 0:nk], axis=AX.X, op=ALU.max), reads=[bf("ISC")], writes=[bf("MXv")])
            E("dve", lambda h, nk=nk: h.tensor_tensor(out=ISC[:, nk - 128:nk], in0=ISC[:, nk - 128:nk], in1=cneg, op=ALU.add),
              reads=CONSTS, writes=[bf("ISC")])
            if use_bis:
                E("dve", lambda h: h.tensor_tensor(out=W0v, in0=MXv, in1=MNv, op=ALU.subtract), reads=[bf("MNv"), bf("MXv")], writes=[bf("W0v")])
                E("dve", lambda h: h.tensor_scalar(out=W0v, in0=W0v, scalar1=1.001, scalar2=1.0e-6, op0=ALU.mult, op1=ALU.add), writes=[bf("W0v")])
                E("dve", lambda h: h.tensor_scalar(out=HWv, in0=misc[:, 64:96], scalar1=W0v, scalar2=None, op0=ALU.mult), reads=[bf("W0v")] + CONSTS, writes=[bf("HWv")])
                E("dve", lambda h: h.tensor_tensor(out=MIDv, in0=MNv, in1=HWv[:, 0:1], op=ALU.add), reads=[bf("MNv"), bf("HWv")], writes=[bf("MIDv")])
                for n_ in range(NB):
                    E("dve", lambda h, nk=nk: h.tensor_scalar(out=MB[:, 0:nk], in0=ISC[:, 0:nk], scalar1=MIDv, scalar2=0.0, op0=ALU.is_ge, op1=ALU.add, accum_out=CNTv),
                      reads=[bf("ISC"), bf("MIDv")], writes=[bf("MB"), bf("CNTv")])
                    nxt = n_ + 1 if n_ < NB - 1 else n_
                    E("dve", lambda h, n_=n_: h.tensor_scalar(out=G2v, in0=CNTv, scalar1=float(TOPK), scalar2=HWv[:, n_:n_ + 1], op0=ALU.is_ge, op1=ALU.mult),
                      reads=[bf("CNTv"), bf("HWv")], writes=[bf("G2v")])
                    E("dve", lambda h, nxt=nxt: h.scalar_tensor_tensor(out=MIDv, in0=G2v, scalar=HWv[:, nxt:nxt + 1], in1=MIDv, op0=ALU.subtract, op1=ALU.add),
                      reads=[bf("G2v"), bf("HWv")], writes=[bf("MIDv")])
                E("dve", lambda h, nk=nk: h.tensor_scalar(out=MB[:, 0:nk], in0=ISC[:, 0:nk], scalar1=MIDv, scalar2=None, op0=ALU.is_ge),
                  reads=[bf("ISC"), bf("MIDv")], writes=[bf("MB")])
            elif nk > TOPK:
                M8 = SMALLT[:, 48:56]
                for r in range(TOPK // 8):
                    E("dve", lambda h, nk=nk: h.max(out=M8, in_=ISC[:, 0:nk]), reads=[bf("ISC")], writes=[bf("M8")])
                    E("dve", lambda h, nk=nk: h.match_replace(out=ISC[:, 0:nk], in_to_replace=M8, in_values=ISC[:, 0:nk], imm_value=NEG),
                      reads=[bf("M8")], writes=[bf("ISC")])
                E("dve", lambda h, nk=nk: h.tensor_scalar(out=MB[:, 0:nk], in0=ISC[:, 0:nk], scalar1=NEG, scalar2=None, op0=ALU.is_equal),
                  reads=[bf("ISC")], writes=[bf("MB")])
            else:
                E("dve", lambda h, nk=nk: h.tensor_scalar(out=MB[:, 0:nk], in0=ISC[:, 0:nk], scalar1=-1.0e38, scalar2=None, op0=ALU.is_gt),
                  reads=[bf("ISC")], writes=[bf("MB")])
            for kt0 in range(0, i_g + 1, 4):
                nkt = min(4, i_g + 1 - kt0)
                tbk, tbb = next_tb()
                for jj in range(nkt):
                    E("pe", lambda h, tbk=tbk, jj=jj, kt0=kt0: h.matmul(tbk[:, jj * 128:(jj + 1) * 128], lhsT=MB[:, (kt0 + jj) * 128:(kt0 + jj + 1) * 128], rhs=identb, start=True, stop=True),
                      reads=[bf("MB")] + CONSTS, writes=[tbb])
                E("act", lambda h, tbk=tbk, nkt=nkt, kt0=kt0, qt=qt: h.copy(out=MASKT[:, kt0:kt0 + nkt, qt * 128:(qt + 1) * 128],
                                                                         in_=tbk[:, 0:nkt * 128].rearrange("p (t d) -> p t d", d=128)),
                  reads=[tbb], writes=[bf("MASKT")])

        ck(7)
        scale = HD ** -0.5
        nkt_all = b * 4 + 4

        def q_proj(hq):
            qb = QB[:, hq % 2, :]
            qbb = bf(f"QB{hq % 2}")

            def epi_q(si, c0, n, bank, bb):
                rope_fm(bank, bb, n, qb[:, c0:c0 + n], qbb, pmqk, ROPE[:, 0, c0:c0 + n], ROPE[:, 1, c0:c0 + n])
                if si == 1:
                    E("act", lambda h: h.copy(out=QS[:, :, hq, :], in_=qb[:, TB:NTB].rearrange("p (s t) -> p s t", t=TS)), reads=[qbb], writes=[bf("QS")])
            proj(w_in_d[TI_Q + hq], KC, 128, xsrc, XB, segs, epi_q, fixed_bank=4)

        def attend(hq):
            g = hq // 4
            qb = QB[:, hq % 2, :]
            qbb = bf(f"QB{hq % 2}")
            ob, obb = pb[2], Bpb[2]
            lb, lbb = pb[3], Bpb[3]

            def stage_qk(kt):
                c0 = max(0, (kt - b * 4) * 128)
                n = TB - c0
                sbi = (0, 1, 5)[kt % 3]
                sbk, sbb = pb[sbi], Bpb[sbi]
                E("pe", lambda h: h.matmul(sbk[:, 0:n], lhsT=KT[:, g, kt * 128:(kt + 1) * 128], rhs=qb[:, c0:TB], start=True, stop=True),
                  reads=[bf("KT"), qbb], writes=[sbb])
                pt_ = PTT[:, kt % 4, 0:n]
                ptbuf = bf(f"PTT{kt % 4}")
                E("act", lambda h: h.activation(out=pt_, in_=sbk[:, 0:n], func=AF.Exp, scale=scale), reads=[sbb], writes=[ptbuf])
                E("dve", lambda h: h.tensor_tensor(out=pt_, in0=pt_, in1=MASKT[:, kt, c0:TB], op=ALU.mult),
                  reads=[bf("MASKT")], writes=[ptbuf])

            def stage_pv(kt):
                c0 = max(0, (kt - b * 4) * 128)
                n = TB - c0
                pt_ = PTT[:, kt % 4, 0:n]
                ptbuf = bf(f"PTT{kt % 4}")
                E("pe", lambda h: h.matmul(ob[:, c0:TB], lhsT=V[:, kt, g * HD:(g + 1) * HD], rhs=pt_, start=(kt == 0), stop=(kt == nkt_all - 1)),
                  reads=[bf("V"), ptbuf], writes=[obb])
                E("pe", lambda h: h.matmul(lb[:, c0:TB], lhsT=onesb, rhs=pt_, start=(kt == 0), stop=(kt == nkt_all - 1)),
                  reads=[ptbuf] + CONSTS, writes=[lbb])

            for kt in range(nkt_all + 2):
                if kt < nkt_all:
                    stage_qk(kt)
                if kt >= 2:
                    stage_pv(kt - 2)
            ln_ = LNL[:, hq % 2, :]
            lnb = bf(f"LNL{hq % 2}")
            E("act", lambda h: h.activation(out=ln_, in_=lb[:, 0:TB], func=AF.Ln), reads=[lbb], writes=[lnb])
            E("act", lambda h: h.activation(out=ln_, in_=ln_, func=AF.Exp, scale=-1.0), reads=[lnb], writes=[lnb])
            E("dve", lambda h: h.tensor_tensor(out=OT[:, hq, 0:TB], in0=ob[:, 0:TB], in1=ln_, op=ALU.mult),
              reads=[obb, lnb], writes=[bf("OT")])

        q_proj(0)
        for hq in range(NH):
            if hq + 1 < NH:
                q_proj(hq + 1)
            attend(hq)

        ck(8)
        if last and do_sample:
            P.barrier()
            o = 0
            IDXT, o = carve(RA, o, [128, NSEQ * NPG], F32)
            IDX = IDXT.bitcast(I32)
            PTB_, o = carve(RA, o, [128, NSEQ * NPG], F32)
            PTBi = PTB_.bitcast(I32)
            IT, o = carve(RA, o, [128, NPG + 1, NS], F32)
            CMP, o = carve(RA, o, [128, NPG + 1, NS], F32)
            MKS, o = carve(RA, o, [128, NPG + 1, NS], BF16)
            GG, o = carve(RA, o, [128, 32, IDIM], F32)
            GB, o = carve(RA, o, [128, 32, 128], BF16)
            TMPS, o = carve(RA, o, [128, 8, 64], F32)
            WB, o = carve(RA, o, [128, NSEQ, 64], F32)
            RW, o = carve(RA, o, [128, NSEQ * 64], F32)
            BS, o = carve(RA, o, [128, 256], F32)
            PSS, o = carve(RA, o, [128, 64], F32)
            PTSS, o = carve(RA, o, [128, 2, 64], BF16)
            RLS, o = carve(RA, o, [128, 64], F32)
            assert o <= 10304, o
            o = 0
            IKTD, o = carve(RS, o, [128, PAST], BF16)
            assert o <= 4096, o
            o = 2112
            KVG, o = carve(RC, o, [128, 3, 1024], F32)
            assert o <= 5808, o
            o = 0
            KBF, o = carve(RD, o, [128, 4, 512], BF16)
            VBF, o = carve(RD, o, [128, 4, 512], BF16)
            KTP, o = carve(RD, o, [128, 4, 512], BF16)
            assert o <= 3712, o
            pidx = misc[:, 0:1]
            cneg4 = misc[0:4, 1:5]
            delta = misc[0:NS, 8:24].rearrange("p (s q) -> p s q", q=TS)
            ones_f = misc[:, 32:33]
            ones16 = CF[0:NS, 2, :]
            E("sp", lambda h: h.dma_start(out=PTBi[:, :], in_=pt_d.partition_broadcast(128)), writes=[bf("PTB")], dma_key="ptb")
            E("dve", lambda h: h.tensor_copy(out=PTB_[:, :], in_=PTBi[:, :]), reads=[bf("PTB")], writes=[bf("PTBf")])
            E("dve", lambda h: h.tensor_scalar(out=IDXT[:, :], in0=PTB_[:, :], scalar1=128.0, scalar2=pidx, op0=ALU.mult, op1=ALU.add),
              reads=[bf("PTBf")] + CONSTS, writes=[bf("IDXf")])
            E("dve", lambda h: h.tensor_copy(out=IDX[:, :], in_=IDXT[:, :]), reads=[bf("IDXf")], writes=[bf("IDX")])
            ck(20)
            RW5 = RW.rearrange("p (s r j q) -> p s r j q", s=NSEQ, r=2, j=8)
            for hh in range(NIH):
                j, par = hh // 2, hh % 2
                E("dve", lambda h, j=j, par=par, hh=hh: h.tensor_scalar(out=RW5[0:NS, :, par, j, :], in0=delta, scalar1=WT[0:NS, 4, hh:hh + 1], scalar2=None, op0=ALU.mult),
                  reads=[bf("WT")] + CONSTS, writes=[bf("RW")])
            ONESF = SMALLT[:, 256:384]
            E("dve", lambda h: h.memset(ONESF, 1.0), writes=[bf("ONESF")])
            wbk, wbb = pb[5], Bpb[5]
            E("pe", lambda h: h.matmul(wbk[:, 0:256], lhsT=ONESF[0:NS, :], rhs=RW[0:NS, :], start=True, stop=True),
              reads=[bf("RW"), bf("ONESF")], writes=[wbb])
            E("act", lambda h: h.copy(out=WB[:, :, :].rearrange("p s c -> p (s c)"), in_=wbk[:, 0:256]), reads=[wbb], writes=[bf("WB")])
            ck(21)
            IQS = SMALLT[:, 160:224].bitcast(BF16).rearrange("p (s j q) -> p s j q", s=NSEQ, j=8)
            for s_ in range(NSEQ):
                E("act", lambda h, s_=s_: h.copy(out=IQS[:, s_, :, :], in_=IQT[:, :, TB + s_ * TS:TB + (s_ + 1) * TS]), reads=[bf("IQT")], writes=[bf("IQS")])
            E("dve", lambda h: h.memset(IT[:, NPG, :], NEG), writes=[bf("IT")])
            for s_ in range(NSEQ):
                scol = TB + s_ * TS
                for half in range(2):
                    for pl in range(32):
                        pg = half * 32 + pl
                        E("pool", lambda h, pl=pl, pg=pg, s_=s_: h.indirect_dma_start(
                            out=GG[:, pl, :], out_offset=None, in_=cik_d,
                            in_offset=bass.IndirectOffsetOnAxis(ap=IDX[:, s_ * NPG + pg:s_ * NPG + pg + 1], axis=0)),
                          reads=[bf("IDX")], writes=[bf("GG")], dma_key="gg", acc=True)
                    E("dve", lambda h: h.tensor_copy(out=GB[:, :, 0:IDIM], in_=GG[:, :, :]), reads=[bf("GG")], writes=[bf("GB")])
                    E("act", lambda h: h.copy(out=GB[:, :, IDIM:128], in_=GG[:, :, :]), reads=[bf("GG")], writes=[bf("GB")])
                    for q4 in range(8):
                        tbk, tbb = next_tb()
                        for jj in range(4):
                            E("pe", lambda h, tbk=tbk, jj=jj, q4=q4: h.matmul(tbk[:, jj * 128:(jj + 1) * 128], lhsT=GB[:, q4 * 4 + jj, :], rhs=identb, start=True, stop=True),
                              reads=[bf("GB")] + CONSTS, writes=[tbb])
                        p0 = (half * 32 + q4 * 4) * 128
                        E("act", lambda h, tbk=tbk, p0=p0: h.copy(out=IKTD[:, p0:p0 + 512], in_=tbk[:, 0:512]), reads=[tbb], writes=[bf("IKTD")])
                ck(22)
                for p8 in range(8):
                    banks2 = [next_bank(), next_bank()]
                    for pl in range(8):
                        pg = p8 * 8 + pl
                        for par in range(2):
                            bank, bb = banks2[par]
                            E("pe", lambda h, bank=bank, pl=pl, pg=pg, par=par, s_=s_: h.matmul(
                                bank[:, pl * 32:pl * 32 + 32],
                                lhsT=IKTD[par * 64:(par + 1) * 64, pg * 128:(pg + 1) * 128],
                                rhs=IQS[par * 64:(par + 1) * 64, s_, :, :].rearrange("p j q -> p (j q)"), start=True, stop=True),
                              reads=[bf("IKTD"), bf("IQS")], writes=[bb])
                    ck(30)
                    for par in range(2):
                        bank, bb = banks2[par]
                        E("dve", lambda h, bank=bank, s_=s_, par=par: h.scalar_tensor_tensor(
                            out=TMPS[:, :, par * 32:(par + 1) * 32], in0=bank[:, 0:256].rearrange("p (g c) -> p g c", c=32), scalar=0.0,
                            in1=WB[:, s_, par * 32:(par + 1) * 32].unsqueeze(1).to_broadcast([128, 8, 32]), op0=ALU.max, op1=ALU.mult),
                          reads=[bb, bf("WB")], writes=[bf("TMPS")])
                    ck(31)
                    E("dve", lambda h, p8=p8, s_=s_: h.tensor_reduce(
                        out=IT[:, p8 * 8:(p8 + 1) * 8, s_ * TS:(s_ + 1) * TS],
                        in_=TMPS[:, :, :].rearrange("p g (hh q) -> p g q hh", q=TS), axis=AX.X, op=ALU.add),
                      reads=[bf("TMPS")], writes=[bf("IT")])
                ck(32)
                banks2 = [next_bank(), next_bank()]
                for par in range(2):
                    bank, bb = banks2[par]
                    E("pe", lambda h, bank=bank, par=par, s_=s_: h.matmul(
                        bank[0:TS, 0:32], lhsT=IKT[par * 64:(par + 1) * 64, T + s_ * TS:T + (s_ + 1) * TS],
                        rhs=IQS[par * 64:(par + 1) * 64, s_, :, :].rearrange("p j q -> p (j q)"), start=True, stop=True),
                      reads=[bf("IKT"), bf("IQS")], writes=[bb])
                    E("dve", lambda h, bank=bank, s_=s_, par=par: h.scalar_tensor_tensor(out=TMPS[0:TS, 0, par * 32:(par + 1) * 32], in0=bank[0:TS, 0:32], scalar=0.0,
                                                                                       in1=WB[0:TS, s_, par * 32:(par + 1) * 32], op0=ALU.max, op1=ALU.mult),
                      reads=[bb, bf("WB")], writes=[bf("TMPS")])
                E("dve", lambda h, s_=s_: h.tensor_reduce(out=IT[0:TS, NPG, s_ * TS:(s_ + 1) * TS],
                                                          in_=TMPS[0:TS, 0, :].rearrange("p (hh q) -> p q hh", q=TS), axis=AX.X, op=ALU.add),
                  reads=[bf("TMPS")], writes=[bf("IT")])
                E("dve", lambda h, s_=s_: h.tensor_tensor(out=IT[0:TS, NPG, s_ * TS:(s_ + 1) * TS], in0=IT[0:TS, NPG, s_ * TS:(s_ + 1) * TS], in1=cneg4, op=ALU.add),
                  reads=CONSTS, writes=[bf("IT")])
            ck(23)
            MXP = BS[:, 0:16]
            MNP = BS[:, 16:32]
            LO = BS[:, 32:33]
            HI = BS[:, 33:34]
            MID = BS[:, 34:35]
            GE = BS[:, 35:36]
            DLT = BS[:, 36:37]
            DG = BS[:, 48:64]
            TRS = BS[:, 64:80]
            CNTP = BS[:, 80:96]
            IT_sg = IT[:, :, :].rearrange("p g s -> p s g")
            E("dve", lambda h: h.tensor_reduce(out=MXP, in_=IT_sg, axis=AX.X, op=ALU.max), reads=[bf("IT")], writes=[bf("MXP")])
            E("dve", lambda h: h.tensor_reduce(out=MNP, in_=IT[:, 0:NPG, :].rearrange("p g s -> p s g"), axis=AX.X, op=ALU.min), reads=[bf("IT")], writes=[bf("MNP")])
            bk5, bb5 = pb[5], Bpb[5]
            E("pe", lambda h: h.transpose(bk5[0:NS, 0:128], MXP, identf), reads=[bf("MXP")] + CONSTS, writes=[bb5])
            E("dve", lambda h: h.tensor_reduce(out=HI[0:NS, :], in_=bk5[0:NS, 0:128], axis=AX.X, op=ALU.max), reads=[bb5], writes=[bf("HI")])
            E("dve", lambda h: h.tensor_scalar(out=HI[0:NS, :], in0=HI[0:NS, :], scalar1=1.0, scalar2=None, op0=ALU.add), writes=[bf("HI")])
            E("pe", lambda h: h.transpose(bk5[0:NS, 128:256], MNP, identf), reads=[bf("MNP")] + CONSTS, writes=[bb5])
            E("dve", lambda h: h.tensor_reduce(out=LO[0:NS, :], in_=bk5[0:NS, 128:256], axis=AX.X, op=ALU.min), reads=[bb5], writes=[bf("LO")])

            def thresh_compare(src, out_view, out_buf):
                E("dve", lambda h: h.tensor_scalar(out=DG[0:NS, :], in0=identf[0:NS, 0:NS], scalar1=src[0:NS, 0:1], scalar2=None, op0=ALU.mult),
                  reads=[bf("LO"), bf("HI"), bf("MID")] + CONSTS, writes=[bf("DG")])
                E("pe", lambda h: h.matmul(bk5[:, 256:272], lhsT=ONESF[0:NS, :], rhs=DG[0:NS, :], start=True, stop=True),
                  reads=[bf("DG"), bf("ONESF")], writes=[bb5])
                E("act", lambda h: h.copy(out=TRS, in_=bk5[:, 256:272]), reads=[bb5], writes=[bf("TRS")])
                E("dve", lambda h: h.tensor_tensor(out=out_view, in0=IT[:, :, :], in1=TRS.unsqueeze(1).to_broadcast([128, NPG + 1, NS]), op=ALU.is_ge),
                  reads=[bf("IT"), bf("TRS")], writes=[out_buf])

            for it in range(NBIS):
                E("dve", lambda h: h.tensor_tensor(out=MID[0:NS, :], in0=LO[0:NS, :], in1=HI[0:NS, :], op=ALU.add), reads=[bf("LO"), bf("HI")], writes=[bf("MID")])
                E("dve", lambda h: h.tensor_scalar(out=MID[0:NS, :], in0=MID[0:NS, :], scalar1=0.5, scalar2=None, op0=ALU.mult), writes=[bf("MID")])
                thresh_compare(MID, CMP[:, :, :], bf("CMP"))
                E("dve", lambda h: h.tensor_reduce(out=CNTP, in_=CMP[:, :, :].rearrange("p g s -> p s g"), axis=AX.X, op=ALU.add), reads=[bf("CMP")], writes=[bf("CNTP")])
                E("pe", lambda h: h.matmul(bk5[0:NS, 288:289], lhsT=CNTP, rhs=ONESF[:, 0:1], start=True, stop=True), reads=[bf("CNTP"), bf("ONESF")], writes=[bb5])
                E("dve", lambda h: h.tensor_scalar(out=GE[0:NS, :], in0=bk5[0:NS, 288:289], scalar1=float(TOPK), scalar2=None, op0=ALU.is_ge), reads=[bb5], writes=[bf("GE")])
                E("dve", lambda h: h.tensor_tensor(out=DLT[0:NS, :], in0=MID[0:NS, :], in1=LO[0:NS, :], op=ALU.subtract), reads=[bf("MID"), bf("LO")], writes=[bf("DLT")])
                E("dve", lambda h: h.scalar_tensor_tensor(out=LO[0:NS, :], in0=DLT[0:NS, :], scalar=GE[0:NS, 0:1], in1=LO[0:NS, :], op0=ALU.mult, op1=ALU.add),
                  reads=[bf("DLT"), bf("GE")], writes=[bf("LO")])
                E("dve", lambda h: h.tensor_tensor(out=DLT[0:NS, :], in0=HI[0:NS, :], in1=MID[0:NS, :], op=ALU.subtract), reads=[bf("MID"), bf("HI")], writes=[bf("DLT")])
                E("dve", lambda h: h.scalar_tensor_tensor(out=HI[0:NS, :], in0=DLT[0:NS, :], scalar=GE[0:NS, 0:1], in1=MID[0:NS, :], op0=ALU.mult, op1=ALU.add),
                  reads=[bf("DLT"), bf("GE"), bf("MID")], writes=[bf("HI")])
            thresh_compare(LO, MKS[:, :, :], bf("MKS"))

            ck(24)
            P.barrier()
            KVG2 = RS[:, 0:4096].rearrange("p (s c) -> p s c", c=1024)

            def kvg(slot):
                return (KVG[:, slot, :] if slot < 3 else KVG2[:, slot - 3, :]), bf(f"KVG{slot}")
            for s_ in range(NSEQ):
                ob, obb = pb[2], Bpb[2]
                lb, lbb = pb[3], Bpb[3]

                def st_g(pg, s_=s_):
                    if pg >= NPG:
                        return
                    gv, gb_ = kvg(pg % 7)
                    col = s_ * NPG + pg
                    E("pool", lambda h: h.indirect_dma_start(out=gv, out_offset=None, in_=ckv_d,
                                                             in_offset=bass.IndirectOffsetOnAxis(ap=IDX[:, col:col + 1], axis=0)),
                      reads=[bf("IDX")], writes=[gb_], dma_key=f"kvg{pg % 7}")

                def st_a(pg, s_=s_):
                    if pg == NPG:
                        return
                    sl = pg % 4
                    gv, gb_ = kvg(pg % 7)
                    E("dve", lambda h: h.tensor_copy(out=KBF[:, sl, :], in_=gv[:, 0:512]), reads=[gb_], writes=[bf(f"KBF{sl}")])
                    E("act", lambda h: h.copy(out=VBF[:, sl, :], in_=gv[:, 512:1024]), reads=[gb_], writes=[bf(f"VBF{sl}")])
                    tbk, tbb = next_tb()
                    for g in range(NKV):
                        E("pe", lambda h, g=g: h.matmul(tbk[:, g * 128:(g + 1) * 128], lhsT=KBF[:, sl, g * HD:(g + 1) * HD], rhs=identb, start=True, stop=True),
                          reads=[bf(f"KBF{sl}")] + CONSTS, writes=[tbb])
                    E("act", lambda h: h.copy(out=KTP[:, sl, :], in_=tbk[:, 0:512]), reads=[tbb], writes=[bf(f"KTP{sl}")])

                def st_b(pg, s_=s_):
                    new = (pg == NPG)
                    sl = pg % 4
                    sbk, sbb = pb[pg % 2], Bpb[pg % 2]
                    np_ = TS if new else 128
                    pts = PTSS[:, pg % 2, :]
                    ptsb = bf(f"PTS{pg % 2}")
                    for g in range(NKV):
                        if new:
                            lhs = KT[:, g, T + s_ * TS:T + (s_ + 1) * TS]
                            rd = [bf("KT")]
                        else:
                            lhs = KTP[:, sl, g * 128:(g + 1) * 128]
                            rd = [bf(f"KTP{sl}")]
                        E("pe", lambda h, g=g, lhs=lhs: h.matmul(sbk[0:np_, g * 16:(g + 1) * 16], lhsT=lhs,
                                                               rhs=QS[:, s_, 4 * g:4 * g + 4, :].rearrange("p a q -> p (a q)"), start=True, stop=True),
                          reads=rd + [bf("QS")], writes=[sbb])
                    E("act", lambda h: h.activation(out=PSS[0:np_, :], in_=sbk[0:np_, 0:64], func=AF.Exp, scale=scale), reads=[sbb], writes=[bf("PSS")])
                    E("dve", lambda h: h.tensor_tensor(
                        out=pts[0:np_, :].rearrange("p (a q) -> p a q", q=TS), in0=PSS[0:np_, :].rearrange("p (a q) -> p a q", q=TS),
                        in1=MKS[0:np_, pg, s_ * TS:(s_ + 1) * TS].unsqueeze(1).to_broadcast([np_, 16, TS]), op=ALU.mult),
                      reads=[bf("PSS"), bf("MKS")], writes=[ptsb])

                def st_c(pg, s_=s_):
                    new = (pg == NPG)
                    sl = pg % 4
                    np_ = TS if new else 128
                    pts = PTSS[:, pg % 2, :]
                    ptsb = bf(f"PTS{pg % 2}")
                    for g in range(NKV):
                        if new:
                            lhs = VS[0:TS, s_, g * HD:(g + 1) * HD]
                            rd = [bf("VS")]
                        else:
                            lhs = VBF[:, sl, g * HD:(g + 1) * HD]
                            rd = [bf(f"VBF{sl}")]
                        E("pe", lambda h, g=g, lhs=lhs: h.matmul(ob[:, g * 16:(g + 1) * 16], lhsT=lhs, rhs=pts[0:np_, g * 16:(g + 1) * 16],
                                                               start=(pg == 0), stop=new),
                          reads=rd + [ptsb], writes=[obb])
                    E("pe", lambda h: h.matmul(lb[:, 0:64], lhsT=onesb[0:np_, :], rhs=pts[0:np_, :], start=(pg == 0), stop=new),
                      reads=[ptsb] + CONSTS, writes=[lbb])

                NP1 = NPG + 1
                GA = 4
                for step in range(NP1 + 2 + GA):
                    if step < NP1:
                        st_g(step)
                    if GA <= step < NP1 + GA:
                        st_a(step - GA)
                    if GA + 1 <= step <= NP1 + GA:
                        st_b(step - GA - 1)
                    if step >= GA + 2:
                        st_c(step - GA - 2)
                E("act", lambda h, lb=lb: h.activation(out=RLS, in_=lb[:, 0:64], func=AF.Ln), reads=[lbb], writes=[bf("RLS")])
                E("act", lambda h: h.activation(out=RLS, in_=RLS, func=AF.Exp, scale=-1.0), writes=[bf("RLS")])
                E("dve", lambda h, ob=ob, s_=s_: h.tensor_tensor(out=OT[:, :, TB + s_ * TS:TB + (s_ + 1) * TS],
                                                                in0=ob[:, 0:64].rearrange("p (a q) -> p a q", q=TS),
                                                                in1=RLS.rearrange("p (a q) -> p a q", q=TS), op=ALU.mult),
                  reads=[obb, bf("RLS")], writes=[bf("OT")])
        elif last:
            E("dve", lambda h: h.memset(OT[:, :, TB:NTB], 0.0), writes=[bf("OT")])

        P.barrier()
        ck(9)
        asrc = lambda k, c0, n: AT[:, k, c0:c0 + n]
        osrc = lambda k, c0, n: OT[:, k, c0:c0 + n]
        for j in range(KC):
            def epi_ga(si, c0, n, bank, bb):
                E("act", lambda h: h.activation(out=TB1[:, c0:c0 + n], in_=bank[:, 0:n], func=AF.Sigmoid), reads=[bb], writes=[bf("TB1")])
            proj(w_in_d[TI_GA + j], KC, 128, xsrc, XB, segs, epi_ga)

            def epi_ya(si, c0, n, bank, bb):
                E("dve", lambda h: h.tensor_tensor(out=TB1[:, c0:c0 + n], in0=bank[:, 0:n], in1=TB1[:, c0:c0 + n], op=ALU.mult), reads=[bb], writes=[bf("TB1")])
            proj(w_a_d[j], 8, 128, asrc, [bf("AT")], segs, epi_ya)

            def epi_gb(si, c0, n, bank, bb):
                E("act", lambda h: h.activation(out=TB2[:, c0:c0 + n], in_=bank[:, 0:n], func=AF.Sigmoid), reads=[bb], writes=[bf("TB2")])
            proj(w_in_d[TI_GB + j], KC, 128, xsrc, XB, segs, epi_gb)

            def epi_yb(si, c0, n, bank, bb, j=j):
                E("dve", lambda h: h.tensor_tensor(out=TB2[:, c0:c0 + n], in0=bank[:, 0:n], in1=TB2[:, c0:c0 + n], op=ALU.mult), reads=[bb], writes=[bf("TB2")])
                E("dve", lambda h: h.tensor_tensor(out=MT[:, j, c0:c0 + n], in0=TB1[:, c0:c0 + n], in1=TB2[:, c0:c0 + n], op=ALU.add),
                  reads=[bf("TB1"), bf("TB2")], writes=[bf("MT")])
            proj(w_at_d[j], KC, 128, osrc, [bf("OT")], segs, epi_yb)

        P.barrier()

        def proj_to_tok(wsel, kc, srcf, src_bufs, first):
            for jg in range(4):
                for jj in range(4):
                    j = jg * 4 + jj
                    def epi_s(si, c0, n, bank, bb, jj=jj):
                        E("act", lambda h: h.copy(out=STG4[:, jj, c0:c0 + n], in_=bank[:, 0:n]), reads=[bb], writes=[bf(f"STG{jj}")])
                    proj(wsel(j), kc, 128, srcf, src_bufs, segs, epi_s)
                for tt in range(ntiles):
                    _, nr = tile_rows(tt)
                    lc0 = tt * 128
                    bank, bb = next_bank()
                    for jj in range(4):
                        E("pe", lambda h, bank=bank, jj=jj, nr=nr, lc0=lc0: h.transpose(bank[0:nr, jj * 128:(jj + 1) * 128], STG4[:, jj, lc0:lc0 + nr], identf),
                          reads=[bf(f"STG{jj}")] + CONSTS, writes=[bb])
                    tv = TOK[0:nr, tt, jg * 512:(jg + 1) * 512]
                    if first:
                        E("dve", lambda h, bank=bank, nr=nr, tv=tv: h.scalar_tensor_tensor(out=tv, in0=tv, scalar=ALPHA, in1=bank[0:nr, 0:512], op0=ALU.mult, op1=ALU.add),
                          reads=[bb], writes=[bf(f"TOK{tt}")])
                    else:
                        E("dve", lambda h, bank=bank, nr=nr, tv=tv: h.tensor_tensor(out=tv, in0=tv, in1=bank[0:nr, 0:512], op=ALU.add),
                          reads=[bb], writes=[bf(f"TOK{tt}")])

        def ln_tok(which):
            E("sp", lambda h: h.dma_start(out=LNB[:, 0, :], in_=ln_d[2 * which].partition_broadcast(128)), writes=[bf("OT")], dma_key="lnb")
            E("sp", lambda h: h.dma_start(out=LNB[:, 1, :], in_=ln_d[2 * which + 1].partition_broadcast(128)), writes=[bf("OT")], dma_key="lnb")
            for tt in range(ntiles):
                _, nr = tile_rows(tt)
                tb_ = bf(f"TOK{tt}")
                tv = TOK[:, tt, :]
                MV, RSv = layer_norm_rows(tv, nr, D, tb_, "ln")
                E("dve", lambda h, nr=nr, tv=tv: h.tensor_scalar(out=tv[0:nr, :], in0=tv[0:nr, :], scalar1=MV[0:nr, 0:1], scalar2=RSv[0:nr, 0:1], op0=ALU.subtract, op1=ALU.mult),
                  reads=[bf("MV"), bf("RSv")], writes=[tb_])
                E("dve", lambda h, nr=nr, tv=tv: h.tensor_tensor(out=tv[0:nr, :], in0=tv[0:nr, :], in1=LNB[0:nr, 0, :], op=ALU.mult), reads=[bf("OT")], writes=[tb_])
                E("dve", lambda h, nr=nr, tv=tv: h.tensor_tensor(out=tv[0:nr, :], in0=tv[0:nr, :], in1=LNB[0:nr, 1, :], op=ALU.add), reads=[bf("OT")], writes=[tb_])

        ck(10)
        for tt in range(ntiles):
            r0, nr = tile_rows(tt)
            E("sp", lambda h, tt=tt, r0=r0, nr=nr: h.dma_start(out=TOK[0:nr, tt, :], in_=xtok_d[r0:r0 + nr, :]), writes=[bf(f"TOK{tt}")], dma_key=f"tok{tt}")
        msrc = lambda k, c0, n: MT[:, k, c0:c0 + n]
        proj_to_tok(lambda j: w_mix_d[j], KC, msrc, [bf("MT")], True)
        ln_tok(0)
        HBv = RD[:, 0:1024].bitcast(BF16)
        for tt in range(ntiles):
            _, nr = tile_rows(tt)
            lc0 = tt * 128
            E("act", lambda h, nr=nr, tt=tt: h.copy(out=HBv[0:nr, :], in_=TOK[0:nr, tt, :]), reads=[bf(f"TOK{tt}")],
              writes=[bf("STG0"), bf("STG1"), bf("STG2"), bf("STG3")])
            for jg in range(4):
                tbk, tbb = next_tb()
                for jj in range(4):
                    j = jg * 4 + jj
                    E("pe", lambda h, tbk=tbk, jj=jj, j=j, nr=nr: h.matmul(tbk[:, jj * 128:jj * 128 + nr], lhsT=HBv[0:nr, j * 128:(j + 1) * 128], rhs=identb[0:nr, 0:nr], start=True, stop=True),
                      reads=[bf("STG0")] + CONSTS, writes=[tbb])
                E("act", lambda h, tbk=tbk, jg=jg, nr=nr, lc0=lc0: h.copy(out=XT[:, jg * 4:jg * 4 + 4, lc0:lc0 + nr],
                                                                        in_=tbk[:, 0:512].rearrange("p (a t) -> p a t", t=128)[:, :, 0:nr]),
                  reads=[tbb], writes=[bf("XT")])
        P.barrier()

        ck(11)
        UB = TA[:, 0:TB + 2]
        UBS = TA[:, 520:544].rearrange("p (s t) -> p s t", t=6)
        hsrc = lambda k, c0, n: XT[:, k, c0:c0 + n]
        for hf in range(2):
            for c in range(22):
                cc = hf * 22 + c
                E("act", lambda h, cc=cc: h.copy(out=UB[:, 0:2], in_=UH[:, cc, :]), reads=[bf("UH")], writes=[bf("TA")])
                if last:
                    E("act", lambda h, cc=cc: h.copy(out=UBS[:, :, 0:2], in_=SCF[:, cc, :, :]), reads=[bf("SCF")], writes=[bf("TA")])

                def epi_u(si, c0, n, bank, bb):
                    if si == 0:
                        E("act", lambda h: h.copy(out=UB[:, 2:2 + TB], in_=bank[:, 0:TB]), reads=[bb], writes=[bf("TA")])
                    else:
                        E("act", lambda h: h.copy(out=UBS[:, :, 2:6], in_=bank[:, 0:NS].rearrange("p (s t) -> p s t", t=TS)), reads=[bb], writes=[bf("TA")])
                proj(w_up_d[cc], KC, 128, hsrc, XB, segs, epi_u)
                E("dve", lambda h, cc=cc: h.tensor_scalar(out=TB1[:, 0:TB], in0=UB[:, 2:2 + TB], scalar1=CFW[:, cc, 2:3], scalar2=CFB[:, cc:cc + 1], op0=ALU.mult, op1=ALU.add),
                  reads=[bf("TA"), bf("SM")], writes=[bf("TB1")])
                E("dve", lambda h, cc=cc: h.scalar_tensor_tensor(out=TB1[:, 0:TB], in0=UB[:, 1:1 + TB], scalar=CFW[:, cc, 1:2], in1=TB1[:, 0:TB], op0=ALU.mult, op1=ALU.add),
                  reads=[bf("TA"), bf("SM")], writes=[bf("TB1")])
                E("dve", lambda h, cc=cc: h.scalar_tensor_tensor(out=TB1[:, 0:TB], in0=UB[:, 0:TB], scalar=CFW[:, cc, 0:1], in1=TB1[:, 0:TB], op0=ALU.mult, op1=ALU.add),
                  reads=[bf("TA"), bf("SM")], writes=[bf("TB1")])
                if last:
                    E("dve", lambda h, cc=cc: h.tensor_scalar(out=ACCS, in0=UBS[:, :, 2:6], scalar1=CFW[:, cc, 2:3], scalar2=CFB[:, cc:cc + 1], op0=ALU.mult, op1=ALU.add),
                      reads=[bf("TA"), bf("SM")], writes=[bf("TB1")])
                    E("dve", lambda h, cc=cc: h.scalar_tensor_tensor(out=ACCS, in0=UBS[:, :, 1:5], scalar=CFW[:, cc, 1:2], in1=ACCS, op0=ALU.mult, op1=ALU.add),
                      reads=[bf("TA"), bf("SM")], writes=[bf("TB1")])
                    E("dve", lambda h, cc=cc: h.scalar_tensor_tensor(out=ACCS, in0=UBS[:, :, 0:4], scalar=CFW[:, cc, 0:1], in1=ACCS, op0=ALU.mult, op1=ALU.add),
                      reads=[bf("TA"), bf("SM")], writes=[bf("TB1")])
                    E("act", lambda h, cc=cc: h.copy(out=CFS[:, cc, :, :], in_=UBS[:, :, 4:6]), reads=[bf("TA")], writes=[bf("CFS")])
                E("act", lambda h, cc=cc: h.copy(out=UH[:, cc, :], in_=UB[:, TB:TB + 2]), reads=[bf("TA")], writes=[bf("UH")])
                E("act", lambda h: h.activation(out=TB2[:, 0:ntok], in_=TB1[:, 0:ntok], func=AF.Gelu_apprx_tanh), reads=[bf("TB1")], writes=[bf("TB2")])

                def epi_g(si, c0, n, bank, bb, c=c):
                    E("dve", lambda h: h.tensor_tensor(out=ACTT[:, c, c0:c0 + n], in0=bank[:, 0:n], in1=TB2[:, c0:c0 + n], op=ALU.mult),
                      reads=[bb, bf("TB2")], writes=[bf("ACTT")])
                proj(w_gate_d[cc], KC, 128, hsrc, XB, segs, epi_g)
            fsrc = lambda k, c0, n: ACTT[:, k, c0:c0 + n]
            proj_to_tok(lambda j, hf=hf: w_dn_d[j, hf], 22, fsrc, [bf("ACTT")], hf == 0)
        ln_tok(1)
        for tt in range(ntiles):
            r0, nr = tile_rows(tt)
            E("sp", lambda h, tt=tt, r0=r0, nr=nr: h.dma_start(out=y_o[r0:r0 + nr, :], in_=TOK[0:nr, tt, :]), reads=[bf(f"TOK{tt}")], dma_key=f"tok{tt}")
        P.barrier()

    try:
        for b_ in range(nblk_run):
            do_block(b_)
    except _Stop:
        pass
    E("sp", lambda h: h.dma_start(out=ca_o, in_=AH[:, :, :].rearrange("p a b -> p (a b)")), reads=[bf("AH")], dma_key="o_ca")
    E("sp", lambda h: h.dma_start(out=cf_o, in_=UH[:, :, :].rearrange("p a b -> p (a b)")), reads=[bf("UH")], dma_key="o_cf")
    E("sp", lambda h: h.dma_start(out=cas_o, in_=CAS[:, :, :, :].rearrange("p a b c -> p (a b c)")), reads=[bf("CAS")], dma_key="o_cas")
    E("sp", lambda h: h.dma_start(out=cfs_o, in_=CFS[:, :, :, :].rearrange("p a b c -> p (a b c)")), reads=[bf("CFS")], dma_key="o_cfs")

    P.finalize(st)
    st.close()
    return nc


def _tile_w(w, kc):
    K, N = w.shape
    assert K == kc * 128 and N % 128 == 0
    return np.ascontiguousarray(w.reshape(kc, 128, N // 128, 128).transpose(2, 1, 0, 3).reshape(N // 128, 128, kc * 128))


def _rope_tables():
    pos = np.concatenate([np.arange(T, dtype=np.float32), PAST + np.arange(NS, dtype=np.float32) % TS])
    pos = pos.astype(np.float32)

    def cs(half):
        inv = (np.float32(500000.0) ** (-np.arange(half, dtype=np.float32) / np.float32(half))).astype(np.float32)
        ang = (pos[:, None] * inv[None, :]).astype(np.float32)
        return np.cos(ang).astype(np.float32), np.sin(ang).astype(np.float32)

    c16, s16 = cs(16)
    c8, s8 = cs(8)
    fm = np.zeros((128, 4, NTOT), np.float32)
    fm[:, 0, :] = 1.0
    fm[:, 2, :] = 1.0
    fm[0:16, 0, :] = c16.T
    fm[16:32, 0, :] = c16.T
    fm[0:16, 1, :] = -s16.T
    fm[16:32, 1, :] = s16.T
    for o in (0, 64):
        fm[o:o + 8, 2, :] = c8.T
        fm[o + 8:o + 16, 2, :] = c8.T
        fm[o:o + 8, 3, :] = -s8.T
        fm[o + 8:o + 16, 3, :] = s8.T
    tok = np.concatenate([c8, s8], axis=1).astype(np.float32)
    return fm, tok


def _consts():
    cbf = np.zeros((128, 4, 128), np.float32)
    cbf[:, 0, :] = np.eye(128, dtype=np.float32)
    for m in range(16):
        cbf[m + 16, 1, m] = 1.0
        cbf[m, 1, m + 16] = 1.0
    for o in (0, 64):
        for m in range(8):
            cbf[o + m + 8, 2, o + m] = 1.0
            cbf[o + m, 2, o + m + 8] = 1.0
    cbf[:, 3, :] = 1.0
    cf = np.zeros((128, 3, 128), np.float32)
    cf[:, 0, :] = np.eye(128, dtype=np.float32)
    t = np.arange(128)[:, None]
    s = np.arange(128)[None, :]
    cf[:, 1, :] = np.where(s <= t, 0.0, NEGBIG).astype(np.float32)
    cf[:, 2, 0] = np.arange(128, dtype=np.float32)
    j = np.arange(4)[:, None]
    q = np.arange(4)[None, :]
    cf[0:4, 2, 1:5] = np.where(j <= q, 0.0, NEG).astype(np.float32)
    for tok in range(NS):
        cf[tok, 2, 8 + tok] = 1.0
    cf[:, 2, 32:64] = 1.0
    cf[:, 2, 64:96] = (2.0 ** -(np.arange(32, dtype=np.float64) + 1.0)).astype(np.float32)[None, :]
    return cbf.reshape(128, 512), cf.reshape(128, 384)


_PROGRAM = None


def _prepare_shared(inp):
    w_in = np.asarray(inp["w_in"][0], np.float32)
    offs = np.cumsum([0, 1024, 1024, 1024, 2048, 512, 512, 1024, 64, 16, 2048, 2048])
    oB, oC, oH, oQ, oK, oV, oIQ, oIK, oIW, oGA, oGB = offs[:11]
    cols = []
    for c in range(8):
        for base in (oC, oH, oB):
            cols.append(np.arange(base + c * 128, base + (c + 1) * 128))
    for base, n in ((oQ, 16), (oK, 4), (oV, 4), (oIQ, 8), (oGA, 16), (oGB, 16)):
        for i in range(n):
            cols.append(np.arange(base + i * 128, base + (i + 1) * 128))
    cols = np.concatenate(cols)
    assert cols.size == N_WIN_TILES * 128
    w_in_t = _tile_w(w_in[:, cols], KC)
    w_ikw = np.ascontiguousarray(w_in[:, oIK:oIK + 80].reshape(KC, 128, 80).transpose(1, 0, 2).reshape(128, KC * 80))
    w_a_t = _tile_w(np.asarray(inp["w_a_out"][0], np.float32), 8)
    w_attn_t = _tile_w(np.asarray(inp["w_attn_out"][0], np.float32), KC)
    w_mix_t = _tile_w(np.asarray(inp["w_mix_out"][0], np.float32), KC)
    w_up_t = _tile_w(np.asarray(inp["w_up"][0], np.float32), KC)
    w_gate_t = _tile_w(np.asarray(inp["w_gate"][0], np.float32), KC)
    wd = _tile_w(np.asarray(inp["w_down"][0], np.float32), NFC)
    w_down_t = np.ascontiguousarray(wd.reshape(16, 128, 2, 22 * 128).transpose(0, 2, 1, 3))
    caw = np.asarray(inp["conv_a_w"][0], np.float32)
    cfw = np.asarray(inp["conv_ffn_w"][0], np.float32)
    cfb = np.asarray(inp["conv_ffn_b"][0], np.float32)
    small = np.concatenate([
        caw.T.reshape(8, 128, 3).transpose(1, 0, 2).reshape(128, 24),
        cfw.T.reshape(NFC, 128, 3).transpose(1, 0, 2).reshape(128, NFC * 3),
        cfb.reshape(NFC, 128).T], axis=1).astype(np.float32)
    ikgb = np.stack([np.asarray(inp["idx_k_norm_g"][0], np.float32), np.asarray(inp["idx_k_norm_b"][0], np.float32)])
    lnp = np.stack([np.asarray(inp[k][0], np.float32) for k in ("ln1_g", "ln1_b", "ln2_g", "ln2_b")])
    rope_fm, rope_tok = _rope_tables()
    cbf, cf = _consts()
    d = {}
    o = 0
    for i, n in enumerate(WIN_SPLIT):
        d[f"w_in_t{i}"] = np.ascontiguousarray(w_in_t[o:o + n])
        o += n
    for i in range(2):
        d[f"w_up_t{i}"] = np.ascontiguousarray(w_up_t[22 * i:22 * (i + 1)])
        d[f"w_gate_t{i}"] = np.ascontiguousarray(w_gate_t[22 * i:22 * (i + 1)])
        d[f"w_down_t{i}"] = np.ascontiguousarray(w_down_t[:, i])
    d.update(dict(w_ikw=w_ikw, w_a_t=w_a_t, w_attn_t=w_attn_t, w_mix_t=w_mix_t, small=np.ascontiguousarray(small), ikgb=ikgb, lnp=lnp,
                rope_fm=rope_fm, rope_tok=rope_tok, cbf=cbf, cf=cf,
                cache_kv=np.concatenate([np.asarray(inp["cache_k"], np.float32).reshape(NPHYS * 128, NKV * HD),
                                         np.asarray(inp["cache_v"], np.float32).reshape(NPHYS * 128, NKV * HD)], axis=1),
                cache_ik=np.asarray(inp["cache_idx_k"], np.float32).reshape(NPHYS * 128, IDIM)))
    return d


def kernel(**inp):
    global _PROGRAM
    if _PROGRAM is None:
        _PROGRAM = build_program()
    nc = _PROGRAM
    shared = _prepare_shared(inp)
    x_prompt = np.asarray(inp["x_prompt"], np.float32)
    x_sample = np.asarray(inp["x_sample"], np.float32)
    sca_all = np.asarray(inp["state_conv_a"][0], np.float32)
    scf_all = np.asarray(inp["state_conv_ffn"][0], np.float32)
    pt_all = np.asarray(inp["page_table"], np.int32)
    in_maps = []
    for c in range(8):
        xs = x_sample[NSEQ * c:NSEQ * (c + 1)].reshape(NS, D)
        xtok = np.concatenate([x_prompt[c], xs], axis=0)
        m = dict(shared)
        m["xtok"] = np.ascontiguousarray(xtok)
        m["xT"] = np.ascontiguousarray(xtok.T)
        a = sca_all[NSEQ * c:NSEQ * (c + 1)]
        m["sca"] = np.ascontiguousarray(a.reshape(NSEQ, 2, 8, 128).transpose(3, 2, 0, 1).reshape(128, 8 * NSEQ * 2))
        f = scf_all[NSEQ * c:NSEQ * (c + 1)]
        m["scf"] = np.ascontiguousarray(f.reshape(NSEQ, 2, NFC, 128).transpose(3, 2, 0, 1).reshape(128, NFC * NSEQ * 2))
        m["pt"] = np.ascontiguousarray(pt_all[NSEQ * c:NSEQ * (c + 1)].reshape(NSEQ * NPG))
        in_maps.append(m)
    res = run_bass_kernel_spmd(nc, in_maps, core_ids=list(range(8)))
    R = res.results
    y_p = np.stack([R[c]["y_o"][:T] for c in range(8)])
    y_s = np.concatenate([R[c]["y_o"][T:].reshape(NSEQ, TS, D) for c in range(8)])

    def tok_major(name, c):
        return R[c][name].transpose(2, 0, 1)

    k_p = np.stack([tok_major("kT_o", c)[:T] for c in range(8)])[None]
    v_p = np.stack([tok_major("vT_o", c)[:T] for c in range(8)])[None]
    k_s = np.concatenate([tok_major("kT_o", c)[T:].reshape(NSEQ, TS, NKV, HD) for c in range(8)])[None]
    v_s = np.concatenate([tok_major("vT_o", c)[T:].reshape(NSEQ, TS, NKV, HD) for c in range(8)])[None]
    ik_p = np.stack([R[c]["ik_o"][:T] for c in range(8)])[None]
    ik_s = np.concatenate([R[c]["ik_o"][T:].reshape(NSEQ, TS, IDIM) for c in range(8)])[None]
    ca_p = np.stack([R[c]["ca_o"].reshape(128, 8, 2).transpose(2, 1, 0).reshape(2, DCONV) for c in range(8)])[None]
    cf_p = np.stack([R[c]["cf_o"].reshape(128, NFC, 2).transpose(2, 1, 0).reshape(2, DFF) for c in range(8)])[None]
    ca_s = np.concatenate([R[c]["cas_o"].reshape(128, 8, NSEQ, 2).transpose(2, 3, 1, 0).reshape(NSEQ, 2, DCONV) for c in range(8)])[None]
    cf_s = np.concatenate([R[c]["cfs_o"].reshape(128, NFC, NSEQ, 2).transpose(2, 3, 1, 0).reshape(NSEQ, 2, DFF) for c in range(8)])[None]
    outs = (y_p, y_s, k_p, v_p, ik_p, ca_p, cf_p, k_s, v_s, ik_s, ca_s, cf_s)
    return tuple(np.ascontiguousarray(o, dtype=np.float32) for o in outs)
```

```python
import os
from contextlib import ExitStack

import numpy as np
import concourse.bass as bass
import concourse.mybir as mybir
from concourse.bass_utils import run_bass_kernel_spmd

F32 = mybir.dt.float32
BF16 = mybir.dt.bfloat16
I32 = mybir.dt.int32
AF = mybir.ActivationFunctionType
ALU = mybir.AluOpType
AX = mybir.AxisListType

D = 2048
KC = 16
T = 2048
TB = 512
NBLK = 4
NSEQ = 4
TS = 4
NS = NSEQ * TS
NTOT = T + NS
DCONV = 1024
DFF = 5632
NFC = 44
NH = 16
NKV = 4
HD = 128
NIH = 16
IDIM = 64
PAST = 8192
NPG = 64
NPHYS = 2560
TOPK = 256
ALPHA = 2.0 ** 0.25
EPS = 1e-5
NEG = -1.0e30
NEGBIG = -3.0e38
NBIS = 27
TOPK_BISECT = True

TI_CONV, TI_Q, TI_K, TI_V, TI_IQ, TI_GA, TI_GB = 0, 24, 40, 44, 48, 56, 72
N_WIN_TILES = 88
WIN_SPLIT = (24, 16, 8, 8, 16, 16)


class _Stop(Exception):
    pass


class Buf:
    __slots__ = ("name", "w", "r", "rd", "excl")

    def __init__(self, name, excl=False):
        self.name = name
        self.w = []
        self.r = {}
        self.rd = []
        self.excl = excl


class Ins:
    __slots__ = ("eng", "fn", "deps", "needs_inc", "val", "dma_key", "dma_val", "is_dma", "acc")

    def __init__(self, eng, fn, is_dma=False, dma_key=None):
        self.acc = False
        self.eng = eng
        self.fn = fn
        self.deps = []
        self.needs_inc = False
        self.val = 0
        self.is_dma = is_dma
        self.dma_key = dma_key
        self.dma_val = 0


class Prog:
    ENGS = ("pe", "act", "dve", "pool", "sp")

    def __init__(self, nc):
        self.nc = nc
        self.streams = {e: [] for e in self.ENGS}
        self.same = {"act", "dve", "pool"}
        self.dma_keys = {}
        self.last_dma = {}
        self.all = []

    def emit(self, eng, fn, reads=(), writes=(), dma_key=None, acc=False):
        is_dma = dma_key is not None
        ins = Ins(eng, fn, is_dma, dma_key)
        ins.acc = acc
        deps = []
        for b in reads:
            deps.extend(b.w)
            if b.excl:
                deps.extend(v for k, v in b.r.items() if k != eng)
        for b in writes:
            if acc:
                deps.extend(w for w in b.w if not w.acc)
            else:
                deps.extend(b.w)
            deps.extend(b.r.values())
            deps.extend(b.rd)
        seen = set()
        for d in deps:
            if id(d) in seen:
                continue
            seen.add(id(d))
            if (not d.is_dma) and d.eng == eng and eng not in self.same:
                continue
            ins.deps.append(d)
        for b in writes:
            if acc:
                b.w = [w for w in b.w if w.acc] + [ins]
            else:
                b.w = [ins]
                b.r = {}
                b.rd = []
        for b in reads:
            if b in writes:
                continue
            if is_dma:
                b.rd.append(ins)
            else:
                b.r[eng] = ins
        if is_dma:
            self.dma_keys[dma_key] = self.dma_keys.get(dma_key, 0) + 1
            ins.dma_val = 16 * self.dma_keys[dma_key]
            self.last_dma[dma_key] = ins
        self.streams[eng].append(ins)
        self.all.append(ins)
        return ins

    def barrier(self):
        lasts = []
        for e in self.ENGS:
            for ins in reversed(self.streams[e]):
                if not ins.is_dma and ins.fn is not None:
                    lasts.append(ins)
                    break
        lasts.extend(self.last_dma.values())
        for e in self.ENGS:
            ins = Ins(e, None)
            ins.deps = list(lasts)
            self.streams[e].append(ins)
            self.all.append(ins)

    def finalize(self, stack):
        nc = self.nc
        for ins in self.all:
            for d in ins.deps:
                if not d.is_dma:
                    d.needs_inc = True
        sems = {e: stack.enter_context(nc.semaphore(f"s_{e}")) for e in self.ENGS}
        ksems = {k: stack.enter_context(nc.semaphore(f"k_{i}")) for i, k in enumerate(self.dma_keys)}
        for e in self.ENGS:
            c = 0
            for ins in self.streams[e]:
                if ins.needs_inc and not ins.is_dma:
                    c += 1
                    ins.val = c
        final = [(ksems[k], 16 * n) for k, n in self.dma_keys.items()]
        block = stack.enter_context(nc.Block())
        streams = self.streams

        def run(e, h):
            seen = {}
            for ins in streams[e]:
                need = {}
                for d in ins.deps:
                    if d.is_dma:
                        s, v = ksems[d.dma_key], d.dma_val
                    else:
                        s, v = sems[d.eng], d.val
                    cur = need.get(id(s))
                    if cur is None or cur[1] < v:
                        need[id(s)] = (s, v)
                for sid, (s, v) in need.items():
                    if seen.get(sid, 0) >= v:
                        continue
                    seen[sid] = v
                    h.wait_ge(s, v)
                if ins.fn is None:
                    continue
                r = ins.fn(h)
                if ins.is_dma:
                    r.then_inc(ksems[ins.dma_key], 16)
                elif ins.needs_inc:
                    r.then_inc(sems[e], 1)
            if e == "sp":
                for s, v in final:
                    h.wait_ge(s, v)

        @block.tensor
        def _(h):
            run("pe", h)

        @block.scalar
        def _(h):
            run("act", h)

        @block.vector
        def _(h):
            run("dve", h)

        @block.gpsimd
        def _(h):
            run("pool", h)

        @block.sync
        def _(h):
            run("sp", h)


def build_program(nblk_run=NBLK, do_sample=True, nphys=NPHYS, stop=None):
    def ck(i):
        if stop is not None and stop == i:
            raise _Stop()

    nc = bass.Bass("TRN2", target_bir_lowering=False)
    st = ExitStack()

    def din(name, shape, dt=F32):
        return nc.dram_tensor(name, list(shape), dt, kind="ExternalInput").ap()

    def dout(name, shape, dt=F32):
        return nc.dram_tensor(name, list(shape), dt, kind="ExternalOutput").ap()

    xT_d = din("xT", [D, NTOT])
    xtok_d = din("xtok", [NTOT, D])
    w_in_parts = [din(f"w_in_t{i}", [n, 128, KC * 128]) for i, n in enumerate(WIN_SPLIT)]

    class _WIn:
        def __getitem__(self, idx):
            for i, n in enumerate(WIN_SPLIT):
                if idx < n:
                    return w_in_parts[i][idx]
                idx -= n
            raise IndexError
    w_in_d = _WIn()
    w_ikw_d = din("w_ikw", [128, KC * 80])
    w_a_d = din("w_a_t", [16, 128, 8 * 128])
    w_at_d = din("w_attn_t", [16, 128, KC * 128])
    w_mix_d = din("w_mix_t", [16, 128, KC * 128])
    w_up_parts = [din(f"w_up_t{i}", [22, 128, KC * 128]) for i in range(2)]
    w_gate_parts = [din(f"w_gate_t{i}", [22, 128, KC * 128]) for i in range(2)]
    w_dn_parts = [din(f"w_down_t{i}", [16, 128, 22 * 128]) for i in range(2)]

    class _Split2:
        def __init__(self, parts):
            self.parts = parts
        def __getitem__(self, idx):
            return self.parts[idx // 22][idx % 22]
    w_up_d = _Split2(w_up_parts)
    w_gate_d = _Split2(w_gate_parts)

    class _Dn:
        def __getitem__(self, jh):
            j, hf = jh
            return w_dn_parts[hf][j]
    w_dn_d = _Dn()
    cbf_d = din("cbf", [128, 4 * 128])
    cf_d = din("cf", [128, 3 * 128])
    small_d = din("small", [128, 8 * 3 + NFC * 3 + NFC])
    ikgb_d = din("ikgb", [2, IDIM])
    ln_d = din("lnp", [4, D])
    rope_fm_d = din("rope_fm", [128, 4, NTOT])
    rope_tok_d = din("rope_tok", [NTOT, 16])
    sca_d = din("sca", [128, 8 * NSEQ * 2])
    scf_d = din("scf", [128, NFC * NSEQ * 2])
    pt_d = din("pt", [NSEQ * NPG], I32)
    ckv_d = din("cache_kv", [nphys * 128, 2 * NKV * HD])
    cik_d = din("cache_ik", [nphys * 128, IDIM])

    y_o = dout("y_o", [NTOT, D])
    kT_o = dout("kT_o", [NKV, 128, NTOT])
    vT_o = dout("vT_o", [NKV, 128, NTOT])
    ik_o = dout("ik_o", [NTOT, IDIM])
    ca_o = dout("ca_o", [128, 8 * 2])
    cf_o = dout("cf_o", [128, NFC * 2])
    cas_o = dout("cas_o", [128, 8 * NSEQ * 2])
    cfs_o = dout("cfs_o", [128, NFC * NSEQ * 2])

    def sb(name, shape, dt):
        return st.enter_context(nc.sbuf_tensor(name, list(shape), dt))

    def ps(name, shape, dt):
        return st.enter_context(nc.psum_tensor(name, list(shape), dt))

    P = Prog(nc)
    E = P.emit

    KT = sb("KT", [128, NKV, NTOT], BF16)
    V = sb("V", [128, 16, NKV * HD], BF16)
    VS = sb("VS", [128, NSEQ, NKV * HD], BF16)
    IKT = sb("IKT", [128, NTOT], BF16)
    NWS = 3
    WSLOT = 22 * 128
    WR = sb("WR", [128, NWS, WSLOT], BF16)
    CB = sb("CB", [128, 4, 128], BF16)
    CF = sb("CF", [128, 3, 128], F32)
    SM = sb("SM", [128, 8 * 3 + NFC * 3 + NFC], F32)
    IKGB = sb("IKGB", [128, 2, IDIM], F32)
    UH = sb("UH", [128, NFC, 2], F32)
    AH = sb("AH", [128, 8, 2], F32)
    SCA = sb("SCA", [128, 8, NSEQ, 2], F32)
    SCF = sb("SCF", [128, NFC, NSEQ, 2], F32)
    CAS = sb("CAS", [128, 8, NSEQ, 2], F32)
    CFS = sb("CFS", [128, NFC, NSEQ, 2], F32)
    QS = sb("QS", [128, NSEQ, NH, TS], BF16)
    WT = sb("WT", [128, 5, NIH], F32)
    XT = sb("XT", [128, KC, TB + NS], BF16)
    AT = sb("AT", [128, 8, TB + NS], BF16)
    OT = sb("OT", [128, KC, TB + NS], BF16)
    NTB = TB + NS
    RA = sb("RA", [128, 10304], F32)
    RC = sb("RC", [128, 5808], F32)
    RD = sb("RD", [128, 4096], F32)
    RS = sb("RS", [128, 4096], F32) if do_sample else None

    identb = CB[:, 0, :]
    pmqk = CB[:, 1, :]
    pmiq = CB[:, 2, :]
    onesb = CB[:, 3, :]
    identf = CF[:, 0, :]
    cneg = CF[:, 1, :]
    misc = CF[:, 2, :]
    CAW = SM[:, 0:24].rearrange("p (c j) -> p c j", j=3)
    CFW = SM[:, 24:24 + NFC * 3].rearrange("p (c j) -> p c j", j=3)
    CFB = SM[:, 24 + NFC * 3:24 + NFC * 4]

    def carve(region, off, shape, dt):
        n = 1
        for s in shape[1:]:
            n *= s
        if dt == F32:
            v = region[:, off:off + n]
            words = n
        else:
            words = (n + 1) // 2
            v = region[:, off:off + words].bitcast(BF16)[:, 0:n]
        if len(shape) == 2:
            return v, off + words
        names = " ".join(f"d{i}" for i in range(len(shape) - 1))
        kw = {f"d{i}": shape[i + 1] for i in range(len(shape) - 1)}
        return v.rearrange(f"p ({names}) -> p {names}", **kw), off + words

    o = 0
    MASKT, o = carve(RA, o, [128, 16, TB], BF16)
    ISC, o = carve(RA, o, [128, 2048], F32)
    RTMP, o = carve(RA, o, [128, 2, 512], F32)
    MB, o = carve(RA, o, [128, 2048], BF16)
    ROPE, o = carve(RA, o, [128, 4, NTB], F32)
    assert o <= 10304, o
    TOK = RA[:, 0:5 * D].rearrange("p (t d) -> p t d", d=D)
    o = 0
    IQT, o = carve(RC, o, [128, 8, NTB], BF16)
    QB, o = carve(RC, o, [128, 2, NTB], BF16)
    PTT, o = carve(RC, o, [128, 4, 512], BF16)
    LNL, o = carve(RC, o, [128, 2, 512], F32)
    VTB, o = carve(RC, o, [128, NTB], BF16)
    IKD, o = carve(RC, o, [128, 2, 128], BF16)
    IKTOK, o = carve(RC, o, [128, 2, 96], F32)
    assert o <= 5808, o
    MT = RC[:, 0:(KC * NTB) // 2].bitcast(BF16).rearrange("p (k t) -> p k t", t=NTB)
    ACTT = RC[:, 0:(22 * NTB) // 2].bitcast(BF16).rearrange("p (k t) -> p k t", t=NTB)
    o = 0
    STG4, o = carve(RD, o, [128, 4, NTB], F32)
    TA, o = carve(RD, o, [128, 544], F32)
    TB1, o = carve(RD, o, [128, NTB], F32)
    TB2, o = carve(RD, o, [128, NTB], F32)
    SMALLT, o = carve(RD, o, [128, 384], F32)
    assert o <= 4096, o
    HB = STG4
    LNB = OT[:, :, :].rearrange("p k t -> p (k t)")[:, 0:8192].bitcast(F32).rearrange("p (a d) -> p a d", d=D)

    pb = [ps(f"pb{i}", [128, 512], F32) for i in range(6)]
    ptb = [ps(f"ptb{i}", [128, 512], F32) for i in range(2)]
    Bpb = [Buf(f"pb{i}", excl=True) for i in range(6)]
    Bptb = [Buf(f"ptb{i}", excl=True) for i in range(2)]

    class Ctr:
        bank = 0
        tb = 0
        ws = 0

    def next_bank():
        i = Ctr.bank % 5
        Ctr.bank += 1
        return pb[i], Bpb[i]

    def next_tb():
        i = Ctr.tb % 2
        Ctr.tb += 1
        return ptb[i], Bptb[i]

    Bws = [Buf(f"ws{i}") for i in range(NWS)]

    def wload(src_ap, nelem):
        i = Ctr.ws % NWS
        Ctr.ws += 1
        E("pool", lambda h, i=i, src_ap=src_ap, nelem=nelem: h.dma_start(out=WR[:, i, 0:nelem], in_=src_ap),
          writes=[Bws[i]], dma_key=f"w{i}")
        return WR[:, i, :], Bws[i]

    B = {}

    def bf(name):
        if name not in B:
            B[name] = Buf(name)
        return B[name]

    E("pool", lambda h: h.dma_start(out=CB[:, :, :].rearrange("p a b -> p (a b)"), in_=cbf_d), writes=[bf("CB")], dma_key="cb")
    E("sp", lambda h: h.dma_start(out=CF[:, :, :].rearrange("p a b -> p (a b)"), in_=cf_d), writes=[bf("CF")], dma_key="cf")
    E("sp", lambda h: h.dma_start(out=SM[:, :], in_=small_d), writes=[bf("SM")], dma_key="sm")
    E("sp", lambda h: h.dma_start(out=IKGB[:, :, :].rearrange("p a b -> p (a b)"),
                                  in_=ikgb_d.rearrange("a b -> (a b)").partition_broadcast(128)),
      writes=[bf("IKGB")], dma_key="ikgb")
    E("sp", lambda h: h.dma_start(out=SCA[:, :, :, :].rearrange("p a b c -> p (a b c)"), in_=sca_d), writes=[bf("SCA")], dma_key="sca")
    E("sp", lambda h: h.dma_start(out=SCF[:, :, :, :].rearrange("p a b c -> p (a b c)"), in_=scf_d), writes=[bf("SCF")], dma_key="scf")
    E("dve", lambda h: h.memset(UH[:, :, :], 0.0), writes=[bf("UH")])
    E("dve", lambda h: h.memset(AH[:, :, :], 0.0), writes=[bf("AH")])

    CONSTS = [bf("CB"), bf("CF")]

    def proj(w_src, kc, M, srcf, src_bufs, segs, epi, fixed_bank=None):
        wt, wb = wload(w_src, kc * M)
        for si, (c0, n) in enumerate(segs):
            bank, bb = next_bank() if fixed_bank is None else (pb[fixed_bank], Bpb[fixed_bank])
            for k in range(kc):
                E("pe", lambda h, bank=bank, wt=wt, k=k, c0=c0, n=n, M=M, kc=kc:
                  h.matmul(bank[0:M, 0:n], lhsT=wt[:, k * M:(k + 1) * M], rhs=srcf(k, c0, n),
                           start=(k == 0), stop=(k == kc - 1)),
                  reads=[wb] + src_bufs, writes=[bb])
            epi(si, c0, n, bank, bb)

    def rope_fm(bank, bb, n, xb_view, xb_buf, pm, ctab, stab, f32_out=None, f32_buf=None):
        lvl = int(os.environ.get("KDBG_ROPE", 9))
        E("act", lambda h: h.copy(out=xb_view, in_=bank[:, 0:n]), reads=[bb], writes=[xb_buf])
        if lvl < 1:
            return
        pbk, pbb = pb[5], Bpb[5]
        E("pe", lambda h: h.matmul(pbk[:, 0:n], lhsT=pm, rhs=xb_view, start=True, stop=True),
          reads=[xb_buf] + CONSTS, writes=[pbb])
        if lvl < 2:
            return
        var = os.environ.get("KDBG_VAR", "")
        if var == "A":
            E("dve", lambda h: h.tensor_tensor(out=TB1[:, 0:n], in0=ctab, in1=bank[:, 0:n], op=ALU.mult),
              reads=[bb, bf("ROPE")], writes=[bf("TB1")])
        elif var == "B":
            E("act", lambda h: h.copy(out=TB1[:, 0:n], in_=bank[:, 0:n]), reads=[bb], writes=[bf("TB1")])
            E("dve", lambda h: h.tensor_tensor(out=TB1[:, 0:n], in0=TB1[:, 0:n], in1=ctab, op=ALU.mult),
              reads=[bf("ROPE")], writes=[bf("TB1")])
        elif var == "C":
            E("dve", lambda h: h.tensor_copy(out=TB1[:, 0:n], in_=ctab), reads=[bf("ROPE")], writes=[bf("TB1")])
        else:
            E("dve", lambda h: h.tensor_tensor(out=TB1[:, 0:n], in0=bank[:, 0:n], in1=ctab, op=ALU.mult),
              reads=[bb, bf("ROPE")], writes=[bf("TB1")])
        if lvl < 3:
            return
        E("dve", lambda h: h.tensor_tensor(out=TB2[:, 0:n], in0=pbk[:, 0:n], in1=stab, op=ALU.mult),
          reads=[pbb, bf("ROPE")], writes=[bf("TB2")])
        if lvl < 4:
            return
        if f32_out is None:
            E("dve", lambda h: h.tensor_tensor(out=xb_view, in0=TB1[:, 0:n], in1=TB2[:, 0:n], op=ALU.add),
              reads=[bf("TB1"), bf("TB2")], writes=[xb_buf])
        else:
            E("dve", lambda h: h.tensor_tensor(out=f32_out, in0=TB1[:, 0:n], in1=TB2[:, 0:n], op=ALU.add),
              reads=[bf("TB1"), bf("TB2")], writes=[f32_buf])
            E("act", lambda h: h.copy(out=xb_view, in_=f32_out), reads=[f32_buf], writes=[xb_buf])

    def layer_norm_rows(x_view, ntp, ncol, xbuf, tag):
        nch = (ncol + 511) // 512
        STv = SMALLT[:, 0:nch * 6].rearrange("p (a s) -> p a s", s=6)
        MV = SMALLT[:, 32:34]
        RSv = SMALLT[:, 34:35]
        for a in range(nch):
            w = min(512, ncol - a * 512)
            E("dve", lambda h, a=a, w=w: h.bn_stats(out=STv[0:ntp, a, :], in_=x_view[0:ntp, a * 512:a * 512 + w]),
              reads=[xbuf], writes=[bf("ST" )])
        E("dve", lambda h: h.bn_aggr(out=MV[0:ntp, :], in_=STv[0:ntp, :, :]), reads=[bf("ST")], writes=[bf("MV")])
        E("dve", lambda h: h.tensor_scalar(out=RSv[0:ntp, :], in0=MV[0:ntp, 1:2], scalar1=EPS, scalar2=None, op0=ALU.add),
          reads=[bf("MV")], writes=[bf("RSv")])
        E("act", lambda h: h.activation(out=RSv[0:ntp, :], in_=RSv[0:ntp, :], func=AF.Sqrt), reads=[bf("RSv")], writes=[bf("RSv")])
        E("dve", lambda h: h.reciprocal(out=RSv[0:ntp, :], in_=RSv[0:ntp, :]), reads=[bf("RSv")], writes=[bf("RSv")])
        return MV, RSv

    def do_block(b):
      if True:
        last = (b == NBLK - 1)
        tok0 = b * TB
        ntok = TB + (NS if last else 0)
        segs = [(0, TB)] + ([(TB, NS)] if last else [])
        ntiles = 4 + (1 if last else 0)

        def gcols(c0, n):
            return (tok0 + c0, n) if c0 < TB else (T + (c0 - TB), n)

        def tile_rows(tt):
            return (tok0 + tt * 128, 128) if tt < 4 else (T, NS)

        xT_v = xT_d.rearrange("(k p) t -> p k t", p=128)
        E("pool", lambda h: h.dma_start(out=XT[:, :, 0:TB], in_=xT_v[:, :, tok0:tok0 + TB]), writes=[bf("XT")], dma_key="xt")
        E("sp", lambda h: h.dma_start(out=ROPE[:, :, 0:TB], in_=rope_fm_d[:, :, tok0:tok0 + TB]), writes=[bf("ROPE")], dma_key="rope")
        if last:
            E("pool", lambda h: h.dma_start(out=XT[:, :, TB:NTB], in_=xT_v[:, :, T:NTOT]), writes=[bf("XT")], dma_key="xt")
            E("sp", lambda h: h.dma_start(out=ROPE[:, :, TB:NTB], in_=rope_fm_d[:, :, T:NTOT]), writes=[bf("ROPE")], dma_key="rope")
        ROPT = SMALLT[:, 64:64 + 5 * 16].rearrange("p (t c) -> p t c", c=16)
        for tt in range(ntiles):
            r0, nr = tile_rows(tt)
            E("sp", lambda h, tt=tt, r0=r0, nr=nr: h.dma_start(out=ROPT[0:nr, tt, :], in_=rope_tok_d[r0:r0 + nr, :]),
              writes=[bf("ROPT")], dma_key="ropt")

        xsrc = lambda k, c0, n: XT[:, k, c0:c0 + n]
        XB = [bf("XT")]

        ck(1)
        PA = TA[:, 0:TB + 2]
        PAS = TA[:, 520:544].rearrange("p (s t) -> p s t", t=6)
        ACCS = TB1[:, TB:NTB].rearrange("p (s t) -> p s t", t=TS)
        for c in range(8):
            def epi_cc(si, c0, n, bank, bb):
                E("act", lambda h: h.copy(out=TB2[:, c0:c0 + n], in_=bank[:, 0:n]), reads=[bb], writes=[bf("TB2")])
            proj(w_in_d[TI_CONV + 3 * c + 0], KC, 128, xsrc, XB, segs, epi_cc)
            E("act", lambda h, c=c: h.copy(out=PA[:, 0:2], in_=AH[:, c, :]), reads=[bf("AH")], writes=[bf("TA")])
            if last:
                E("act", lambda h, c=c: h.copy(out=PAS[:, :, 0:2], in_=SCA[:, c, :, :]), reads=[bf("SCA")], writes=[bf("TA")])

            def epi_ch(si, c0, n, bank, bb):
                if si == 0:
                    E("dve", lambda h: h.tensor_tensor(out=PA[:, 2:2 + TB], in0=bank[:, 0:TB], in1=TB2[:, 0:TB], op=ALU.mult),
                      reads=[bb, bf("TB2")], writes=[bf("TA")])
                else:
                    E("dve", lambda h: h.tensor_tensor(out=PAS[:, :, 2:6],
                                                        in0=bank[:, 0:NS].rearrange("p (s t) -> p s t", t=TS),
                                                        in1=TB2[:, TB:NTB].rearrange("p (s t) -> p s t", t=TS), op=ALU.mult),
                      reads=[bb, bf("TB2")], writes=[bf("TA")])
            proj(w_in_d[TI_CONV + 3 * c + 1], KC, 128, xsrc, XB, segs, epi_ch)
            E("dve", lambda h, c=c: h.tensor_scalar(out=TB1[:, 0:TB], in0=PA[:, 2:2 + TB], scalar1=CAW[:, c, 2:3], scalar2=None, op0=ALU.mult),
              reads=[bf("TA"), bf("SM")], writes=[bf("TB1")])
            E("dve", lambda h, c=c: h.scalar_tensor_tensor(out=TB1[:, 0:TB], in0=PA[:, 1:1 + TB], scalar=CAW[:, c, 1:2], in1=TB1[:, 0:TB], op0=ALU.mult, op1=ALU.add),
              reads=[bf("TA"), bf("SM")], writes=[bf("TB1")])
            E("dve", lambda h, c=c: h.scalar_tensor_tensor(out=TB1[:, 0:TB], in0=PA[:, 0:TB], scalar=CAW[:, c, 0:1], in1=TB1[:, 0:TB], op0=ALU.mult, op1=ALU.add),
              reads=[bf("TA"), bf("SM")], writes=[bf("TB1")])
            if last:
                E("dve", lambda h, c=c: h.tensor_scalar(out=ACCS, in0=PAS[:, :, 2:6], scalar1=CAW[:, c, 2:3], scalar2=None, op0=ALU.mult),
                  reads=[bf("TA"), bf("SM")], writes=[bf("TB1")])
                E("dve", lambda h, c=c: h.scalar_tensor_tensor(out=ACCS, in0=PAS[:, :, 1:5], scalar=CAW[:, c, 1:2], in1=ACCS, op0=ALU.mult, op1=ALU.add),
                  reads=[bf("TA"), bf("SM")], writes=[bf("TB1")])
                E("dve", lambda h, c=c: h.scalar_tensor_tensor(out=ACCS, in0=PAS[:, :, 0:4], scalar=CAW[:, c, 0:1], in1=ACCS, op0=ALU.mult, op1=ALU.add),
                  reads=[bf("TA"), bf("SM")], writes=[bf("TB1")])
                E("act", lambda h, c=c: h.copy(out=CAS[:, c, :, :], in_=PAS[:, :, 4:6]), reads=[bf("TA")], writes=[bf("CAS")])
            E("act", lambda h, c=c: h.copy(out=AH[:, c, :], in_=PA[:, TB:TB + 2]), reads=[bf("TA")], writes=[bf("AH")])

            def epi_cb(si, c0, n, bank, bb, c=c):
                E("dve", lambda h: h.tensor_tensor(out=AT[:, c, c0:c0 + n], in0=bank[:, 0:n], in1=TB1[:, c0:c0 + n], op=ALU.mult),
                  reads=[bb, bf("TB1")], writes=[bf("AT")])
            proj(w_in_d[TI_CONV + 3 * c + 2], KC, 128, xsrc, XB, segs, epi_cb)

        ck(2)
        for g in range(int(os.environ.get('KDBG_NG', NKV))):
            def epi_k(si, c0, n, bank, bb, g=g):
                gc0, _ = gcols(c0, n)
                stg = STG4[:, g % 2, c0:c0 + n]
                sbuf = bf(f"STG{g % 2}")
                rope_fm(bank, bb, n, KT[:, g, gc0:gc0 + n], bf("KT"), pmqk, ROPE[:, 0, c0:c0 + n], ROPE[:, 1, c0:c0 + n],
                        f32_out=stg, f32_buf=sbuf)
                if not os.environ.get('KDBG_NOKDMA'):
                    E("sp", lambda h: h.dma_start(out=kT_o[g, :, gc0:gc0 + n], in_=stg), reads=[sbuf], dma_key=f"o_stg{g % 2}")
            proj(w_in_d[TI_K + g], KC, 128, xsrc, XB, segs, epi_k)

        ck(3)
        for g in range(NKV):
            def epi_v(si, c0, n, bank, bb, g=g):
                gc0, _ = gcols(c0, n)
                stg = STG4[:, 2 + g % 2, c0:c0 + n]
                sbuf = bf(f"STG{2 + g % 2}")
                E("act", lambda h: h.copy(out=stg, in_=bank[:, 0:n]), reads=[bb], writes=[sbuf])
                E("dve", lambda h: h.tensor_copy(out=VTB[:, c0:c0 + n], in_=bank[:, 0:n]), reads=[bb], writes=[bf("VTB")])
                E("sp", lambda h: h.dma_start(out=vT_o[g, :, gc0:gc0 + n], in_=stg), reads=[sbuf], dma_key=f"o_stg{2 + g % 2}")
                if si == 0:
                    tbk, tbb = next_tb()
                    for tt in range(4):
                        E("pe", lambda h, tt=tt: h.matmul(tbk[:, tt * 128:(tt + 1) * 128], lhsT=VTB[:, tt * 128:(tt + 1) * 128], rhs=identb, start=True, stop=True),
                          reads=[bf("VTB")] + CONSTS, writes=[tbb])
                    E("act", lambda h: h.copy(out=V[:, b * 4:b * 4 + 4, g * HD:(g + 1) * HD],
                                              in_=tbk[:, 0:512].rearrange("p (t d) -> p t d", d=128)),
                      reads=[tbb], writes=[bf("V")])
                else:
                    for s_ in range(NSEQ):
                        tbk, tbb = next_tb()
                        E("pe", lambda h, s_=s_, tbk=tbk: h.matmul(tbk[0:TS, 0:128], lhsT=VTB[:, TB + s_ * TS:TB + (s_ + 1) * TS], rhs=identb, start=True, stop=True),
                          reads=[bf("VTB")] + CONSTS, writes=[tbb])
                        E("act", lambda h, s_=s_, tbk=tbk: h.copy(out=VS[0:TS, s_, g * HD:(g + 1) * HD], in_=tbk[0:TS, 0:128]),
                          reads=[tbb], writes=[bf("VS")])
            proj(w_in_d[TI_V + g], KC, 128, xsrc, XB, segs, epi_v)

        ck(4)
        IKS = STG4[:, 0, :]
        def epi_ikw(si, c0, n, bank, bb):
            E("act", lambda h: h.copy(out=IKS[0:80, c0:c0 + n], in_=bank[0:80, 0:n]), reads=[bb], writes=[bf("STG0")])
        proj(w_ikw_d, KC, 80, xsrc, XB, segs, epi_ikw)
        for tt in range(ntiles):
            r0, nr = tile_rows(tt)
            lc0 = tt * 128
            pk, pkb = pb[5], Bpb[5]
            E("pe", lambda h, lc0=lc0, nr=nr: h.transpose(pk[0:nr, 0:80], IKS[0:80, lc0:lc0 + nr], identf[0:80, 0:80]),
              reads=[bf("STG0")] + CONSTS, writes=[pkb])
            MV, RSv = layer_norm_rows(pk, nr, IDIM, pkb, "ik")
            ikt = IKTOK[:, tt % 2, :]
            ikb_ = bf(f"IKTOK{tt % 2}")
            E("dve", lambda h, nr=nr, ikt=ikt: h.tensor_scalar(out=ikt[0:nr, 0:IDIM], in0=pk[0:nr, 0:IDIM], scalar1=MV[0:nr, 0:1], scalar2=RSv[0:nr, 0:1],
                                                             op0=ALU.subtract, op1=ALU.mult),
              reads=[pkb, bf("MV"), bf("RSv")], writes=[ikb_])
            E("dve", lambda h, nr=nr, ikt=ikt: h.tensor_tensor(out=ikt[0:nr, 0:IDIM], in0=ikt[0:nr, 0:IDIM], in1=IKGB[0:nr, 0, :], op=ALU.mult),
              reads=[bf("IKGB")], writes=[ikb_])
            E("dve", lambda h, nr=nr, ikt=ikt: h.tensor_tensor(out=ikt[0:nr, 0:IDIM], in0=ikt[0:nr, 0:IDIM], in1=IKGB[0:nr, 1, :], op=ALU.add),
              reads=[bf("IKGB")], writes=[ikb_])
            cs, sn = ROPT[:, tt, 0:8], ROPT[:, tt, 8:16]
            for (dst, a, tab) in ((64, 0, cs), (72, 8, sn), (80, 8, cs), (88, 0, sn)):
                E("dve", lambda h, nr=nr, ikt=ikt, dst=dst, a=a, tab=tab: h.tensor_tensor(out=ikt[0:nr, dst:dst + 8], in0=ikt[0:nr, a:a + 8], in1=tab[0:nr, :], op=ALU.mult),
                  reads=[bf("ROPT")], writes=[ikb_])
            E("dve", lambda h, nr=nr, ikt=ikt: h.tensor_tensor(out=ikt[0:nr, 0:8], in0=ikt[0:nr, 64:72], in1=ikt[0:nr, 72:80], op=ALU.subtract),
              writes=[ikb_])
            E("dve", lambda h, nr=nr, ikt=ikt: h.tensor_tensor(out=ikt[0:nr, 8:16], in0=ikt[0:nr, 80:88], in1=ikt[0:nr, 88:96], op=ALU.add),
              writes=[ikb_])
            E("act", lambda h, nr=nr, tt=tt: h.activation(out=WT[0:nr, tt, :], in_=pk[0:nr, 64:80], func=AF.Copy, scale=1.0 / 32.0),
              reads=[pkb], writes=[bf("WT")])
            E("sp", lambda h, nr=nr, r0=r0, ikt=ikt: h.dma_start(out=ik_o[r0:r0 + nr, :], in_=ikt[0:nr, 0:IDIM]), reads=[ikb_], dma_key=f"o_ik{tt % 2}")
            ikd = IKD[:, tt % 2, :]
            ikdb = bf(f"IKD{tt % 2}")
            E("act", lambda h, nr=nr, ikt=ikt, ikd=ikd: h.copy(out=ikd[0:nr, 0:IDIM], in_=ikt[0:nr, 0:IDIM]), reads=[ikb_], writes=[ikdb])
            E("act", lambda h, nr=nr, ikt=ikt, ikd=ikd: h.copy(out=ikd[0:nr, IDIM:128], in_=ikt[0:nr, 0:IDIM]), reads=[ikb_], writes=[ikdb])
            tbk, tbb = next_tb()
            E("pe", lambda h, nr=nr, ikd=ikd, tbk=tbk: h.matmul(tbk[:, 0:nr], lhsT=ikd[0:nr, :], rhs=identb[0:nr, 0:nr], start=True, stop=True),
              reads=[ikdb] + CONSTS, writes=[tbb])
            E("act", lambda h, nr=nr, r0=r0, tbk=tbk: h.copy(out=IKT[:, r0:r0 + nr], in_=tbk[:, 0:nr]), reads=[tbb], writes=[bf("IKT")])

        ck(5)
        for j in range(8):
            def epi_iq(si, c0, n, bank, bb, j=j):
                rope_fm(bank, bb, n, IQT[:, j, c0:c0 + n], bf("IQT"), pmiq, ROPE[:, 2, c0:c0 + n], ROPE[:, 3, c0:c0 + n])
            proj(w_in_d[TI_IQ + j], KC, 128, xsrc, XB, segs, epi_iq)

        ck(6)
        for qt in range(4):
            i_g = b * 4 + qt
            nk = (i_g + 1) * 128
            ngrp = (nk + 511) // 512
            for kg in range(ngrp):
                k0 = kg * 512
                kw = min(512, nk - k0)
                for hh in range(NIH):
                    j, par = hh // 2, hh % 2
                    bank, bb = next_bank()
                    E("pe", lambda h, bank=bank, j=j, par=par, k0=k0, kw=kw, qt=qt:
                      h.matmul(bank[:, 0:kw], lhsT=IQT[par * 64:(par + 1) * 64, j, qt * 128:(qt + 1) * 128],
                               rhs=IKT[par * 64:(par + 1) * 64, k0:k0 + kw], start=True, stop=True),
                      reads=[bf("IQT"), bf("IKT")], writes=[bb])
                    rt = RTMP[:, hh % 2, 0:kw]
                    rtb = bf(f"RTMP{hh % 2}")
                    E("act", lambda h, bank=bank, kw=kw, rt=rt: h.activation(out=rt, in_=bank[:, 0:kw], func=AF.Relu), reads=[bb], writes=[rtb])
                    if hh == 0:
                        E("dve", lambda h, rt=rt, k0=k0, kw=kw, qt=qt: h.tensor_scalar(out=ISC[:, k0:k0 + kw], in0=rt, scalar1=WT[:, qt, 0:1], scalar2=None, op0=ALU.mult),
                          reads=[rtb, bf("WT")], writes=[bf("ISC")])
                    else:
                        E("dve", lambda h, rt=rt, k0=k0, kw=kw, qt=qt, hh=hh: h.scalar_tensor_tensor(out=ISC[:, k0:k0 + kw], in0=rt, scalar=WT[:, qt, hh:hh + 1],
                                                                                                   in1=ISC[:, k0:k0 + kw], op0=ALU.mult, op1=ALU.add),
                          reads=[rtb, bf("WT")], writes=[bf("ISC")])
            NB = 26
            use_bis = (nk > TOPK) and TOPK_BISECT
            MNv = SMALLT[:, 36:37]
            MXv = SMALLT[:, 37:38]
            W0v = SMALLT[:, 38:39]
            MIDv = SMALLT[:, 39:40]
            CNTv = SMALLT[:, 40:41]
            G2v = SMALLT[:, 41:42]
            HWv = SMALLT[:, 224:256]
            if use_bis:
                E("dve", lambda h, nk=nk: h.tensor_reduce(out=MNv, in_=ISC[:, 0:nk], axis=AX.X, op=ALU.min), reads=[bf("ISC")], writes=[bf("MNv")])
                E("dve", lambda h, nk=nk: h.tensor_reduce(out=MXv, in_=ISC[:, 0:nk], axis=AX.X, op=ALU.max), reads=[bf("ISC")], writes=[bf("MXv")])
            E("dve", lambda h, nk=nk: h.tensor_tensor(out=ISC[:, nk - 128:nk], in0=ISC[:, nk - 128:nk], in1=cneg, op=ALU.add),
              reads=CONSTS, writes=[bf("ISC")])
            if use_bis:
                E("dve", lambda h: h.tensor_tensor(out=W0v, in0=MXv, in1=MNv, op=ALU.subtract), reads=[bf("MNv"), bf("MXv")], writes=[bf("W0v")])
                E("dve", lambda h: h.tensor_scalar(out=W0v, in0=W0v, scalar1=1.001, scalar2=1.0e-6, op0=ALU.mult, op1=ALU.add), writes=[bf("W0v")])
                E("dve", lambda h: h.tensor_scalar(out=HWv, in0=misc[:, 64:96], scalar1=W0v, scalar2=None, op0=ALU.mult), reads=[bf("W0v")] + CONSTS, writes=[bf("HWv")])
                E("dve", lambda h: h.tensor_tensor(out=MIDv, in0=MNv, in1=HWv[:, 0:1], op=ALU.add), reads=[bf("MNv"), bf("HWv")], writes=[bf("MIDv")])
                for n_ in range(NB):
                    E("dve", lambda h, nk=nk: h.tensor_scalar(out=MB[:, 0:nk], in0=ISC[:, 0:nk], scalar1=MIDv, scalar2=0.0, op0=ALU.is_ge, op1=ALU.add, accum_out=CNTv),
                      reads=[bf("ISC"), bf("MIDv")], writes=[bf("MB"), bf("CNTv")])
                    nxt = n_ + 1 if n_ < NB - 1 else n_
                    E("dve", lambda h, n_=n_: h.tensor_scalar(out=G2v, in0=CNTv, scalar1=float(TOPK), scalar2=HWv[:, n_:n_ + 1], op0=ALU.is_ge, op1=ALU.mult),
                      reads=[bf("CNTv"), bf("HWv")], writes=[bf("G2v")])
                    E("dve", lambda h, nxt=nxt: h.scalar_tensor_tensor(out=MIDv, in0=G2v, scalar=HWv[:, nxt:nxt + 1], in1=MIDv, op0=ALU.subtract, op1=ALU.add),
                      reads=[bf("G2v"), bf("HWv")], writes=[bf("MIDv")])
                E("dve", lambda h, nk=nk: h.tensor_scalar(out=MB[:, 0:nk], in0=ISC[:, 0:nk], scalar1=MIDv, scalar2=None, op0=ALU.is_ge),
                  reads=[bf("ISC"), bf("MIDv")], writes=[bf("MB")])
            elif nk > TOPK:
                M8 = SMALLT[:, 48:56]
                for r in range(TOPK // 8):
                    E("dve", lambda h, nk=nk: h.max(out=M8, in_=ISC[:, 0:nk]), reads=[bf("ISC")], writes=[bf("M8")])
                    E("dve", lambda h, nk=nk: h.match_replace(out=ISC[:, 0:nk], in_to_replace=M8, in_values=ISC[:, 0:nk], imm_value=NEG),
                      reads=[bf("M8")], writes=[bf("ISC")])
                E("dve", lambda h, nk=nk: h.tensor_scalar(out=MB[:, 0:nk], in0=ISC[:, 0:nk], scalar1=NEG, scalar2=None, op0=ALU.is_equal),
                  reads=[bf("ISC")], writes=[bf("MB")])
            else:
                E("dve", lambda h, nk=nk: h.tensor_scalar(out=MB[:, 0:nk], in0=ISC[:, 0:nk], scalar1=-1.0e38, scalar2=None, op0=ALU.is_gt),
                  reads=[bf("ISC")], writes=[bf("MB")])
            for kt0 in range(0, i_g + 1, 4):
                nkt = min(4, i_g + 1 - kt0)
                tbk, tbb = next_tb()
                for jj in range(nkt):
                    E("pe", lambda h, tbk=tbk, jj=jj, kt0=kt0: h.matmul(tbk[:, jj * 128:(jj + 1) * 128], lhsT=MB[:, (kt0 + jj) * 128:(kt0 + jj + 1) * 128], rhs=identb, start=True, stop=True),
                      reads=[bf("MB")] + CONSTS, writes=[tbb])
                E("act", lambda h, tbk=tbk, nkt=nkt, kt0=kt0, qt=qt: h.copy(out=MASKT[:, kt0:kt0 + nkt, qt * 128:(qt + 1) * 128],
                                                                         in_=tbk[:, 0:nkt * 128].rearrange("p (t d) -> p t d", d=128)),
                  reads=[tbb], writes=[bf("MASKT")])

        ck(7)
        scale = HD ** -0.5
        nkt_all = b * 4 + 4

        def q_proj(hq):
            qb = QB[:, hq % 2, :]
            qbb = bf(f"QB{hq % 2}")

            def epi_q(si, c0, n, bank, bb):
                rope_fm(bank, bb, n, qb[:, c0:c0 + n], qbb, pmqk, ROPE[:, 0, c0:c0 + n], ROPE[:, 1, c0:c0 + n])
                if si == 1:
                    E("act", lambda h: h.copy(out=QS[:, :, hq, :], in_=qb[:, TB:NTB].rearrange("p (s t) -> p s t", t=TS)), reads=[qbb], writes=[bf("QS")])
            proj(w_in_d[TI_Q + hq], KC, 128, xsrc, XB, segs, epi_q, fixed_bank=4)

        def attend(hq):
            g = hq // 4
            qb = QB[:, hq % 2, :]
            qbb = bf(f"QB{hq % 2}")
            ob, obb = pb[2], Bpb[2]
            lb, lbb = pb[3], Bpb[3]

            def stage_qk(kt):
                c0 = max(0, (kt - b * 4) * 128)
                n = TB - c0
                sbi = (0, 1, 5)[kt % 3]
                sbk, sbb = pb[sbi], Bpb[sbi]
                E("pe", lambda h: h.matmul(sbk[:, 0:n], lhsT=KT[:, g, kt * 128:(kt + 1) * 128], rhs=qb[:, c0:TB], start=True, stop=True),
                  reads=[bf("KT"), qbb], writes=[sbb])
                pt_ = PTT[:, kt % 4, 0:n]
                ptbuf = bf(f"PTT{kt % 4}")
                E("act", lambda h: h.activation(out=pt_, in_=sbk[:, 0:n], func=AF.Exp, scale=scale), reads=[sbb], writes=[ptbuf])
                E("dve", lambda h: h.tensor_tensor(out=pt_, in0=pt_, in1=MASKT[:, kt, c0:TB], op=ALU.mult),
                  reads=[bf("MASKT")], writes=[ptbuf])

            def stage_pv(kt):
                c0 = max(0, (kt - b * 4) * 128)
                n = TB - c0
                pt_ = PTT[:, kt % 4, 0:n]
                ptbuf = bf(f"PTT{kt % 4}")
                E("pe", lambda h: h.matmul(ob[:, c0:TB], lhsT=V[:, kt, g * HD:(g + 1) * HD], rhs=pt_, start=(kt == 0), stop=(kt == nkt_all - 1)),
                  reads=[bf("V"), ptbuf], writes=[obb])
                E("pe", lambda h: h.matmul(lb[:, c0:TB], lhsT=onesb, rhs=pt_, start=(kt == 0), stop=(kt == nkt_all - 1)),
                  reads=[ptbuf] + CONSTS, writes=[lbb])

            for kt in range(nkt_all + 2):
                if kt < nkt_all:
                    stage_qk(kt)
                if kt >= 2:
                    stage_pv(kt - 2)
            ln_ = LNL[:, hq % 2, :]
            lnb = bf(f"LNL{hq % 2}")
            E("act", lambda h: h.activation(out=ln_, in_=lb[:, 0:TB], func=AF.Ln), reads=[lbb], writes=[lnb])
            E("act", lambda h: h.activation(out=ln_, in_=ln_, func=AF.Exp, scale=-1.0), reads=[lnb], writes=[lnb])
            E("dve", lambda h: h.tensor_tensor(out=OT[:, hq, 0:TB], in0=ob[:, 0:TB], in1=ln_, op=ALU.mult),
              reads=[obb, lnb], writes=[bf("OT")])

        q_proj(0)
        for hq in range(NH):
            if hq + 1 < NH:
                q_proj(hq + 1)
            attend(hq)

        ck(8)
        if last and do_sample:
            P.barrier()
            o = 0
            IDXT, o = carve(RA, o, [128, NSEQ * NPG], F32)
            IDX = IDXT.bitcast(I32)
            PTB_, o = carve(RA, o, [128, NSEQ * NPG], F32)
            PTBi = PTB_.bitcast(I32)
            IT, o = carve(RA, o, [128, NPG + 1, NS], F32)
            CMP, o = carve(RA, o, [128, NPG + 1, NS], F32)
            MKS, o = carve(RA, o, [128, NPG + 1, NS], BF16)
            GG, o = carve(RA, o, [128, 32, IDIM], F32)
            GB, o = carve(RA, o, [128, 32, 128], BF16)
            TMPS, o = carve(RA, o, [128, 8, 64], F32)
            WB, o = carve(RA, o, [128, NSEQ, 64], F32)
            RW, o = carve(RA, o, [128, NSEQ * 64], F32)
            BS, o = carve(RA, o, [128, 256], F32)
            PSS, o = carve(RA, o, [128, 64], F32)
            PTSS, o = carve(RA, o, [128, 2, 64], BF16)
            RLS, o = carve(RA, o, [128, 64], F32)
            assert o <= 10304, o
            o = 0
            IKTD, o = carve(RS, o, [128, PAST], BF16)
            assert o <= 4096, o
            o = 2112
            KVG, o = carve(RC, o, [128, 3, 1024], F32)
            assert o <= 5808, o
            o = 0
            KBF, o = carve(RD, o, [128, 4, 512], BF16)
            VBF, o = carve(RD, o, [128, 4, 512], BF16)
            KTP, o = carve(RD, o, [128, 4, 512], BF16)
            assert o <= 3712, o
            pidx = misc[:, 0:1]
            cneg4 = misc[0:4, 1:5]
            delta = misc[0:NS, 8:24].rearrange("p (s q) -> p s q", q=TS)
            ones_f = misc[:, 32:33]
            ones16 = CF[0:NS, 2, :]
            E("sp", lambda h: h.dma_start(out=PTBi[:, :], in_=pt_d.partition_broadcast(128)), writes=[bf("PTB")], dma_key="ptb")
            E("dve", lambda h: h.tensor_copy(out=PTB_[:, :], in_=PTBi[:, :]), reads=[bf("PTB")], writes=[bf("PTBf")])
            E("dve", lambda h: h.tensor_scalar(out=IDXT[:, :], in0=PTB_[:, :], scalar1=128.0, scalar2=pidx, op0=ALU.mult, op1=ALU.add),
              reads=[bf("PTBf")] + CONSTS, writes=[bf("IDXf")])
            E("dve", lambda h: h.tensor_copy(out=IDX[:, :], in_=IDXT[:, :]), reads=[bf("IDXf")], writes=[bf("IDX")])
            ck(20)
            RW5 = RW.rearrange("p (s r j q) -> p s r j q", s=NSEQ, r=2, j=8)
            for hh in range(NIH):
                j, par = hh // 2, hh % 2
                E("dve", lambda h, j=j, par=par, hh=hh: h.tensor_scalar(out=RW5[0:NS, :, par, j, :], in0=delta, scalar1=WT[0:NS, 4, hh:hh + 1], scalar2=None, op0=ALU.mult),
                  reads=[bf("WT")] + CONSTS, writes=[bf("RW")])
            ONESF = SMALLT[:, 256:384]
            E("dve", lambda h: h.memset(ONESF, 1.0), writes=[bf("ONESF")])
            wbk, wbb = pb[5], Bpb[5]
            E("pe", lambda h: h.matmul(wbk[:, 0:256], lhsT=ONESF[0:NS, :], rhs=RW[0:NS, :], start=True, stop=True),
              reads=[bf("RW"), bf("ONESF")], writes=[wbb])
            E("act", lambda h: h.copy(out=WB[:, :, :].rearrange("p s c -> p (s c)"), in_=wbk[:, 0:256]), reads=[wbb], writes=[bf("WB")])
            ck(21)
            IQS = SMALLT[:, 160:224].bitcast(BF16).rearrange("p (s j q) -> p s j q", s=NSEQ, j=8)
            for s_ in range(NSEQ):
                E("act", lambda h, s_=s_: h.copy(out=IQS[:, s_, :, :], in_=IQT[:, :, TB + s_ * TS:TB + (s_ + 1) * TS]), reads=[bf("IQT")], writes=[bf("IQS")])
            E("dve", lambda h: h.memset(IT[:, NPG, :], NEG), writes=[bf("IT")])
            for s_ in range(NSEQ):
                scol = TB + s_ * TS
                for half in range(2):
                    for pl in range(32):
                        pg = half * 32 + pl
                        E("pool", lambda h, pl=pl, pg=pg, s_=s_: h.indirect_dma_start(
                            out=GG[:, pl, :], out_offset=None, in_=cik_d,
                            in_offset=bass.IndirectOffsetOnAxis(ap=IDX[:, s_ * NPG + pg:s_ * NPG + pg + 1], axis=0)),
                          reads=[bf("IDX")], writes=[bf("GG")], dma_key="gg", acc=True)
                    E("dve", lambda h: h.tensor_copy(out=GB[:, :, 0:IDIM], in_=GG[:, :, :]), reads=[bf("GG")], writes=[bf("GB")])
                    E("act", lambda h: h.copy(out=GB[:, :, IDIM:128], in_=GG[:, :, :]), reads=[bf("GG")], writes=[bf("GB")])
                    for q4 in range(8):
                        tbk, tbb = next_tb()
                        for jj in range(4):
                            E("pe", lambda h, tbk=tbk, jj=jj, q4=q4: h.matmul(tbk[:, jj * 128:(jj + 1) * 128], lhsT=GB[:, q4 * 4 + jj, :], rhs=identb, start=True, stop=True),
                              reads=[bf("GB")] + CONSTS, writes=[tbb])
                        p0 = (half * 32 + q4 * 4) * 128
                        E("act", lambda h, tbk=tbk, p0=p0: h.copy(out=IKTD[:, p0:p0 + 512], in_=tbk[:, 0:512]), reads=[tbb], writes=[bf("IKTD")])
                ck(22)
                for p8 in range(8):
                    banks2 = [next_bank(), next_bank()]
                    for pl in range(8):
                        pg = p8 * 8 + pl
                        for par in range(2):
                            bank, bb = banks2[par]
                            E("pe", lambda h, bank=bank, pl=pl, pg=pg, par=par, s_=s_: h.matmul(
                                bank[:, pl * 32:pl * 32 + 32],
                                lhsT=IKTD[par * 64:(par + 1) * 64, pg * 128:(pg + 1) * 128],
                                rhs=IQS[par * 64:(par + 1) * 64, s_, :, :].rearrange("p j q -> p (j q)"), start=True, stop=True),
                              reads=[bf("IKTD"), bf("IQS")], writes=[bb])
                    ck(30)
                    for par in range(2):
                        bank, bb = banks2[par]
                        E("dve", lambda h, bank=bank, s_=s_, par=par: h.scalar_tensor_tensor(
                            out=TMPS[:, :, par * 32:(par + 1) * 32], in0=bank[:, 0:256].rearrange("p (g c) -> p g c", c=32), scalar=0.0,
                            in1=WB[:, s_, par * 32:(par + 1) * 32].unsqueeze(1).to_broadcast([128, 8, 32]), op0=ALU.max, op1=ALU.mult),
                          reads=[bb, bf("WB")], writes=[bf("TMPS")])
                    ck(31)
                    E("dve", lambda h, p8=p8, s_=s_: h.tensor_reduce(
                        out=IT[:, p8 * 8:(p8 + 1) * 8, s_ * TS:(s_ + 1) * TS],
                        in_=TMPS[:, :, :].rearrange("p g (hh q) -> p g q hh", q=TS), axis=AX.X, op=ALU.add),
                      reads=[bf("TMPS")], writes=[bf("IT")])
                ck(32)
                banks2 = [next_bank(), next_bank()]
                for par in range(2):
                    bank, bb = banks2[par]
                    E("pe", lambda h, bank=bank, par=par, s_=s_: h.matmul(
                        bank[0:TS, 0:32], lhsT=IKT[par * 64:(par + 1) * 64, T + s_ * TS:T + (s_ + 1) * TS],
                        rhs=IQS[par * 64:(par + 1) * 64, s_, :, :].rearrange("p j q -> p (j q)"), start=True, stop=True),
                      reads=[bf("IKT"), bf("IQS")], writes=[bb])
                    E("dve", lambda h, bank=bank, s_=s_, par=par: h.scalar_tensor_tensor(out=TMPS[0:TS, 0, par * 32:(par + 1) * 32], in0=bank[0:TS, 0:32], scalar=0.0,
                                                                                       in1=WB[0:TS, s_, par * 32:(par + 1) * 32], op0=ALU.max, op1=ALU.mult),
                      reads=[bb, bf("WB")], writes=[bf("TMPS")])
                E("dve", lambda h, s_=s_: h.tensor_reduce(out=IT[0:TS, NPG, s_ * TS:(s_ + 1) * TS],
                                                          in_=TMPS[0:TS, 0, :].rearrange("p (hh q) -> p q hh", q=TS), axis=AX.X, op=ALU.add),
                  reads=[bf("TMPS")], writes=[bf("IT")])
                E("dve", lambda h, s_=s_: h.tensor_tensor(out=IT[0:TS, NPG, s_ * TS:(s_ + 1) * TS], in0=IT[0:TS, NPG, s_ * TS:(s_ + 1) * TS], in1=cneg4, op=ALU.add),
                  reads=CONSTS, writes=[bf("IT")])
            ck(23)
            MXP = BS[:, 0:16]
            MNP = BS[:, 16:32]
            LO = BS[:, 32:33]
            HI = BS[:, 33:34]
            MID = BS[:, 34:35]
            GE = BS[:, 35:36]
            DLT = BS[:, 36:37]
            DG = BS[:, 48:64]
            TRS = BS[:, 64:80]
            CNTP = BS[:, 80:96]
            IT_sg = IT[:, :, :].rearrange("p g s -> p s g")
            E("dve", lambda h: h.tensor_reduce(out=MXP, in_=IT_sg, axis=AX.X, op=ALU.max), reads=[bf("IT")], writes=[bf("MXP")])
            E("dve", lambda h: h.tensor_reduce(out=MNP, in_=IT[:, 0:NPG, :].rearrange("p g s -> p s g"), axis=AX.X, op=ALU.min), reads=[bf("IT")], writes=[bf("MNP")])
            bk5, bb5 = pb[5], Bpb[5]
            E("pe", lambda h: h.transpose(bk5[0:NS, 0:128], MXP, identf), reads=[bf("MXP")] + CONSTS, writes=[bb5])
            E("dve", lambda h: h.tensor_reduce(out=HI[0:NS, :], in_=bk5[0:NS, 0:128], axis=AX.X, op=ALU.max), reads=[bb5], writes=[bf("HI")])
            E("dve", lambda h: h.tensor_scalar(out=HI[0:NS, :], in0=HI[0:NS, :], scalar1=1.0, scalar2=None, op0=ALU.add), writes=[bf("HI")])
            E("pe", lambda h: h.transpose(bk5[0:NS, 128:256], MNP, identf), reads=[bf("MNP")] + CONSTS, writes=[bb5])
            E("dve", lambda h: h.tensor_reduce(out=LO[0:NS, :], in_=bk5[0:NS, 128:256], axis=AX.X, op=ALU.min), reads=[bb5], writes=[bf("LO")])

            def thresh_compare(src, out_view, out_buf):
                E("dve", lambda h: h.tensor_scalar(out=DG[0:NS, :], in0=identf[0:NS, 0:NS], scalar1=src[0:NS, 0:1], scalar2=None, op0=ALU.mult),
                  reads=[bf("LO"), bf("HI"), bf("MID")] + CONSTS, writes=[bf("DG")])
                E("pe", lambda h: h.matmul(bk5[:, 256:272], lhsT=ONESF[0:NS, :], rhs=DG[0:NS, :], start=True, stop=True),
                  reads=[bf("DG"), bf("ONESF")], writes=[bb5])
                E("act", lambda h: h.copy(out=TRS, in_=bk5[:, 256:272]), reads=[bb5], writes=[bf("TRS")])
                E("dve", lambda h: h.tensor_tensor(out=out_view, in0=IT[:, :, :], in1=TRS.unsqueeze(1).to_broadcast([128, NPG + 1, NS]), op=ALU.is_ge),
                  reads=[bf("IT"), bf("TRS")], writes=[out_buf])

            for it in range(NBIS):
                E("dve", lambda h: h.tensor_tensor(out=MID[0:NS, :], in0=LO[0:NS, :], in1=HI[0:NS, :], op=ALU.add), reads=[bf("LO"), bf("HI")], writes=[bf("MID")])
                E("dve", lambda h: h.tensor_scalar(out=MID[0:NS, :], in0=MID[0:NS, :], scalar1=0.5, scalar2=None, op0=ALU.mult), writes=[bf("MID")])
                thresh_compare(MID, CMP[:, :, :], bf("CMP"))
                E("dve", lambda h: h.tensor_reduce(out=CNTP, in_=CMP[:, :, :].rearrange("p g s -> p s g"), axis=AX.X, op=ALU.add), reads=[bf("CMP")], writes=[bf("CNTP")])
                E("pe", lambda h: h.matmul(bk5[0:NS, 288:289], lhsT=CNTP, rhs=ONESF[:, 0:1], start=True, stop=True), reads=[bf("CNTP"), bf("ONESF")], writes=[bb5])
                E("dve", lambda h: h.tensor_scalar(out=GE[0:NS, :], in0=bk5[0:NS, 288:289], scalar1=float(TOPK), scalar2=None, op0=ALU.is_ge), reads=[bb5], writes=[bf("GE")])
                E("dve", lambda h: h.tensor_tensor(out=DLT[0:NS, :], in0=MID[0:NS, :], in1=LO[0:NS, :], op=ALU.subtract), reads=[bf("MID"), bf("LO")], writes=[bf("DLT")])
                E("dve", lambda h: h.scalar_tensor_tensor(out=LO[0:NS, :], in0=DLT[0:NS, :], scalar=GE[0:NS, 0:1], in1=LO[0:NS, :], op0=ALU.mult, op1=ALU.add),
                  reads=[bf("DLT"), bf("GE")], writes=[bf("LO")])
                E("dve", lambda h: h.tensor_tensor(out=DLT[0:NS, :], in0=HI[0:NS, :], in1=MID[0:NS, :], op=ALU.subtract), reads=[bf("MID"), bf("HI")], writes=[bf("DLT")])
                E("dve", lambda h: h.scalar_tensor_tensor(out=HI[0:NS, :], in0=DLT[0:NS, :], scalar=GE[0:NS, 0:1], in1=MID[0:NS, :], op0=ALU.mult, op1=ALU.add),
                  reads=[bf("DLT"), bf("GE"), bf("MID")], writes=[bf("HI")])
            thresh_compare(LO, MKS[:, :, :], bf("MKS"))

            ck(24)
            P.barrier()
            KVG2 = RS[:, 0:4096].rearrange("p (s c) -> p s c", c=1024)

            def kvg(slot):
                return (KVG[:, slot, :] if slot < 3 else KVG2[:, slot - 3, :]), bf(f"KVG{slot}")
            for s_ in range(NSEQ):
                ob, obb = pb[2], Bpb[2]
                lb, lbb = pb[3], Bpb[3]

                def st_g(pg, s_=s_):
                    if pg >= NPG:
                        return
                    gv, gb_ = kvg(pg % 7)
                    col = s_ * NPG + pg
                    E("pool", lambda h: h.indirect_dma_start(out=gv, out_offset=None, in_=ckv_d,
                                                             in_offset=bass.IndirectOffsetOnAxis(ap=IDX[:, col:col + 1], axis=0)),
                      reads=[bf("IDX")], writes=[gb_], dma_key=f"kvg{pg % 7}")

                def st_a(pg, s_=s_):
                    if pg == NPG:
                        return
                    sl = pg % 4
                    gv, gb_ = kvg(pg % 7)
                    E("dve", lambda h: h.tensor_copy(out=KBF[:, sl, :], in_=gv[:, 0:512]), reads=[gb_], writes=[bf(f"KBF{sl}")])
                    E("act", lambda h: h.copy(out=VBF[:, sl, :], in_=gv[:, 512:1024]), reads=[gb_], writes=[bf(f"VBF{sl}")])
                    tbk, tbb = next_tb()
                    for g in range(NKV):
                        E("pe", lambda h, g=g: h.matmul(tbk[:, g * 128:(g + 1) * 128], lhsT=KBF[:, sl, g * HD:(g + 1) * HD], rhs=identb, start=True, stop=True),
                          reads=[bf(f"KBF{sl}")] + CONSTS, writes=[tbb])
                    E("act", lambda h: h.copy(out=KTP[:, sl, :], in_=tbk[:, 0:512]), reads=[tbb], writes=[bf(f"KTP{sl}")])

                def st_b(pg, s_=s_):
                    new = (pg == NPG)
                    sl = pg % 4
                    sbk, sbb = pb[pg % 2], Bpb[pg % 2]
                    np_ = TS if new else 128
                    pts = PTSS[:, pg % 2, :]
                    ptsb = bf(f"PTS{pg % 2}")
                    for g in range(NKV):
                        if new:
                            lhs = KT[:, g, T + s_ * TS:T + (s_ + 1) * TS]
                            rd = [bf("KT")]
                        else:
                            lhs = KTP[:, sl, g * 128:(g + 1) * 128]
                            rd = [bf(f"KTP{sl}")]
                        E("pe", lambda h, g=g, lhs=lhs: h.matmul(sbk[0:np_, g * 16:(g + 1) * 16], lhsT=lhs,
                                                               rhs=QS[:, s_, 4 * g:4 * g + 4, :].rearrange("p a q -> p (a q)"), start=True, stop=True),
                          reads=rd + [bf("QS")], writes=[sbb])
                    E("act", lambda h: h.activation(out=PSS[0:np_, :], in_=sbk[0:np_, 0:64], func=AF.Exp, scale=scale), reads=[sbb], writes=[bf("PSS")])
                    E("dve", lambda h: h.tensor_tensor(
                        out=pts[0:np_, :].rearrange("p (a q) -> p a q", q=TS), in0=PSS[0:np_, :].rearrange("p (a q) -> p a q", q=TS),
                        in1=MKS[0:np_, pg, s_ * TS:(s_ + 1) * TS].unsqueeze(1).to_broadcast([np_, 16, TS]), op=ALU.mult),
                      reads=[bf("PSS"), bf("MKS")], writes=[ptsb])

                def st_c(pg, s_=s_):
                    new = (pg == NPG)
                    sl = pg % 4
                    np_ = TS if new else 128
                    pts = PTSS[:, pg % 2, :]
                    ptsb = bf(f"PTS{pg % 2}")
                    for g in range(NKV):
                        if new:
                            lhs = VS[0:TS, s_, g * HD:(g + 1) * HD]
                            rd = [bf("VS")]
                        else:
                            lhs = VBF[:, sl, g * HD:(g + 1) * HD]
                            rd = [bf(f"VBF{sl}")]
                        E("pe", lambda h, g=g, lhs=lhs: h.matmul(ob[:, g * 16:(g + 1) * 16], lhsT=lhs, rhs=pts[0:np_, g * 16:(g + 1) * 16],
                                                               start=(pg == 0), stop=new),
                          reads=rd + [ptsb], writes=[obb])
                    E("pe", lambda h: h.matmul(lb[:, 0:64], lhsT=onesb[0:np_, :], rhs=pts[0:np_, :], start=(pg == 0), stop=new),
                      reads=[ptsb] + CONSTS, writes=[lbb])

                NP1 = NPG + 1
                GA = 4
                for step in range(NP1 + 2 + GA):
                    if step < NP1:
                        st_g(step)
                    if GA <= step < NP1 + GA:
                        st_a(step - GA)
                    if GA + 1 <= step <= NP1 + GA:
                        st_b(step - GA - 1)
                    if step >= GA + 2:
                        st_c(step - GA - 2)
                E("act", lambda h, lb=lb: h.activation(out=RLS, in_=lb[:, 0:64], func=AF.Ln), reads=[lbb], writes=[bf("RLS")])
                E("act", lambda h: h.activation(out=RLS, in_=RLS, func=AF.Exp, scale=-1.0), writes=[bf("RLS")])
                E("dve", lambda h, ob=ob, s_=s_: h.tensor_tensor(out=OT[:, :, TB + s_ * TS:TB + (s_ + 1) * TS],
                                                                in0=ob[:, 0:64].rearrange("p (a q) -> p a q", q=TS),
                                                                in1=RLS.rearrange("p (a q) -> p a q", q=TS), op=ALU.mult),
                  reads=[obb, bf("RLS")], writes=[bf("OT")])
        elif last:
            E("dve", lambda h: h.memset(OT[:, :, TB:NTB], 0.0), writes=[bf("OT")])

        P.barrier()
        ck(9)
        asrc = lambda k, c0, n: AT[:, k, c0:c0 + n]
        osrc = lambda k, c0, n: OT[:, k, c0:c0 + n]
        for j in range(KC):
            def epi_ga(si, c0, n, bank, bb):
                E("act", lambda h: h.activation(out=TB1[:, c0:c0 + n], in_=bank[:, 0:n], func=AF.Sigmoid), reads=[bb], writes=[bf("TB1")])
            proj(w_in_d[TI_GA + j], KC, 128, xsrc, XB, segs, epi_ga)

            def epi_ya(si, c0, n, bank, bb):
                E("dve", lambda h: h.tensor_tensor(out=TB1[:, c0:c0 + n], in0=bank[:, 0:n], in1=TB1[:, c0:c0 + n], op=ALU.mult), reads=[bb], writes=[bf("TB1")])
            proj(w_a_d[j], 8, 128, asrc, [bf("AT")], segs, epi_ya)

            def epi_gb(si, c0, n, bank, bb):
                E("act", lambda h: h.activation(out=TB2[:, c0:c0 + n], in_=bank[:, 0:n], func=AF.Sigmoid), reads=[bb], writes=[bf("TB2")])
            proj(w_in_d[TI_GB + j], KC, 128, xsrc, XB, segs, epi_gb)

            def epi_yb(si, c0, n, bank, bb, j=j):
                E("dve", lambda h: h.tensor_tensor(out=TB2[:, c0:c0 + n], in0=bank[:, 0:n], in1=TB2[:, c0:c0 + n], op=ALU.mult), reads=[bb], writes=[bf("TB2")])
                E("dve", lambda h: h.tensor_tensor(out=MT[:, j, c0:c0 + n], in0=TB1[:, c0:c0 + n], in1=TB2[:, c0:c0 + n], op=ALU.add),
                  reads=[bf("TB1"), bf("TB2")], writes=[bf("MT"), bf("ACTT")])
            proj(w_at_d[j], KC, 128, osrc, [bf("OT")], segs, epi_yb)

        pass

        def proj_to_tok(wsel, kc, srcf, src_bufs, first):
            for jg in range(4):
                for jj in range(4):
                    j = jg * 4 + jj
                    def epi_s(si, c0, n, bank, bb, jj=jj):
                        E("act", lambda h: h.copy(out=STG4[:, jj, c0:c0 + n], in_=bank[:, 0:n]), reads=[bb], writes=[bf(f"STG{jj}")])
                    proj(wsel(j), kc, 128, srcf, src_bufs, segs, epi_s)
                for tt in range(ntiles):
                    _, nr = tile_rows(tt)
                    lc0 = tt * 128
                    bank, bb = next_bank()
                    for jj in range(4):
                        E("pe", lambda h, bank=bank, jj=jj, nr=nr, lc0=lc0: h.transpose(bank[0:nr, jj * 128:(jj + 1) * 128], STG4[:, jj, lc0:lc0 + nr], identf),
                          reads=[bf(f"STG{jj}")] + CONSTS, writes=[bb])
                    tv = TOK[0:nr, tt, jg * 512:(jg + 1) * 512]
                    if first:
                        E("dve", lambda h, bank=bank, nr=nr, tv=tv: h.scalar_tensor_tensor(out=tv, in0=tv, scalar=ALPHA, in1=bank[0:nr, 0:512], op0=ALU.mult, op1=ALU.add),
                          reads=[bb], writes=[bf(f"TOK{tt}")])
                    else:
                        E("dve", lambda h, bank=bank, nr=nr, tv=tv: h.tensor_tensor(out=tv, in0=tv, in1=bank[0:nr, 0:512], op=ALU.add),
                          reads=[bb], writes=[bf(f"TOK{tt}")])

        def ln_tok(which):
            E("sp", lambda h: h.dma_start(out=LNB[:, 0, :], in_=ln_d[2 * which].partition_broadcast(128)), writes=[bf("OT")], dma_key="lnb")
            E("sp", lambda h: h.dma_start(out=LNB[:, 1, :], in_=ln_d[2 * which + 1].partition_broadcast(128)), writes=[bf("OT")], dma_key="lnb")
            for tt in range(ntiles):
                _, nr = tile_rows(tt)
                tb_ = bf(f"TOK{tt}")
                tv = TOK[:, tt, :]
                MV, RSv = layer_norm_rows(tv, nr, D, tb_, "ln")
                E("dve", lambda h, nr=nr, tv=tv: h.tensor_scalar(out=tv[0:nr, :], in0=tv[0:nr, :], scalar1=MV[0:nr, 0:1], scalar2=RSv[0:nr, 0:1], op0=ALU.subtract, op1=ALU.mult),
                  reads=[bf("MV"), bf("RSv")], writes=[tb_])
                E("dve", lambda h, nr=nr, tv=tv: h.tensor_tensor(out=tv[0:nr, :], in0=tv[0:nr, :], in1=LNB[0:nr, 0, :], op=ALU.mult), reads=[bf("OT")], writes=[tb_])
                E("dve", lambda h, nr=nr, tv=tv: h.tensor_tensor(out=tv[0:nr, :], in0=tv[0:nr, :], in1=LNB[0:nr, 1, :], op=ALU.add), reads=[bf("OT")], writes=[tb_])

        ck(10)
        for tt in range(ntiles):
            r0, nr = tile_rows(tt)
            E("sp", lambda h, tt=tt, r0=r0, nr=nr: h.dma_start(out=TOK[0:nr, tt, :], in_=xtok_d[r0:r0 + nr, :]), writes=[bf(f"TOK{tt}")], dma_key=f"tok{tt}")
        msrc = lambda k, c0, n: MT[:, k, c0:c0 + n]
        proj_to_tok(lambda j: w_mix_d[j], KC, msrc, [bf("MT")], True)
        ln_tok(0)
        HBv = RD[:, 0:1024].bitcast(BF16)
        for tt in range(ntiles):
            _, nr = tile_rows(tt)
            lc0 = tt * 128
            E("act", lambda h, nr=nr, tt=tt: h.copy(out=HBv[0:nr, :], in_=TOK[0:nr, tt, :]), reads=[bf(f"TOK{tt}")],
              writes=[bf("STG0"), bf("STG1"), bf("STG2"), bf("STG3")])
            for jg in range(4):
                tbk, tbb = next_tb()
                for jj in range(4):
                    j = jg * 4 + jj
                    E("pe", lambda h, tbk=tbk, jj=jj, j=j, nr=nr: h.matmul(tbk[:, jj * 128:jj * 128 + nr], lhsT=HBv[0:nr, j * 128:(j + 1) * 128], rhs=identb[0:nr, 0:nr], start=True, stop=True),
                      reads=[bf("STG0")] + CONSTS, writes=[tbb])
                E("act", lambda h, tbk=tbk, jg=jg, nr=nr, lc0=lc0: h.copy(out=XT[:, jg * 4:jg * 4 + 4, lc0:lc0 + nr],
                                                                        in_=tbk[:, 0:512].rearrange("p (a t) -> p a t", t=128)[:, :, 0:nr]),
                  reads=[tbb], writes=[bf("XT")])
        pass

        ck(11)
        UB = TA[:, 0:TB + 2]
        UBS = TA[:, 520:544].rearrange("p (s t) -> p s t", t=6)
        hsrc = lambda k, c0, n: XT[:, k, c0:c0 + n]
        for hf in range(2):
            for c in range(22):
                cc = hf * 22 + c
                E("act", lambda h, cc=cc: h.copy(out=UB[:, 0:2], in_=UH[:, cc, :]), reads=[bf("UH")], writes=[bf("TA")])
                if last:
                    E("act", lambda h, cc=cc: h.copy(out=UBS[:, :, 0:2], in_=SCF[:, cc, :, :]), reads=[bf("SCF")], writes=[bf("TA")])

                def epi_u(si, c0, n, bank, bb):
                    if si == 0:
                        E("act", lambda h: h.copy(out=UB[:, 2:2 + TB], in_=bank[:, 0:TB]), reads=[bb], writes=[bf("TA")])
                    else:
                        E("act", lambda h: h.copy(out=UBS[:, :, 2:6], in_=bank[:, 0:NS].rearrange("p (s t) -> p s t", t=TS)), reads=[bb], writes=[bf("TA")])
                proj(w_up_d[cc], KC, 128, hsrc, XB, segs, epi_u)
                E("dve", lambda h, cc=cc: h.tensor_scalar(out=TB1[:, 0:TB], in0=UB[:, 2:2 + TB], scalar1=CFW[:, cc, 2:3], scalar2=CFB[:, cc:cc + 1], op0=ALU.mult, op1=ALU.add),
                  reads=[bf("TA"), bf("SM")], writes=[bf("TB1")])
                E("dve", lambda h, cc=cc: h.scalar_tensor_tensor(out=TB1[:, 0:TB], in0=UB[:, 1:1 + TB], scalar=CFW[:, cc, 1:2], in1=TB1[:, 0:TB], op0=ALU.mult, op1=ALU.add),
                  reads=[bf("TA"), bf("SM")], writes=[bf("TB1")])
                E("dve", lambda h, cc=cc: h.scalar_tensor_tensor(out=TB1[:, 0:TB], in0=UB[:, 0:TB], scalar=CFW[:, cc, 0:1], in1=TB1[:, 0:TB], op0=ALU.mult, op1=ALU.add),
                  reads=[bf("TA"), bf("SM")], writes=[bf("TB1")])
                if last:
                    E("dve", lambda h, cc=cc: h.tensor_scalar(out=ACCS, in0=UBS[:, :, 2:6], scalar1=CFW[:, cc, 2:3], scalar2=CFB[:, cc:cc + 1], op0=ALU.mult, op1=ALU.add),
                      reads=[bf("TA"), bf("SM")], writes=[bf("TB1")])
                    E("dve", lambda h, cc=cc: h.scalar_tensor_tensor(out=ACCS, in0=UBS[:, :, 1:5], scalar=CFW[:, cc, 1:2], in1=ACCS, op0=ALU.mult, op1=ALU.add),
                      reads=[bf("TA"), bf("SM")], writes=[bf("TB1")])
                    E("dve", lambda h, cc=cc: h.scalar_tensor_tensor(out=ACCS, in0=UBS[:, :, 0:4], scalar=CFW[:, cc, 0:1], in1=ACCS, op0=ALU.mult, op1=ALU.add),
                      reads=[bf("TA"), bf("SM")], writes=[bf("TB1")])
                    E("act", lambda h, cc=cc: h.copy(out=CFS[:, cc, :, :], in_=UBS[:, :, 4:6]), reads=[bf("TA")], writes=[bf("CFS")])
                E("act", lambda h, cc=cc: h.copy(out=UH[:, cc, :], in_=UB[:, TB:TB + 2]), reads=[bf("TA")], writes=[bf("UH")])
                E("act", lambda h: h.activation(out=TB2[:, 0:ntok], in_=TB1[:, 0:ntok], func=AF.Gelu_apprx_tanh), reads=[bf("TB1")], writes=[bf("TB2")])

                def epi_g(si, c0, n, bank, bb, c=c):
                    E("dve", lambda h: h.tensor_tensor(out=ACTT[:, c, c0:c0 + n], in0=bank[:, 0:n], in1=TB2[:, c0:c0 + n], op=ALU.mult),
                      reads=[bb, bf("TB2")], writes=[bf("ACTT"), bf("MT")])
                proj(w_gate_d[cc], KC, 128, hsrc, XB, segs, epi_g)
            fsrc = lambda k, c0, n: ACTT[:, k, c0:c0 + n]
            proj_to_tok(lambda j, hf=hf: w_dn_d[j, hf], 22, fsrc, [bf("ACTT")], hf == 0)
        ln_tok(1)
        for tt in range(ntiles):
            r0, nr = tile_rows(tt)
            E("sp", lambda h, tt=tt, r0=r0, nr=nr: h.dma_start(out=y_o[r0:r0 + nr, :], in_=TOK[0:nr, tt, :]), reads=[bf(f"TOK{tt}")], dma_key=f"tok{tt}")
        P.barrier()

    try:
        for b_ in range(nblk_run):
            do_block(b_)
    except _Stop:
        pass
    E("sp", lambda h: h.dma_start(out=ca_o, in_=AH[:, :, :].rearrange("p a b -> p (a b)")), reads=[bf("AH")], dma_key="o_ca")
    E("sp", lambda h: h.dma_start(out=cf_o, in_=UH[:, :, :].rearrange("p a b -> p (a b)")), reads=[bf("UH")], dma_key="o_cf")
    E("sp", lambda h: h.dma_start(out=cas_o, in_=CAS[:, :, :, :].rearrange("p a b c -> p (a b c)")), reads=[bf("CAS")], dma_key="o_cas")
    E("sp", lambda h: h.dma_start(out=cfs_o, in_=CFS[:, :, :, :].rearrange("p a b c -> p (a b c)")), reads=[bf("CFS")], dma_key="o_cfs")

    P.finalize(st)
    st.close()
    return nc


def _tile_w(w, kc):
    K, N = w.shape
    assert K == kc * 128 and N % 128 == 0
    return np.ascontiguousarray(w.reshape(kc, 128, N // 128, 128).transpose(2, 1, 0, 3).reshape(N // 128, 128, kc * 128))


def _rope_tables():
    pos = np.concatenate([np.arange(T, dtype=np.float32), PAST + np.arange(NS, dtype=np.float32) % TS])
    pos = pos.astype(np.float32)

    def cs(half):
        inv = (np.float32(500000.0) ** (-np.arange(half, dtype=np.float32) / np.float32(half))).astype(np.float32)
        ang = (pos[:, None] * inv[None, :]).astype(np.float32)
        return np.cos(ang).astype(np.float32), np.sin(ang).astype(np.float32)

    c16, s16 = cs(16)
    c8, s8 = cs(8)
    fm = np.zeros((128, 4, NTOT), np.float32)
    fm[:, 0, :] = 1.0
    fm[:, 2, :] = 1.0
    fm[0:16, 0, :] = c16.T
    fm[16:32, 0, :] = c16.T
    fm[0:16, 1, :] = -s16.T
    fm[16:32, 1, :] = s16.T
    for o in (0, 64):
        fm[o:o + 8, 2, :] = c8.T
        fm[o + 8:o + 16, 2, :] = c8.T
        fm[o:o + 8, 3, :] = -s8.T
        fm[o + 8:o + 16, 3, :] = s8.T
    tok = np.concatenate([c8, s8], axis=1).astype(np.float32)
    return fm, tok


def _consts():
    cbf = np.zeros((128, 4, 128), np.float32)
    cbf[:, 0, :] = np.eye(128, dtype=np.float32)
    for m in range(16):
        cbf[m + 16, 1, m] = 1.0
        cbf[m, 1, m + 16] = 1.0
    for o in (0, 64):
        for m in range(8):
            cbf[o + m + 8, 2, o + m] = 1.0
            cbf[o + m, 2, o + m + 8] = 1.0
    cbf[:, 3, :] = 1.0
    cf = np.zeros((128, 3, 128), np.float32)
    cf[:, 0, :] = np.eye(128, dtype=np.float32)
    t = np.arange(128)[:, None]
    s = np.arange(128)[None, :]
    cf[:, 1, :] = np.where(s <= t, 0.0, NEGBIG).astype(np.float32)
    cf[:, 2, 0] = np.arange(128, dtype=np.float32)
    j = np.arange(4)[:, None]
    q = np.arange(4)[None, :]
    cf[0:4, 2, 1:5] = np.where(j <= q, 0.0, NEG).astype(np.float32)
    for tok in range(NS):
        cf[tok, 2, 8 + tok] = 1.0
    cf[:, 2, 32:64] = 1.0
    cf[:, 2, 64:96] = (2.0 ** -(np.arange(32, dtype=np.float64) + 1.0)).astype(np.float32)[None, :]
    return cbf.reshape(128, 512), cf.reshape(128, 384)


_PROGRAM = None


def _prepare_shared(inp):
    w_in = np.asarray(inp["w_in"][0], np.float32)
    offs = np.cumsum([0, 1024, 1024, 1024, 2048, 512, 512, 1024, 64, 16, 2048, 2048])
    oB, oC, oH, oQ, oK, oV, oIQ, oIK, oIW, oGA, oGB = offs[:11]
    cols = []
    for c in range(8):
        for base in (oC, oH, oB):
            cols.append(np.arange(base + c * 128, base + (c + 1) * 128))
    for base, n in ((oQ, 16), (oK, 4), (oV, 4), (oIQ, 8), (oGA, 16), (oGB, 16)):
        for i in range(n):
            cols.append(np.arange(base + i * 128, base + (i + 1) * 128))
    cols = np.concatenate(cols)
    assert cols.size == N_WIN_TILES * 128
    w_in_t = _tile_w(w_in[:, cols], KC)
    w_ikw = np.ascontiguousarray(w_in[:, oIK:oIK + 80].reshape(KC, 128, 80).transpose(1, 0, 2).reshape(128, KC * 80))
    w_a_t = _tile_w(np.asarray(inp["w_a_out"][0], np.float32), 8)
    w_attn_t = _tile_w(np.asarray(inp["w_attn_out"][0], np.float32), KC)
    w_mix_t = _tile_w(np.asarray(inp["w_mix_out"][0], np.float32), KC)
    w_up_t = _tile_w(np.asarray(inp["w_up"][0], np.float32), KC)
    w_gate_t = _tile_w(np.asarray(inp["w_gate"][0], np.float32), KC)
    wd = _tile_w(np.asarray(inp["w_down"][0], np.float32), NFC)
    w_down_t = np.ascontiguousarray(wd.reshape(16, 128, 2, 22 * 128).transpose(0, 2, 1, 3))
    caw = np.asarray(inp["conv_a_w"][0], np.float32)
    cfw = np.asarray(inp["conv_ffn_w"][0], np.float32)
    cfb = np.asarray(inp["conv_ffn_b"][0], np.float32)
    small = np.concatenate([
        caw.T.reshape(8, 128, 3).transpose(1, 0, 2).reshape(128, 24),
        cfw.T.reshape(NFC, 128, 3).transpose(1, 0, 2).reshape(128, NFC * 3),
        cfb.reshape(NFC, 128).T], axis=1).astype(np.float32)
    ikgb = np.stack([np.asarray(inp["idx_k_norm_g"][0], np.float32), np.asarray(inp["idx_k_norm_b"][0], np.float32)])
    lnp = np.stack([np.asarray(inp[k][0], np.float32) for k in ("ln1_g", "ln1_b", "ln2_g", "ln2_b")])
    rope_fm, rope_tok = _rope_tables()
    cbf, cf = _consts()
    d = {}
    o = 0
    for i, n in enumerate(WIN_SPLIT):
        d[f"w_in_t{i}"] = np.ascontiguousarray(w_in_t[o:o + n])
        o += n
    for i in range(2):
        d[f"w_up_t{i}"] = np.ascontiguousarray(w_up_t[22 * i:22 * (i + 1)])
        d[f"w_gate_t{i}"] = np.ascontiguousarray(w_gate_t[22 * i:22 * (i + 1)])
        d[f"w_down_t{i}"] = np.ascontiguousarray(w_down_t[:, i])
    d.update(dict(w_ikw=w_ikw, w_a_t=w_a_t, w_attn_t=w_attn_t, w_mix_t=w_mix_t, small=np.ascontiguousarray(small), ikgb=ikgb, lnp=lnp,
                rope_fm=rope_fm, rope_tok=rope_tok, cbf=cbf, cf=cf,
                cache_kv=np.concatenate([np.asarray(inp["cache_k"], np.float32).reshape(NPHYS * 128, NKV * HD),
                                         np.asarray(inp["cache_v"], np.float32).reshape(NPHYS * 128, NKV * HD)], axis=1),
                cache_ik=np.asarray(inp["cache_idx_k"], np.float32).reshape(NPHYS * 128, IDIM)))
    return d


def kernel(**inp):
    global _PROGRAM
    if _PROGRAM is None:
        _PROGRAM = build_program()
    nc = _PROGRAM
    shared = _prepare_shared(inp)
    x_prompt = np.asarray(inp["x_prompt"], np.float32)
    x_sample = np.asarray(inp["x_sample"], np.float32)
    sca_all = np.asarray(inp["state_conv_a"][0], np.float32)
    scf_all = np.asarray(inp["state_conv_ffn"][0], np.float32)
    pt_all = np.asarray(inp["page_table"], np.int32)
    in_maps = []
    for c in range(8):
        xs = x_sample[NSEQ * c:NSEQ * (c + 1)].reshape(NS, D)
        xtok = np.concatenate([x_prompt[c], xs], axis=0)
        m = dict(shared)
        m["xtok"] = np.ascontiguousarray(xtok)
        m["xT"] = np.ascontiguousarray(xtok.T)
        a = sca_all[NSEQ * c:NSEQ * (c + 1)]
        m["sca"] = np.ascontiguousarray(a.reshape(NSEQ, 2, 8, 128).transpose(3, 2, 0, 1).reshape(128, 8 * NSEQ * 2))
        f = scf_all[NSEQ * c:NSEQ * (c + 1)]
        m["scf"] = np.ascontiguousarray(f.reshape(NSEQ, 2, NFC, 128).transpose(3, 2, 0, 1).reshape(128, NFC * NSEQ * 2))
        m["pt"] = np.ascontiguousarray(pt_all[NSEQ * c:NSEQ * (c + 1)].reshape(NSEQ * NPG))
        in_maps.append(m)
    res = run_bass_kernel_spmd(nc, in_maps, core_ids=list(range(8)))
    R = res.results
    y_p = np.stack([R[c]["y_o"][:T] for c in range(8)])
    y_s = np.concatenate([R[c]["y_o"][T:].reshape(NSEQ, TS, D) for c in range(8)])

    def tok_major(name, c):
        return R[c][name].transpose(2, 0, 1)

    k_p = np.stack([tok_major("kT_o", c)[:T] for c in range(8)])[None]
    v_p = np.stack([tok_major("vT_o", c)[:T] for c in range(8)])[None]
    k_s = np.concatenate([tok_major("kT_o", c)[T:].reshape(NSEQ, TS, NKV, HD) for c in range(8)])[None]
    v_s = np.concatenate([tok_major("vT_o", c)[T:].reshape(NSEQ, TS, NKV, HD) for c in range(8)])[None]
    ik_p = np.stack([R[c]["ik_o"][:T] for c in range(8)])[None]
    ik_s = np.concatenate([R[c]["ik_o"][T:].reshape(NSEQ, TS, IDIM) for c in range(8)])[None]
    ca_p = np.stack([R[c]["ca_o"].reshape(128, 8, 2).transpose(2, 1, 0).reshape(2, DCONV) for c in range(8)])[None]
    cf_p = np.stack([R[c]["cf_o"].reshape(128, NFC, 2).transpose(2, 1, 0).reshape(2, DFF) for c in range(8)])[None]
    ca_s = np.concatenate([R[c]["cas_o"].reshape(128, 8, NSEQ, 2).transpose(2, 3, 1, 0).reshape(NSEQ, 2, DCONV) for c in range(8)])[None]
    cf_s = np.concatenate([R[c]["cfs_o"].reshape(128, NFC, NSEQ, 2).transpose(2, 3, 1, 0).reshape(NSEQ, 2, DFF) for c in range(8)])[None]
    outs = (y_p, y_s, k_p, v_p, ik_p, ca_p, cf_p, k_s, v_s, ik_s, ca_s, cf_s)
    return tuple(np.ascontiguousarray(o, dtype=np.float32) for o in outs)
```

```python
import os
from contextlib import ExitStack

import numpy as np
import concourse.bass as bass
import concourse.mybir as mybir
from concourse.bass_utils import run_bass_kernel_spmd

F32 = mybir.dt.float32
BF16 = mybir.dt.bfloat16
I32 = mybir.dt.int32
AF = mybir.ActivationFunctionType
ALU = mybir.AluOpType
AX = mybir.AxisListType

D = 2048
KC = 16
T = 2048
TB = 512
NBLK = 4
NSEQ = 4
TS = 4
NS = NSEQ * TS
NTOT = T + NS
DCONV = 1024
DFF = 5632
NFC = 44
NH = 16
NKV = 4
HD = 128
NIH = 16
IDIM = 64
PAST = 8192
NPG = 64
NPHYS = 2560
TOPK = 256
ALPHA = 2.0 ** 0.25
EPS = 1e-5
NEG = -1.0e30
NEGBIG = -3.0e38
NBIS = 27
TOPK_BISECT = True

TI_CONV, TI_Q, TI_K, TI_V, TI_IQ, TI_GA, TI_GB = 0, 24, 40, 44, 48, 56, 72
N_WIN_TILES = 88
WIN_SPLIT = (24, 16, 8, 8, 16, 16)


class _Stop(Exception):
    pass


class Buf:
    __slots__ = ("name", "w", "r", "rd", "excl")

    def __init__(self, name, excl=False):
        self.name = name
        self.w = []
        self.r = {}
        self.rd = []
        self.excl = excl


class Ins:
    __slots__ = ("eng", "fn", "deps", "needs_inc", "val", "dma_key", "dma_val", "is_dma", "acc")

    def __init__(self, eng, fn, is_dma=False, dma_key=None):
        self.acc = False
        self.eng = eng
        self.fn = fn
        self.deps = []
        self.needs_inc = False
        self.val = 0
        self.is_dma = is_dma
        self.dma_key = dma_key
        self.dma_val = 0


class Prog:
    ENGS = ("pe", "act", "dve", "pool", "sp")

    def __init__(self, nc):
        self.nc = nc
        self.streams = {e: [] for e in self.ENGS}
        self.same = {"act", "dve", "pool"}
        self.dma_keys = {}
        self.last_dma = {}
        self.all = []

    def emit(self, eng, fn, reads=(), writes=(), dma_key=None, acc=False):
        is_dma = dma_key is not None
        ins = Ins(eng, fn, is_dma, dma_key)
        ins.acc = acc
        deps = []
        for b in reads:
            deps.extend(b.w)
            if b.excl:
                deps.extend(v for k, v in b.r.items() if k != eng)
        for b in writes:
            if acc:
                deps.extend(w for w in b.w if not w.acc)
            else:
                deps.extend(b.w)
            deps.extend(b.r.values())
            deps.extend(b.rd)
        seen = set()
        for d in deps:
            if id(d) in seen:
                continue
            seen.add(id(d))
            if (not d.is_dma) and d.eng == eng and eng not in self.same:
                continue
            ins.deps.append(d)
        for b in writes:
            if acc:
                b.w = [w for w in b.w if w.acc] + [ins]
            else:
                b.w = [ins]
                b.r = {}
                b.rd = []
        for b in reads:
            if b in writes:
                continue
            if is_dma:
                b.rd.append(ins)
            else:
                b.r[eng] = ins
        if is_dma:
            self.dma_keys[dma_key] = self.dma_keys.get(dma_key, 0) + 1
            ins.dma_val = 16 * self.dma_keys[dma_key]
            self.last_dma[dma_key] = ins
        self.streams[eng].append(ins)
        self.all.append(ins)
        return ins

    def barrier(self):
        lasts = []
        for e in self.ENGS:
            for ins in reversed(self.streams[e]):
                if not ins.is_dma and ins.fn is not None:
                    lasts.append(ins)
                    break
        lasts.extend(self.last_dma.values())
        for e in self.ENGS:
            ins = Ins(e, None)
            ins.deps = list(lasts)
            self.streams[e].append(ins)
            self.all.append(ins)

    def finalize(self, stack):
        nc = self.nc
        for ins in self.all:
            for d in ins.deps:
                if not d.is_dma:
                    d.needs_inc = True
        sems = {e: stack.enter_context(nc.semaphore(f"s_{e}")) for e in self.ENGS}
        ksems = {k: stack.enter_context(nc.semaphore(f"k_{i}")) for i, k in enumerate(self.dma_keys)}
        for e in self.ENGS:
            c = 0
            for ins in self.streams[e]:
                if ins.needs_inc and not ins.is_dma:
                    c += 1
                    ins.val = c
        final = [(ksems[k], 16 * n) for k, n in self.dma_keys.items()]
        block = stack.enter_context(nc.Block())
        streams = self.streams

        def run(e, h):
            seen = {}
            for ins in streams[e]:
                need = {}
                for d in ins.deps:
                    if d.is_dma:
                        s, v = ksems[d.dma_key], d.dma_val
                    else:
                        s, v = sems[d.eng], d.val
                    cur = need.get(id(s))
                    if cur is None or cur[1] < v:
                        need[id(s)] = (s, v)
                for sid, (s, v) in need.items():
                    if seen.get(sid, 0) >= v:
                        continue
                    seen[sid] = v
                    h.wait_ge(s, v)
                if ins.fn is None:
                    continue
                r = ins.fn(h)
                if ins.is_dma:
                    r.then_inc(ksems[ins.dma_key], 16)
                elif ins.needs_inc:
                    r.then_inc(sems[e], 1)
            if e == "sp":
                for s, v in final:
                    h.wait_ge(s, v)

        @block.tensor
        def _(h):
            run("pe", h)

        @block.scalar
        def _(h):
            run("act", h)

        @block.vector
        def _(h):
            run("dve", h)

        @block.gpsimd
        def _(h):
            run("pool", h)

        @block.sync
        def _(h):
            run("sp", h)


def build_program(nblk_run=NBLK, do_sample=True, nphys=NPHYS, stop=None):
    def ck(i):
        if stop is not None and stop == i:
            raise _Stop()

    nc = bass.Bass("TRN2", target_bir_lowering=False)
    st = ExitStack()

    def din(name, shape, dt=F32):
        return nc.dram_tensor(name, list(shape), dt, kind="ExternalInput").ap()

    def dout(name, shape, dt=F32):
        return nc.dram_tensor(name, list(shape), dt, kind="ExternalOutput").ap()

    xT_d = din("xT", [D, NTOT])
    xtok_d = din("xtok", [NTOT, D])
    w_in_parts = [din(f"w_in_t{i}", [n, 128, KC * 128]) for i, n in enumerate(WIN_SPLIT)]

    class _WIn:
        def __getitem__(self, idx):
            for i, n in enumerate(WIN_SPLIT):
                if idx < n:
                    return w_in_parts[i][idx]
                idx -= n
            raise IndexError
    w_in_d = _WIn()
    w_ikw_d = din("w_ikw", [128, KC * 80])
    w_a_d = din("w_a_t", [16, 128, 8 * 128])
    w_at_d = din("w_attn_t", [16, 128, KC * 128])
    w_mix_d = din("w_mix_t", [16, 128, KC * 128])
    w_up_parts = [din(f"w_up_t{i}", [22, 128, KC * 128]) for i in range(2)]
    w_gate_parts = [din(f"w_gate_t{i}", [22, 128, KC * 128]) for i in range(2)]
    w_dn_parts = [din(f"w_down_t{i}", [16, 128, 22 * 128]) for i in range(2)]

    class _Split2:
        def __init__(self, parts):
            self.parts = parts
        def __getitem__(self, idx):
            return self.parts[idx // 22][idx % 22]
    w_up_d = _Split2(w_up_parts)
    w_gate_d = _Split2(w_gate_parts)

    class _Dn:
        def __getitem__(self, jh):
            j, hf = jh
            return w_dn_parts[hf][j]
    w_dn_d = _Dn()
    cbf_d = din("cbf", [128, 4 * 128])
    cf_d = din("cf", [128, 3 * 128])
    small_d = din("small", [128, 8 * 3 + NFC * 3 + NFC])
    ikgb_d = din("ikgb", [2, IDIM])
    ln_d = din("lnp", [4, D])
    rope_fm_d = din("rope_fm", [128, 4, NTOT])
    rope_tok_d = din("rope_tok", [NTOT, 16])
    sca_d = din("sca", [128, 8 * NSEQ * 2])
    scf_d = din("scf", [128, NFC * NSEQ * 2])
    pt_d = din("pt", [NSEQ * NPG], I32)
    ckv_d = din("cache_kv", [nphys * 128, 2 * NKV * HD])
    cik_d = din("cache_ik", [nphys * 128, IDIM])

    y_o = dout("y_o", [NTOT, D])
    kT_o = dout("kT_o", [NKV, 128, NTOT])
    vT_o = dout("vT_o", [NKV, 128, NTOT])
    ik_o = dout("ik_o", [NTOT, IDIM])
    ca_o = dout("ca_o", [128, 8 * 2])
    cf_o = dout("cf_o", [128, NFC * 2])
    cas_o = dout("cas_o", [128, 8 * NSEQ * 2])
    cfs_o = dout("cfs_o", [128, NFC * NSEQ * 2])

    def sb(name, shape, dt):
        return st.enter_context(nc.sbuf_tensor(name, list(shape), dt))

    def ps(name, shape, dt):
        return st.enter_context(nc.psum_tensor(name, list(shape), dt))

    P = Prog(nc)
    E = P.emit

    KT = sb("KT", [128, NKV, NTOT], BF16)
    V = sb("V", [128, 16, NKV * HD], BF16)
    VS = sb("VS", [128, NSEQ, NKV * HD], BF16)
    IKT = sb("IKT", [128, NTOT], BF16)
    NWS = 3
    WSLOT = 22 * 128
    WR = sb("WR", [128, NWS, WSLOT], BF16)
    CB = sb("CB", [128, 4, 128], BF16)
    CF = sb("CF", [128, 3, 128], F32)
    SM = sb("SM", [128, 8 * 3 + NFC * 3 + NFC], F32)
    IKGB = sb("IKGB", [128, 2, IDIM], F32)
    UH = sb("UH", [128, NFC, 2], F32)
    AH = sb("AH", [128, 8, 2], F32)
    SCA = sb("SCA", [128, 8, NSEQ, 2], F32)
    SCF = sb("SCF", [128, NFC, NSEQ, 2], F32)
    CAS = sb("CAS", [128, 8, NSEQ, 2], F32)
    CFS = sb("CFS", [128, NFC, NSEQ, 2], F32)
    QS = sb("QS", [128, NSEQ, NH, TS], BF16)
    WT = sb("WT", [128, 5, NIH], F32)
    XT = sb("XT", [128, KC, TB + NS], BF16)
    AT = sb("AT", [128, 8, TB + NS], BF16)
    OT = sb("OT", [128, KC, TB + NS], BF16)
    NTB = TB + NS
    RA = sb("RA", [128, 10304], F32)
    RC = sb("RC", [128, 5808], F32)
    RD = sb("RD", [128, 4096], F32)
    RS = sb("RS", [128, 4096], F32) if do_sample else None

    identb = CB[:, 0, :]
    pmqk = CB[:, 1, :]
    pmiq = CB[:, 2, :]
    onesb = CB[:, 3, :]
    identf = CF[:, 0, :]
    cneg = CF[:, 1, :]
    misc = CF[:, 2, :]
    CAW = SM[:, 0:24].rearrange("p (c j) -> p c j", j=3)
    CFW = SM[:, 24:24 + NFC * 3].rearrange("p (c j) -> p c j", j=3)
    CFB = SM[:, 24 + NFC * 3:24 + NFC * 4]

    def carve(region, off, shape, dt):
        n = 1
        for s in shape[1:]:
            n *= s
        if dt == F32:
            v = region[:, off:off + n]
            words = n
        else:
            words = (n + 1) // 2
            v = region[:, off:off + words].bitcast(BF16)[:, 0:n]
        if len(shape) == 2:
            return v, off + words
        names = " ".join(f"d{i}" for i in range(len(shape) - 1))
        kw = {f"d{i}": shape[i + 1] for i in range(len(shape) - 1)}
        return v.rearrange(f"p ({names}) -> p {names}", **kw), off + words

    o = 0
    MASKT, o = carve(RA, o, [128, 16, TB], BF16)
    ISC, o = carve(RA, o, [128, 2048], F32)
    RTMP, o = carve(RA, o, [128, 2, 512], F32)
    MB, o = carve(RA, o, [128, 2048], BF16)
    ROPE, o = carve(RA, o, [128, 4, NTB], F32)
    assert o <= 10304, o
    TOK = RA[:, 0:5 * D].rearrange("p (t d) -> p t d", d=D)
    o = 0
    IQT, o = carve(RC, o, [128, 8, NTB], BF16)
    QB, o = carve(RC, o, [128, 2, NTB], BF16)
    PTT, o = carve(RC, o, [128, 4, 512], BF16)
    LNL, o = carve(RC, o, [128, 2, 512], F32)
    VTB, o = carve(RC, o, [128, NTB], BF16)
    IKD, o = carve(RC, o, [128, 2, 128], BF16)
    IKTOK, o = carve(RC, o, [128, 2, 96], F32)
    assert o <= 5808, o
    MT = RC[:, 0:(KC * NTB) // 2].bitcast(BF16).rearrange("p (k t) -> p k t", t=NTB)
    ACTT = RC[:, 0:(22 * NTB) // 2].bitcast(BF16).rearrange("p (k t) -> p k t", t=NTB)
    o = 0
    STG4, o = carve(RD, o, [128, 4, NTB], F32)
    TA, o = carve(RD, o, [128, 544], F32)
    TB1, o = carve(RD, o, [128, NTB], F32)
    TB2, o = carve(RD, o, [128, NTB], F32)
    SMALLT, o = carve(RD, o, [128, 384], F32)
    assert o <= 4096, o
    HB = STG4
    LNB = OT[:, :, :].rearrange("p k t -> p (k t)")[:, 0:8192].bitcast(F32).rearrange("p (a d) -> p a d", d=D)

    pb = [ps(f"pb{i}", [128, 512], F32) for i in range(6)]
    ptb = [ps(f"ptb{i}", [128, 512], F32) for i in range(2)]
    Bpb = [Buf(f"pb{i}", excl=True) for i in range(6)]
    Bptb = [Buf(f"ptb{i}", excl=True) for i in range(2)]

    class Ctr:
        bank = 0
        tb = 0
        ws = 0

    def next_bank():
        i = Ctr.bank % 5
        Ctr.bank += 1
        return pb[i], Bpb[i]

    def next_tb():
        i = Ctr.tb % 2
        Ctr.tb += 1
        return ptb[i], Bptb[i]

    Bws = [Buf(f"ws{i}") for i in range(NWS)]

    def wload(src_ap, nelem):
        i = Ctr.ws % NWS
        Ctr.ws += 1
        E("pool", lambda h, i=i, src_ap=src_ap, nelem=nelem: h.dma_start(out=WR[:, i, 0:nelem], in_=src_ap),
          writes=[Bws[i]], dma_key=f"w{i}")
        return WR[:, i, :], Bws[i]

    B = {}

    def bf(name):
        if name not in B:
            B[name] = Buf(name)
        return B[name]

    E("pool", lambda h: h.dma_start(out=CB[:, :, :].rearrange("p a b -> p (a b)"), in_=cbf_d), writes=[bf("CB")], dma_key="cb")
    E("sp", lambda h: h.dma_start(out=CF[:, :, :].rearrange("p a b -> p (a b)"), in_=cf_d), writes=[bf("CF")], dma_key="cf")
    E("sp", lambda h: h.dma_start(out=SM[:, :], in_=small_d), writes=[bf("SM")], dma_key="sm")
    E("sp", lambda h: h.dma_start(out=IKGB[:, :, :].rearrange("p a b -> p (a b)"),
                                  in_=ikgb_d.rearrange("a b -> (a b)").partition_broadcast(128)),
      writes=[bf("IKGB")], dma_key="ikgb")
    E("sp", lambda h: h.dma_start(out=SCA[:, :, :, :].rearrange("p a b c -> p (a b c)"), in_=sca_d), writes=[bf("SCA")], dma_key="sca")
    E("sp", lambda h: h.dma_start(out=SCF[:, :, :, :].rearrange("p a b c -> p (a b c)"), in_=scf_d), writes=[bf("SCF")], dma_key="scf")
    E("dve", lambda h: h.memset(UH[:, :, :], 0.0), writes=[bf("UH")])
    E("dve", lambda h: h.memset(AH[:, :, :], 0.0), writes=[bf("AH")])

    CONSTS = [bf("CB"), bf("CF")]

    def proj(w_src, kc, M, srcf, src_bufs, segs, epi, fixed_bank=None):
        wt, wb = wload(w_src, kc * M)
        for si, (c0, n) in enumerate(segs):
            bank, bb = next_bank() if fixed_bank is None else (pb[fixed_bank], Bpb[fixed_bank])
            for k in range(kc):
                E("pe", lambda h, bank=bank, wt=wt, k=k, c0=c0, n=n, M=M, kc=kc:
                  h.matmul(bank[0:M, 0:n], lhsT=wt[:, k * M:(k + 1) * M], rhs=srcf(k, c0, n),
                           start=(k == 0), stop=(k == kc - 1)),
                  reads=[wb] + src_bufs, writes=[bb])
            epi(si, c0, n, bank, bb)

    def rope_fm(bank, bb, n, xb_view, xb_buf, pm, ctab, stab, f32_out=None, f32_buf=None):
        lvl = int(os.environ.get("KDBG_ROPE", 9))
        E("act", lambda h: h.copy(out=xb_view, in_=bank[:, 0:n]), reads=[bb], writes=[xb_buf])
        if lvl < 1:
            return
        pbk, pbb = pb[5], Bpb[5]
        E("pe", lambda h: h.matmul(pbk[:, 0:n], lhsT=pm, rhs=xb_view, start=True, stop=True),
          reads=[xb_buf] + CONSTS, writes=[pbb])
        if lvl < 2:
            return
        var = os.environ.get("KDBG_VAR", "")
        if var == "A":
            E("dve", lambda h: h.tensor_tensor(out=TB1[:, 0:n], in0=ctab, in1=bank[:, 0:n], op=ALU.mult),
              reads=[bb, bf("ROPE")], writes=[bf("TB1")])
        elif var == "B":
            E("act", lambda h: h.copy(out=TB1[:, 0:n], in_=bank[:, 0:n]), reads=[bb], writes=[bf("TB1")])
            E("dve", lambda h: h.tensor_tensor(out=TB1[:, 0:n], in0=TB1[:, 0:n], in1=ctab, op=ALU.mult),
              reads=[bf("ROPE")], writes=[bf("TB1")])
        elif var == "C":
            E("dve", lambda h: h.tensor_copy(out=TB1[:, 0:n], in_=ctab), reads=[bf("ROPE")], writes=[bf("TB1")])
        else:
            E("dve", lambda h: h.tensor_tensor(out=TB1[:, 0:n], in0=bank[:, 0:n], in1=ctab, op=ALU.mult),
              reads=[bb, bf("ROPE")], writes=[bf("TB1")])
        if lvl < 3:
            return
        E("dve", lambda h: h.tensor_tensor(out=TB2[:, 0:n], in0=pbk[:, 0:n], in1=stab, op=ALU.mult),
          reads=[pbb, bf("ROPE")], writes=[bf("TB2")])
        if lvl < 4:
            return
        if f32_out is None:
            E("dve", lambda h: h.tensor_tensor(out=xb_view, in0=TB1[:, 0:n], in1=TB2[:, 0:n], op=ALU.add),
              reads=[bf("TB1"), bf("TB2")], writes=[xb_buf])
        else:
            E("dve", lambda h: h.tensor_tensor(out=f32_out, in0=TB1[:, 0:n], in1=TB2[:, 0:n], op=ALU.add),
              reads=[bf("TB1"), bf("TB2")], writes=[f32_buf])
            E("act", lambda h: h.copy(out=xb_view, in_=f32_out), reads=[f32_buf], writes=[xb_buf])

    def layer_norm_rows(x_view, ntp, ncol, xbuf, tag):
        nch = (ncol + 511) // 512
        STv = SMALLT[:, 0:nch * 6].rearrange("p (a s) -> p a s", s=6)
        MV = SMALLT[:, 32:34]
        RSv = SMALLT[:, 34:35]
        for a in range(nch):
            w = min(512, ncol - a * 512)
            E("dve", lambda h, a=a, w=w: h.bn_stats(out=STv[0:ntp, a, :], in_=x_view[0:ntp, a * 512:a * 512 + w]),
              reads=[xbuf], writes=[bf("ST" )])
        E("dve", lambda h: h.bn_aggr(out=MV[0:ntp, :], in_=STv[0:ntp, :, :]), reads=[bf("ST")], writes=[bf("MV")])
        E("dve", lambda h: h.tensor_scalar(out=RSv[0:ntp, :], in0=MV[0:ntp, 1:2], scalar1=EPS, scalar2=None, op0=ALU.add),
          reads=[bf("MV")], writes=[bf("RSv")])
        E("act", lambda h: h.activation(out=RSv[0:ntp, :], in_=RSv[0:ntp, :], func=AF.Sqrt), reads=[bf("RSv")], writes=[bf("RSv")])
        E("dve", lambda h: h.reciprocal(out=RSv[0:ntp, :], in_=RSv[0:ntp, :]), reads=[bf("RSv")], writes=[bf("RSv")])
        return MV, RSv

    def do_block(b):
      if True:
        last = (b == NBLK - 1)
        tok0 = b * TB
        ntok = TB + (NS if last else 0)
        segs = [(0, TB)] + ([(TB, NS)] if last else [])
        ntiles = 4 + (1 if last else 0)

        def gcols(c0, n):
            return (tok0 + c0, n) if c0 < TB else (T + (c0 - TB), n)

        def tile_rows(tt):
            return (tok0 + tt * 128, 128) if tt < 4 else (T, NS)

        xT_v = xT_d.rearrange("(k p) t -> p k t", p=128)
        E("pool", lambda h: h.dma_start(out=XT[:, :, 0:TB], in_=xT_v[:, :, tok0:tok0 + TB]), writes=[bf("XT")], dma_key="xt")
        E("sp", lambda h: h.dma_start(out=ROPE[:, :, 0:TB], in_=rope_fm_d[:, :, tok0:tok0 + TB]), writes=[bf("ROPE")], dma_key="rope")
        if last:
            E("pool", lambda h: h.dma_start(out=XT[:, :, TB:NTB], in_=xT_v[:, :, T:NTOT]), writes=[bf("XT")], dma_key="xt")
            E("sp", lambda h: h.dma_start(out=ROPE[:, :, TB:NTB], in_=rope_fm_d[:, :, T:NTOT]), writes=[bf("ROPE")], dma_key="rope")
        ROPT = SMALLT[:, 64:64 + 5 * 16].rearrange("p (t c) -> p t c", c=16)
        for tt in range(ntiles):
            r0, nr = tile_rows(tt)
            E("sp", lambda h, tt=tt, r0=r0, nr=nr: h.dma_start(out=ROPT[0:nr, tt, :], in_=rope_tok_d[r0:r0 + nr, :]),
              writes=[bf("ROPT")], dma_key="ropt")

        xsrc = lambda k, c0, n: XT[:, k, c0:c0 + n]
        XB = [bf("XT")]

        ck(1)
        PA = TA[:, 0:TB + 2]
        PAS = TA[:, 520:544].rearrange("p (s t) -> p s t", t=6)
        ACCS = TB1[:, TB:NTB].rearrange("p (s t) -> p s t", t=TS)
        for c in range(8):
            def epi_cc(si, c0, n, bank, bb):
                E("act", lambda h: h.copy(out=TB2[:, c0:c0 + n], in_=bank[:, 0:n]), reads=[bb], writes=[bf("TB2")])
            proj(w_in_d[TI_CONV + 3 * c + 0], KC, 128, xsrc, XB, segs, epi_cc)
            E("act", lambda h, c=c: h.copy(out=PA[:, 0:2], in_=AH[:, c, :]), reads=[bf("AH")], writes=[bf("TA")])
            if last:
                E("act", lambda h, c=c: h.copy(out=PAS[:, :, 0:2], in_=SCA[:, c, :, :]), reads=[bf("SCA")], writes=[bf("TA")])

            def epi_ch(si, c0, n, bank, bb):
                if si == 0:
                    E("dve", lambda h: h.tensor_tensor(out=PA[:, 2:2 + TB], in0=bank[:, 0:TB], in1=TB2[:, 0:TB], op=ALU.mult),
                      reads=[bb, bf("TB2")], writes=[bf("TA")])
                else:
                    E("dve", lambda h: h.tensor_tensor(out=PAS[:, :, 2:6],
                                                        in0=bank[:, 0:NS].rearrange("p (s t) -> p s t", t=TS),
                                                        in1=TB2[:, TB:NTB].rearrange("p (s t) -> p s t", t=TS), op=ALU.mult),
                      reads=[bb, bf("TB2")], writes=[bf("TA")])
            proj(w_in_d[TI_CONV + 3 * c + 1], KC, 128, xsrc, XB, segs, epi_ch)
            E("dve", lambda h, c=c: h.tensor_scalar(out=TB1[:, 0:TB], in0=PA[:, 2:2 + TB], scalar1=CAW[:, c, 2:3], scalar2=None, op0=ALU.mult),
              reads=[bf("TA"), bf("SM")], writes=[bf("TB1")])
            E("dve", lambda h, c=c: h.scalar_tensor_tensor(out=TB1[:, 0:TB], in0=PA[:, 1:1 + TB], scalar=CAW[:, c, 1:2], in1=TB1[:, 0:TB], op0=ALU.mult, op1=ALU.add),
              reads=[bf("TA"), bf("SM")], writes=[bf("TB1")])
            E("dve", lambda h, c=c: h.scalar_tensor_tensor(out=TB1[:, 0:TB], in0=PA[:, 0:TB], scalar=CAW[:, c, 0:1], in1=TB1[:, 0:TB], op0=ALU.mult, op1=ALU.add),
              reads=[bf("TA"), bf("SM")], writes=[bf("TB1")])
            if last:
                E("dve", lambda h, c=c: h.tensor_scalar(out=ACCS, in0=PAS[:, :, 2:6], scalar1=CAW[:, c, 2:3], scalar2=None, op0=ALU.mult),
                  reads=[bf("TA"), bf("SM")], writes=[bf("TB1")])
                E("dve", lambda h, c=c: h.scalar_tensor_tensor(out=ACCS, in0=PAS[:, :, 1:5], scalar=CAW[:, c, 1:2], in1=ACCS, op0=ALU.mult, op1=ALU.add),
                  reads=[bf("TA"), bf("SM")], writes=[bf("TB1")])
                E("dve", lambda h, c=c: h.scalar_tensor_tensor(out=ACCS, in0=PAS[:, :, 0:4], scalar=CAW[:, c, 0:1], in1=ACCS, op0=ALU.mult, op1=ALU.add),
                  reads=[bf("TA"), bf("SM")], writes=[bf("TB1")])
                E("act", lambda h, c=c: h.copy(out=CAS[:, c, :, :], in_=PAS[:, :, 4:6]), reads=[bf("TA")], writes=[bf("CAS")])
            E("act", lambda h, c=c: h.copy(out=AH[:, c, :], in_=PA[:, TB:TB + 2]), reads=[bf("TA")], writes=[bf("AH")])

            def epi_cb(si, c0, n, bank, bb, c=c):
                E("dve", lambda h: h.tensor_tensor(out=AT[:, c, c0:c0 + n], in0=bank[:, 0:n], in1=TB1[:, c0:c0 + n], op=ALU.mult),
                  reads=[bb, bf("TB1")], writes=[bf("AT")])
            proj(w_in_d[TI_CONV + 3 * c + 2], KC, 128, xsrc, XB, segs, epi_cb)

        ck(2)
        for g in range(int(os.environ.get('KDBG_NG', NKV))):
            def epi_k(si, c0, n, bank, bb, g=g):
                gc0, _ = gcols(c0, n)
                stg = STG4[:, g % 2, c0:c0 + n]
                sbuf = bf(f"STG{g % 2}")
                rope_fm(bank, bb, n, KT[:, g, gc0:gc0 + n], bf("KT"), pmqk, ROPE[:, 0, c0:c0 + n], ROPE[:, 1, c0:c0 + n],
                        f32_out=stg, f32_buf=sbuf)
                if not os.environ.get('KDBG_NOKDMA'):
                    E("sp", lambda h: h.dma_start(out=kT_o[g, :, gc0:gc0 + n], in_=stg), reads=[sbuf], dma_key=f"o_stg{g % 2}")
            proj(w_in_d[TI_K + g], KC, 128, xsrc, XB, segs, epi_k)

        ck(3)
        for g in range(NKV):
            def epi_v(si, c0, n, bank, bb, g=g):
                gc0, _ = gcols(c0, n)
                stg = STG4[:, 2 + g % 2, c0:c0 + n]
                sbuf = bf(f"STG{2 + g % 2}")
                E("act", lambda h: h.copy(out=stg, in_=bank[:, 0:n]), reads=[bb], writes=[sbuf])
                E("dve", lambda h: h.tensor_copy(out=VTB[:, c0:c0 + n], in_=bank[:, 0:n]), reads=[bb], writes=[bf("VTB")])
                E("sp", lambda h: h.dma_start(out=vT_o[g, :, gc0:gc0 + n], in_=stg), reads=[sbuf], dma_key=f"o_stg{2 + g % 2}")
                if si == 0:
                    tbk, tbb = next_tb()
                    for tt in range(4):
                        E("pe", lambda h, tt=tt: h.matmul(tbk[:, tt * 128:(tt + 1) * 128], lhsT=VTB[:, tt * 128:(tt + 1) * 128], rhs=identb, start=True, stop=True),
                          reads=[bf("VTB")] + CONSTS, writes=[tbb])
                    E("act", lambda h: h.copy(out=V[:, b * 4:b * 4 + 4, g * HD:(g + 1) * HD],
                                              in_=tbk[:, 0:512].rearrange("p (t d) -> p t d", d=128)),
                      reads=[tbb], writes=[bf("V")])
                else:
                    for s_ in range(NSEQ):
                        tbk, tbb = next_tb()
                        E("pe", lambda h, s_=s_, tbk=tbk: h.matmul(tbk[0:TS, 0:128], lhsT=VTB[:, TB + s_ * TS:TB + (s_ + 1) * TS], rhs=identb, start=True, stop=True),
                          reads=[bf("VTB")] + CONSTS, writes=[tbb])
                        E("act", lambda h, s_=s_, tbk=tbk: h.copy(out=VS[0:TS, s_, g * HD:(g + 1) * HD], in_=tbk[0:TS, 0:128]),
                          reads=[tbb], writes=[bf("VS")])
            proj(w_in_d[TI_V + g], KC, 128, xsrc, XB, segs, epi_v)

        ck(4)
        IKS = STG4[:, 0, :]
        def epi_ikw(si, c0, n, bank, bb):
            E("act", lambda h: h.copy(out=IKS[0:80, c0:c0 + n], in_=bank[0:80, 0:n]), reads=[bb], writes=[bf("STG0")])
        proj(w_ikw_d, KC, 80, xsrc, XB, segs, epi_ikw)
        for tt in range(ntiles):
            r0, nr = tile_rows(tt)
            lc0 = tt * 128
            pk, pkb = pb[5], Bpb[5]
            E("pe", lambda h, lc0=lc0, nr=nr: h.transpose(pk[0:nr, 0:80], IKS[0:80, lc0:lc0 + nr], identf[0:80, 0:80]),
              reads=[bf("STG0")] + CONSTS, writes=[pkb])
            MV, RSv = layer_norm_rows(pk, nr, IDIM, pkb, "ik")
            ikt = IKTOK[:, tt % 2, :]
            ikb_ = bf(f"IKTOK{tt % 2}")
            E("dve", lambda h, nr=nr, ikt=ikt: h.tensor_scalar(out=ikt[0:nr, 0:IDIM], in0=pk[0:nr, 0:IDIM], scalar1=MV[0:nr, 0:1], scalar2=RSv[0:nr, 0:1],
                                                             op0=ALU.subtract, op1=ALU.mult),
              reads=[pkb, bf("MV"), bf("RSv")], writes=[ikb_])
            E("dve", lambda h, nr=nr, ikt=ikt: h.tensor_tensor(out=ikt[0:nr, 0:IDIM], in0=ikt[0:nr, 0:IDIM], in1=IKGB[0:nr, 0, :], op=ALU.mult),
              reads=[bf("IKGB")], writes=[ikb_])
            E("dve", lambda h, nr=nr, ikt=ikt: h.tensor_tensor(out=ikt[0:nr, 0:IDIM], in0=ikt[0:nr, 0:IDIM], in1=IKGB[0:nr, 1, :], op=ALU.add),
              reads=[bf("IKGB")], writes=[ikb_])
            cs, sn = ROPT[:, tt, 0:8], ROPT[:, tt, 8:16]
            for (dst, a, tab) in ((64, 0, cs), (72, 8, sn), (80, 8, cs), (88, 0, sn)):
                E("dve", lambda h, nr=nr, ikt=ikt, dst=dst, a=a, tab=tab: h.tensor_tensor(out=ikt[0:nr, dst:dst + 8], in0=ikt[0:nr, a:a + 8], in1=tab[0:nr, :], op=ALU.mult),
                  reads=[bf("ROPT")], writes=[ikb_])
            E("dve", lambda h, nr=nr, ikt=ikt: h.tensor_tensor(out=ikt[0:nr, 0:8], in0=ikt[0:nr, 64:72], in1=ikt[0:nr, 72:80], op=ALU.subtract),
              writes=[ikb_])
            E("dve", lambda h, nr=nr, ikt=ikt: h.tensor_tensor(out=ikt[0:nr, 8:16], in0=ikt[0:nr, 80:88], in1=ikt[0:nr, 88:96], op=ALU.add),
              writes=[ikb_])
            E("act", lambda h, nr=nr, tt=tt: h.activation(out=WT[0:nr, tt, :], in_=pk[0:nr, 64:80], func=AF.Copy, scale=1.0 / 32.0),
              reads=[pkb], writes=[bf("WT")])
            E("sp", lambda h, nr=nr, r0=r0, ikt=ikt: h.dma_start(out=ik_o[r0:r0 + nr, :], in_=ikt[0:nr, 0:IDIM]), reads=[ikb_], dma_key=f"o_ik{tt % 2}")
            ikd = IKD[:, tt % 2, :]
            ikdb = bf(f"IKD{tt % 2}")
            E("act", lambda h, nr=nr, ikt=ikt, ikd=ikd: h.copy(out=ikd[0:nr, 0:IDIM], in_=ikt[0:nr, 0:IDIM]), reads=[ikb_], writes=[ikdb])
            E("act", lambda h, nr=nr, ikt=ikt, ikd=ikd: h.copy(out=ikd[0:nr, IDIM:128], in_=ikt[0:nr, 0:IDIM]), reads=[ikb_], writes=[ikdb])
            tbk, tbb = next_tb()
            E("pe", lambda h, nr=nr, ikd=ikd, tbk=tbk: h.matmul(tbk[:, 0:nr], lhsT=ikd[0:nr, :], rhs=identb[0:nr, 0:nr], start=True, stop=True),
              reads=[ikdb] + CONSTS, writes=[tbb])
            E("act", lambda h, nr=nr, r0=r0, tbk=tbk: h.copy(out=IKT[:, r0:r0 + nr], in_=tbk[:, 0:nr]), reads=[tbb], writes=[bf("IKT")])

        ck(5)
        for j in range(8):
            def epi_iq(si, c0, n, bank, bb, j=j):
                rope_fm(bank, bb, n, IQT[:, j, c0:c0 + n], bf("IQT"), pmiq, ROPE[:, 2, c0:c0 + n], ROPE[:, 3, c0:c0 + n])
            proj(w_in_d[TI_IQ + j], KC, 128, xsrc, XB, segs, epi_iq)

        ck(6)
        for qt in range(4):
            i_g = b * 4 + qt
            nk = (i_g + 1) * 128
            ngrp = (nk + 511) // 512
            for kg in range(ngrp):
                k0 = kg * 512
                kw = min(512, nk - k0)
                for hh in range(NIH):
                    j, par = hh // 2, hh % 2
                    bank, bb = next_bank()
                    E("pe", lambda h, bank=bank, j=j, par=par, k0=k0, kw=kw, qt=qt:
                      h.matmul(bank[:, 0:kw], lhsT=IQT[par * 64:(par + 1) * 64, j, qt * 128:(qt + 1) * 128],
                               rhs=IKT[par * 64:(par + 1) * 64, k0:k0 + kw], start=True, stop=True),
                      reads=[bf("IQT"), bf("IKT")], writes=[bb])
                    rt = RTMP[:, hh % 2, 0:kw]
                    rtb = bf(f"RTMP{hh % 2}")
                    E("act", lambda h, bank=bank, kw=kw, rt=rt: h.activation(out=rt, in_=bank[:, 0:kw], func=AF.Relu), reads=[bb], writes=[rtb])
                    if hh == 0:
                        E("dve", lambda h, rt=rt, k0=k0, kw=kw, qt=qt: h.tensor_scalar(out=ISC[:, k0:k0 + kw], in0=rt, scalar1=WT[:, qt, 0:1], scalar2=None, op0=ALU.mult),
                          reads=[rtb, bf("WT")], writes=[bf("ISC")])
                    else:
                        E("dve", lambda h, rt=rt, k0=k0, kw=kw, qt=qt, hh=hh: h.scalar_tensor_tensor(out=ISC[:, k0:k0 + kw], in0=rt, scalar=WT[:, qt, hh:hh + 1],
                                                                                                   in1=ISC[:, k0:k0 + kw], op0=ALU.mult, op1=ALU.add),
                          reads=[rtb, bf("WT")], writes=[bf("ISC")])
            NB = 26
            use_bis = (nk > TOPK) and TOPK_BISECT
            MNv = SMALLT[:, 36:37]
            MXv = SMALLT[:, 37:38]
            W0v = SMALLT[:, 38:39]
            MIDv = SMALLT[:, 39:40]
            CNTv = SMALLT[:, 40:41]
            G2v = SMALLT[:, 41:42]
            HWv = SMALLT[:, 224:256]
            if use_bis:
                E("dve", lambda h, nk=nk: h.tensor_reduce(out=MNv, in_=ISC[:, 0:nk], axis=AX.X, op=ALU.min), reads=[bf("ISC")], writes=[bf("MNv")])
                E("dve", lambda h, nk=nk: h.tensor_reduce(out=MXv, in_=ISC[:, 0:nk], axis=AX.X, op=ALU.max), reads=[bf("ISC")], writes=[bf("MXv")])
            E("dve", lambda h, nk=nk: h.tensor_tensor(out=ISC[:, nk - 128:nk], in0=ISC[:, nk - 128:nk], in1=cneg, op=ALU.add),
              reads=CONSTS, writes=[bf("ISC")])
            if use_bis:
                E("dve", lambda h: h.tensor_tensor(out=W0v, in0=MXv, in1=MNv, op=ALU.subtract), reads=[bf("MNv"), bf("MXv")], writes=[bf("W0v")])
                E("dve", lambda h: h.tensor_scalar(out=W0v, in0=W0v, scalar1=1.001, scalar2=1.0e-6, op0=ALU.mult, op1=ALU.add), writes=[bf("W0v")])
                E("dve", lambda h: h.tensor_scalar(out=HWv, in0=misc[:, 64:96], scalar1=W0v, scalar2=None, op0=ALU.mult), reads=[bf("W0v")] + CONSTS, writes=[bf("HWv")])
                E("dve", lambda h: h.tensor_tensor(out=MIDv, in0=MNv, in1=HWv[:, 0:1], op=ALU.add), reads=[bf("MNv"), bf("HWv")], writes=[bf("MIDv")])
                for n_ in range(NB):
                    E("dve", lambda h, nk=nk: h.tensor_scalar(out=MB[:, 0:nk], in0=ISC[:, 0:nk], scalar1=MIDv, scalar2=0.0, op0=ALU.is_ge, op1=ALU.add, accum_out=CNTv),
                      reads=[bf("ISC"), bf("MIDv")], writes=[bf("MB"), bf("CNTv")])
                    nxt = n_ + 1 if n_ < NB - 1 else n_
                    E("dve", lambda h, n_=n_: h.tensor_scalar(out=G2v, in0=CNTv, scalar1=float(TOPK), scalar2=HWv[:, n_:n_ + 1], op0=ALU.is_ge, op1=ALU.mult),
                      reads=[bf("CNTv"), bf("HWv")], writes=[bf("G2v")])
                    E("dve", lambda h, nxt=nxt: h.scalar_tensor_tensor(out=MIDv, in0=G2v, scalar=HWv[:, nxt:nxt + 1], in1=MIDv, op0=ALU.subtract, op1=ALU.add),
                      reads=[bf("G2v"), bf("HWv")], writes=[bf("MIDv")])
                E("dve", lambda h, nk=nk: h.tensor_scalar(out=MB[:, 0:nk], in0=ISC[:, 0:nk], scalar1=MIDv, scalar2=None, op0=ALU.is_ge),
                  reads=[bf("ISC"), bf("MIDv")], writes=[bf("MB")])
            elif nk > TOPK:
                M8 = SMALLT[:, 48:56]
                for r in range(TOPK // 8):
                    E("dve", lambda h, nk=nk: h.max(out=M8, in_=ISC[:, 0:nk]), reads=[bf("ISC")], writes=[bf("M8")])
                    E("dve", lambda h, nk=nk: h.match_replace(out=ISC[:, 0:nk], in_to_replace=M8, in_values=ISC[:, 0:nk], imm_value=NEG),
                      reads=[bf("M8")], writes=[bf("ISC")])
                E("dve", lambda h, nk=nk: h.tensor_scalar(out=MB[:, 0:nk], in0=ISC[:, 0:nk], scalar1=NEG, scalar2=None, op0=ALU.is_equal),
                  reads=[bf("ISC")], writes=[bf("MB")])
            else:
                E("dve", lambda h, nk=nk: h.tensor_scalar(out=MB[:, 0:nk], in0=ISC[:, 0:nk], scalar1=-1.0e38, scalar2=None, op0=ALU.is_gt),
                  reads=[bf("ISC")], writes=[bf("MB")])
            for kt0 in range(0, i_g + 1, 4):
                nkt = min(4, i_g + 1 - kt0)
                tbk, tbb = next_tb()
                for jj in range(nkt):
                    E("pe", lambda h, tbk=tbk, jj=jj, kt0=kt0: h.matmul(tbk[:, jj * 128:(jj + 1) * 128], lhsT=MB[:, (kt0 + jj) * 128:(kt0 + jj + 1) * 128], rhs=identb, start=True, stop=True),
                      reads=[bf("MB")] + CONSTS, writes=[tbb])
                E("act", lambda h, tbk=tbk, nkt=nkt, kt0=kt0, qt=qt: h.copy(out=MASKT[:, kt0:kt0 + nkt, qt * 128:(qt + 1) * 128],
                                                                         in_=tbk[:, 0:nkt * 128].rearrange("p (t d) -> p t d", d=128)),
                  reads=[tbb], writes=[bf("MASKT")])

        ck(7)
        scale = HD ** -0.5
        nkt_all = b * 4 + 4

        def q_proj(hq):
            qb = QB[:, hq % 2, :]
            qbb = bf(f"QB{hq % 2}")

            def epi_q(si, c0, n, bank, bb):
                rope_fm(bank, bb, n, qb[:, c0:c0 + n], qbb, pmqk, ROPE[:, 0, c0:c0 + n], ROPE[:, 1, c0:c0 + n])
                if si == 1:
                    E("act", lambda h: h.copy(out=QS[:, :, hq, :], in_=qb[:, TB:NTB].rearrange("p (s t) -> p s t", t=TS)), reads=[qbb], writes=[bf("QS")])
            proj(w_in_d[TI_Q + hq], KC, 128, xsrc, XB, segs, epi_q, fixed_bank=4)

        def attend(hq):
            g = hq // 4
            qb = QB[:, hq % 2, :]
            qbb = bf(f"QB{hq % 2}")
            ob, obb = pb[2], Bpb[2]
            lb, lbb = pb[3], Bpb[3]

            def stage_qk(kt):
                c0 = max(0, (kt - b * 4) * 128)
                n = TB - c0
                sbi = (0, 1, 5)[kt % 3]
                sbk, sbb = pb[sbi], Bpb[sbi]
                E("pe", lambda h: h.matmul(sbk[:, 0:n], lhsT=KT[:, g, kt * 128:(kt + 1) * 128], rhs=qb[:, c0:TB], start=True, stop=True),
                  reads=[bf("KT"), qbb], writes=[sbb])
                pt_ = PTT[:, kt % 4, 0:n]
                ptbuf = bf(f"PTT{kt % 4}")
                E("act", lambda h: h.activation(out=pt_, in_=sbk[:, 0:n], func=AF.Exp, scale=scale), reads=[sbb], writes=[ptbuf])
                E("dve", lambda h: h.tensor_tensor(out=pt_, in0=pt_, in1=MASKT[:, kt, c0:TB], op=ALU.mult),
                  reads=[bf("MASKT")], writes=[ptbuf])

            def stage_pv(kt):
                c0 = max(0, (kt - b * 4) * 128)
                n = TB - c0
                pt_ = PTT[:, kt % 4, 0:n]
                ptbuf = bf(f"PTT{kt % 4}")
                E("pe", lambda h: h.matmul(ob[:, c0:TB], lhsT=V[:, kt, g * HD:(g + 1) * HD], rhs=pt_, start=(kt == 0), stop=(kt == nkt_all - 1)),
                  reads=[bf("V"), ptbuf], writes=[obb])
                E("pe", lambda h: h.matmul(lb[:, c0:TB], lhsT=onesb, rhs=pt_, start=(kt == 0), stop=(kt == nkt_all - 1)),
                  reads=[ptbuf] + CONSTS, writes=[lbb])

            for kt in range(nkt_all + 2):
                if kt < nkt_all:
                    stage_qk(kt)
                if kt >= 2:
                    stage_pv(kt - 2)
            ln_ = LNL[:, hq % 2, :]
            lnb = bf(f"LNL{hq % 2}")
            E("act", lambda h: h.activation(out=ln_, in_=lb[:, 0:TB], func=AF.Ln), reads=[lbb], writes=[lnb])
            E("act", lambda h: h.activation(out=ln_, in_=ln_, func=AF.Exp, scale=-1.0), reads=[lnb], writes=[lnb])
            E("dve", lambda h: h.tensor_tensor(out=OT[:, hq, 0:TB], in0=ob[:, 0:TB], in1=ln_, op=ALU.mult),
              reads=[obb, lnb], writes=[bf("OT")])

        q_proj(0)
        for hq in range(NH):
            if hq + 1 < NH:
                q_proj(hq + 1)
            attend(hq)

        ck(8)
        if last and do_sample:
            P.barrier()
            o = 0
            IDXT, o = carve(RA, o, [128, NSEQ * NPG], F32)
            IDX = IDXT.bitcast(I32)
            PTB_, o = carve(RA, o, [128, NSEQ * NPG], F32)
            PTBi = PTB_.bitcast(I32)
            IT, o = carve(RA, o, [128, NPG + 1, NS], F32)
            CMP, o = carve(RA, o, [128, NPG + 1, NS], F32)
            MKS, o = carve(RA, o, [128, NPG + 1, NS], BF16)
            GG, o = carve(RA, o, [128, 32, IDIM], F32)
            GB, o = carve(RA, o, [128, 32, 128], BF16)
            TMPS, o = carve(RA, o, [128, 8, 64], F32)
            WB, o = carve(RA, o, [128, NSEQ, 64], F32)
            RW, o = carve(RA, o, [128, NSEQ * 64], F32)
            BS, o = carve(RA, o, [128, 256], F32)
            PSS, o = carve(RA, o, [128, 64], F32)
            PTSS, o = carve(RA, o, [128, 2, 64], BF16)
            RLS, o = carve(RA, o, [128, 64], F32)
            assert o <= 10304, o
            o = 0
            IKTD, o = carve(RS, o, [128, PAST], BF16)
            assert o <= 4096, o
            o = 2112
            KVG, o = carve(RC, o, [128, 3, 1024], F32)
            assert o <= 5808, o
            o = 0
            KBF, o = carve(RD, o, [128, 4, 512], BF16)
            VBF, o = carve(RD, o, [128, 4, 512], BF16)
            KTP, o = carve(RD, o, [128, 4, 512], BF16)
            assert o <= 3712, o
            pidx = misc[:, 0:1]
            cneg4 = misc[0:4, 1:5]
            delta = misc[0:NS, 8:24].rearrange("p (s q) -> p s q", q=TS)
            ones_f = misc[:, 32:33]
            ones16 = CF[0:NS, 2, :]
            E("sp", lambda h: h.dma_start(out=PTBi[:, :], in_=pt_d.partition_broadcast(128)), writes=[bf("PTB")], dma_key="ptb")
            E("dve", lambda h: h.tensor_copy(out=PTB_[:, :], in_=PTBi[:, :]), reads=[bf("PTB")], writes=[bf("PTBf")])
            E("dve", lambda h: h.tensor_scalar(out=IDXT[:, :], in0=PTB_[:, :], scalar1=128.0, scalar2=pidx, op0=ALU.mult, op1=ALU.add),
              reads=[bf("PTBf")] + CONSTS, writes=[bf("IDXf")])
            E("dve", lambda h: h.tensor_copy(out=IDX[:, :], in_=IDXT[:, :]), reads=[bf("IDXf")], writes=[bf("IDX")])
            ck(20)
            RW5 = RW.rearrange("p (s r j q) -> p s r j q", s=NSEQ, r=2, j=8)
            for hh in range(NIH):
                j, par = hh // 2, hh % 2
                E("dve", lambda h, j=j, par=par, hh=hh: h.tensor_scalar(out=RW5[0:NS, :, par, j, :], in0=delta, scalar1=WT[0:NS, 4, hh:hh + 1], scalar2=None, op0=ALU.mult),
                  reads=[bf("WT")] + CONSTS, writes=[bf("RW")])
            ONESF = SMALLT[:, 256:384]
            E("dve", lambda h: h.memset(ONESF, 1.0), writes=[bf("ONESF")])
            wbk, wbb = pb[5], Bpb[5]
            E("pe", lambda h: h.matmul(wbk[:, 0:256], lhsT=ONESF[0:NS, :], rhs=RW[0:NS, :], start=True, stop=True),
              reads=[bf("RW"), bf("ONESF")], writes=[wbb])
            E("act", lambda h: h.copy(out=WB[:, :, :].rearrange("p s c -> p (s c)"), in_=wbk[:, 0:256]), reads=[wbb], writes=[bf("WB")])
            ck(21)
            IQS = SMALLT[:, 160:224].bitcast(BF16).rearrange("p (s j q) -> p s j q", s=NSEQ, j=8)
            for s_ in range(NSEQ):
                E("act", lambda h, s_=s_: h.copy(out=IQS[:, s_, :, :], in_=IQT[:, :, TB + s_ * TS:TB + (s_ + 1) * TS]), reads=[bf("IQT")], writes=[bf("IQS")])
            E("dve", lambda h: h.memset(IT[:, NPG, :], NEG), writes=[bf("IT")])
            for s_ in range(NSEQ):
                scol = TB + s_ * TS
                for half in range(2):
                    for pl in range(32):
                        pg = half * 32 + pl
                        E("pool", lambda h, pl=pl, pg=pg, s_=s_: h.indirect_dma_start(
                            out=GG[:, pl, :], out_offset=None, in_=cik_d,
                            in_offset=bass.IndirectOffsetOnAxis(ap=IDX[:, s_ * NPG + pg:s_ * NPG + pg + 1], axis=0)),
                          reads=[bf("IDX")], writes=[bf("GG")], dma_key="gg", acc=True)
                    E("dve", lambda h: h.tensor_copy(out=GB[:, :, 0:IDIM], in_=GG[:, :, :]), reads=[bf("GG")], writes=[bf("GB")])
                    E("act", lambda h: h.copy(out=GB[:, :, IDIM:128], in_=GG[:, :, :]), reads=[bf("GG")], writes=[bf("GB")])
                    for q4 in range(8):
                        tbk, tbb = next_tb()
                        for jj in range(4):
                            E("pe", lambda h, tbk=tbk, jj=jj, q4=q4: h.matmul(tbk[:, jj * 128:(jj + 1) * 128], lhsT=GB[:, q4 * 4 + jj, :], rhs=identb, start=True, stop=True),
                              reads=[bf("GB")] + CONSTS, writes=[tbb])
                        p0 = (half * 32 + q4 * 4) * 128
                        E("act", lambda h, tbk=tbk, p0=p0: h.copy(out=IKTD[:, p0:p0 + 512], in_=tbk[:, 0:512]), reads=[tbb], writes=[bf("IKTD")])
                ck(22)
                for p8 in range(8):
                    banks2 = [next_bank(), next_bank()]
                    for pl in range(8):
                        pg = p8 * 8 + pl
                        for par in range(2):
                            bank, bb = banks2[par]
                            E("pe", lambda h, bank=bank, pl=pl, pg=pg, par=par, s_=s_: h.matmul(
                                bank[:, pl * 32:pl * 32 + 32],
                                lhsT=IKTD[par * 64:(par + 1) * 64, pg * 128:(pg + 1) * 128],
                                rhs=IQS[par * 64:(par + 1) * 64, s_, :, :].rearrange("p j q -> p (j q)"), start=True, stop=True),
                              reads=[bf("IKTD"), bf("IQS")], writes=[bb])
                    ck(30)
                    for par in range(2):
                        bank, bb = banks2[par]
                        E("dve", lambda h, bank=bank, s_=s_, par=par: h.scalar_tensor_tensor(
                            out=TMPS[:, :, par * 32:(par + 1) * 32], in0=bank[:, 0:256].rearrange("p (g c) -> p g c", c=32), scalar=0.0,
                            in1=WB[:, s_, par * 32:(par + 1) * 32].unsqueeze(1).to_broadcast([128, 8, 32]), op0=ALU.max, op1=ALU.mult),
                          reads=[bb, bf("WB")], writes=[bf("TMPS")])
                    ck(31)
                    E("dve", lambda h, p8=p8, s_=s_: h.tensor_reduce(
                        out=IT[:, p8 * 8:(p8 + 1) * 8, s_ * TS:(s_ + 1) * TS],
                        in_=TMPS[:, :, :].rearrange("p g (hh q) -> p g q hh", q=TS), axis=AX.X, op=ALU.add),
                      reads=[bf("TMPS")], writes=[bf("IT")])
                ck(32)
                banks2 = [next_bank(), next_bank()]
                for par in range(2):
                    bank, bb = banks2[par]
                    E("pe", lambda h, bank=bank, par=par, s_=s_: h.matmul(
                        bank[0:TS, 0:32], lhsT=IKT[par * 64:(par + 1) * 64, T + s_ * TS:T + (s_ + 1) * TS],
                        rhs=IQS[par * 64:(par + 1) * 64, s_, :, :].rearrange("p j q -> p (j q)"), start=True, stop=True),
                      reads=[bf("IKT"), bf("IQS")], writes=[bb])
                    E("dve", lambda h, bank=bank, s_=s_, par=par: h.scalar_tensor_tensor(out=TMPS[0:TS, 0, par * 32:(par + 1) * 32], in0=bank[0:TS, 0:32], scalar=0.0,
                                                                                       in1=WB[0:TS, s_, par * 32:(par + 1) * 32], op0=ALU.max, op1=ALU.mult),
                      reads=[bb, bf("WB")], writes=[bf("TMPS")])
                E("dve", lambda h, s_=s_: h.tensor_reduce(out=IT[0:TS, NPG, s_ * TS:(s_ + 1) * TS],
                                                          in_=TMPS[0:TS, 0, :].rearrange("p (hh q) -> p q hh", q=TS), axis=AX.X, op=ALU.add),
                  reads=[bf("TMPS")], writes=[bf("IT")])
                E("dve", lambda h, s_=s_: h.tensor_tensor(out=IT[0:TS, NPG, s_ * TS:(s_ + 1) * TS], in0=IT[0:TS, NPG, s_ * TS:(s_ + 1) * TS], in1=cneg4, op=ALU.add),
                  reads=CONSTS, writes=[bf("IT")])
            ck(23)
            MXP = BS[:, 0:16]
            MNP = BS[:, 16:32]
            LO = BS[:, 32:33]
            HI = BS[:, 33:34]
            MID = BS[:, 34:35]
            GE = BS[:, 35:36]
            DLT = BS[:, 36:37]
            DG = BS[:, 48:64]
            TRS = BS[:, 64:80]
            CNTP = BS[:, 80:96]
            IT_sg = IT[:, :, :].rearrange("p g s -> p s g")
            E("dve", lambda h: h.tensor_reduce(out=MXP, in_=IT_sg, axis=AX.X, op=ALU.max), reads=[bf("IT")], writes=[bf("MXP")])
            E("dve", lambda h: h.tensor_reduce(out=MNP, in_=IT[:, 0:NPG, :].rearrange("p g s -> p s g"), axis=AX.X, op=ALU.min), reads=[bf("IT")], writes=[bf("MNP")])
            bk5, bb5 = pb[5], Bpb[5]
            E("pe", lambda h: h.transpose(bk5[0:NS, 0:128], MXP, identf), reads=[bf("MXP")] + CONSTS, writes=[bb5])
            E("dve", lambda h: h.tensor_reduce(out=HI[0:NS, :], in_=bk5[0:NS, 0:128], axis=AX.X, op=ALU.max), reads=[bb5], writes=[bf("HI")])
            E("dve", lambda h: h.tensor_scalar(out=HI[0:NS, :], in0=HI[0:NS, :], scalar1=1.0, scalar2=None, op0=ALU.add), writes=[bf("HI")])
            E("pe", lambda h: h.transpose(bk5[0:NS, 128:256], MNP, identf), reads=[bf("MNP")] + CONSTS, writes=[bb5])
            E("dve", lambda h: h.tensor_reduce(out=LO[0:NS, :], in_=bk5[0:NS, 128:256], axis=AX.X, op=ALU.min), reads=[bb5], writes=[bf("LO")])

            def thresh_compare(src, out_view, out_buf):
                E("dve", lambda h: h.tensor_scalar(out=DG[0:NS, :], in0=identf[0:NS, 0:NS], scalar1=src[0:NS, 0:1], scalar2=None, op0=ALU.mult),
                  reads=[bf("LO"), bf("HI"), bf("MID")] + CONSTS, writes=[bf("DG")])
                E("pe", lambda h: h.matmul(bk5[:, 256:272], lhsT=ONESF[0:NS, :], rhs=DG[0:NS, :], start=True, stop=True),
                  reads=[bf("DG"), bf("ONESF")], writes=[bb5])
                E("act", lambda h: h.copy(out=TRS, in_=bk5[:, 256:272]), reads=[bb5], writes=[bf("TRS")])
                E("dve", lambda h: h.tensor_tensor(out=out_view, in0=IT[:, :, :], in1=TRS.unsqueeze(1).to_broadcast([128, NPG + 1, NS]), op=ALU.is_ge),
                  reads=[bf("IT"), bf("TRS")], writes=[out_buf])

            for it in range(NBIS):
                E("dve", lambda h: h.tensor_tensor(out=MID[0:NS, :], in0=LO[0:NS, :], in1=HI[0:NS, :], op=ALU.add), reads=[bf("LO"), bf("HI")], writes=[bf("MID")])
                E("dve", lambda h: h.tensor_scalar(out=MID[0:NS, :], in0=MID[0:NS, :], scalar1=0.5, scalar2=None, op0=ALU.mult), writes=[bf("MID")])
                thresh_compare(MID, CMP[:, :, :], bf("CMP"))
                E("dve", lambda h: h.tensor_reduce(out=CNTP, in_=CMP[:, :, :].rearrange("p g s -> p s g"), axis=AX.X, op=ALU.add), reads=[bf("CMP")], writes=[bf("CNTP")])
                E("pe", lambda h: h.matmul(bk5[0:NS, 288:289], lhsT=CNTP, rhs=ONESF[:, 0:1], start=True, stop=True), reads=[bf("CNTP"), bf("ONESF")], writes=[bb5])
                E("dve", lambda h: h.tensor_scalar(out=GE[0:NS, :], in0=bk5[0:NS, 288:289], scalar1=float(TOPK), scalar2=None, op0=ALU.is_ge), reads=[bb5], writes=[bf("GE")])
                E("dve", lambda h: h.tensor_tensor(out=DLT[0:NS, :], in0=MID[0:NS, :], in1=LO[0:NS, :], op=ALU.subtract), reads=[bf("MID"), bf("LO")], writes=[bf("DLT")])
                E("dve", lambda h: h.scalar_tensor_tensor(out=LO[0:NS, :], in0=DLT[0:NS, :], scalar=GE[0:NS, 0:1], in1=LO[0:NS, :], op0=ALU.mult, op1=ALU.add),
                  reads=[bf("DLT"), bf("GE")], writes=[bf("LO")])
                E("dve", lambda h: h.tensor_tensor(out=DLT[0:NS, :], in0=HI[0:NS, :], in1=MID[0:NS, :], op=ALU.subtract), reads=[bf("MID"), bf("HI")], writes=[bf("DLT")])
                E("dve", lambda h: h.scalar_tensor_tensor(out=HI[0:NS, :], in0=DLT[0:NS, :], scalar=GE[0:NS, 0:1], in1=MID[0:NS, :], op0=ALU.mult, op1=ALU.add),
                  reads=[bf("DLT"), bf("GE"), bf("MID")], writes=[bf("HI")])
            thresh_compare(LO, MKS[:, :, :], bf("MKS"))

            ck(24)
            P.barrier()
            KVG2 = RS[:, 0:4096].rearrange("p (s c) -> p s c", c=1024)

            def kvg(slot):
                return (KVG[:, slot, :] if slot < 3 else KVG2[:, slot - 3, :]), bf(f"KVG{slot}")
            for s_ in range(NSEQ):
                ob, obb = pb[2], Bpb[2]
                lb, lbb = pb[3], Bpb[3]

                def st_g(pg, s_=s_):
                    if pg >= NPG:
                        return
                    gv, gb_ = kvg(pg % 7)
                    col = s_ * NPG + pg
                    E("pool", lambda h: h.indirect_dma_start(out=gv, out_offset=None, in_=ckv_d,
                                                             in_offset=bass.IndirectOffsetOnAxis(ap=IDX[:, col:col + 1], axis=0)),
                      reads=[bf("IDX")], writes=[gb_], dma_key=f"kvg{pg % 7}")

                def st_a(pg, s_=s_):
                    if pg == NPG:
                        return
                    sl = pg % 4
                    gv, gb_ = kvg(pg % 7)
                    E("dve", lambda h: h.tensor_copy(out=KBF[:, sl, :], in_=gv[:, 0:512]), reads=[gb_], writes=[bf(f"KBF{sl}")])
                    E("act", lambda h: h.copy(out=VBF[:, sl, :], in_=gv[:, 512:1024]), reads=[gb_], writes=[bf(f"VBF{sl}")])
                    tbk, tbb = next_tb()
                    for g in range(NKV):
                        E("pe", lambda h, g=g: h.matmul(tbk[:, g * 128:(g + 1) * 128], lhsT=KBF[:, sl, g * HD:(g + 1) * HD], rhs=identb, start=True, stop=True),
                          reads=[bf(f"KBF{sl}")] + CONSTS, writes=[tbb])
                    E("act", lambda h: h.copy(out=KTP[:, sl, :], in_=tbk[:, 0:512]), reads=[tbb], writes=[bf(f"KTP{sl}")])

                def st_b(pg, s_=s_):
                    new = (pg == NPG)
                    sl = pg % 4
                    sbk, sbb = pb[pg % 2], Bpb[pg % 2]
                    np_ = TS if new else 128
                    pts = PTSS[:, pg % 2, :]
                    ptsb = bf(f"PTS{pg % 2}")
                    for g in range(NKV):
                        if new:
                            lhs = KT[:, g, T + s_ * TS:T + (s_ + 1) * TS]
                            rd = [bf("KT")]
                        else:
                            lhs = KTP[:, sl, g * 128:(g + 1) * 128]
                            rd = [bf(f"KTP{sl}")]
                        E("pe", lambda h, g=g, lhs=lhs: h.matmul(sbk[0:np_, g * 16:(g + 1) * 16], lhsT=lhs,
                                                               rhs=QS[:, s_, 4 * g:4 * g + 4, :].rearrange("p a q -> p (a q)"), start=True, stop=True),
                          reads=rd + [bf("QS")], writes=[sbb])
                    E("act", lambda h: h.activation(out=PSS[0:np_, :], in_=sbk[0:np_, 0:64], func=AF.Exp, scale=scale), reads=[sbb], writes=[bf("PSS")])
                    E("dve", lambda h: h.tensor_tensor(
                        out=pts[0:np_, :].rearrange("p (a q) -> p a q", q=TS), in0=PSS[0:np_, :].rearrange("p (a q) -> p a q", q=TS),
                        in1=MKS[0:np_, pg, s_ * TS:(s_ + 1) * TS].unsqueeze(1).to_broadcast([np_, 16, TS]), op=ALU.mult),
                      reads=[bf("PSS"), bf("MKS")], writes=[ptsb])

                def st_c(pg, s_=s_):
                    new = (pg == NPG)
                    sl = pg % 4
                    np_ = TS if new else 128
                    pts = PTSS[:, pg % 2, :]
                    ptsb = bf(f"PTS{pg % 2}")
                    for g in range(NKV):
                        if new:
                            lhs = VS[0:TS, s_, g * HD:(g + 1) * HD]
                            rd = [bf("VS")]
                        else:
                            lhs = VBF[:, sl, g * HD:(g + 1) * HD]
                            rd = [bf(f"VBF{sl}")]
                        E("pe", lambda h, g=g, lhs=lhs: h.matmul(ob[:, g * 16:(g + 1) * 16], lhsT=lhs, rhs=pts[0:np_, g * 16:(g + 1) * 16],
                                                               start=(pg == 0), stop=new),
                          reads=rd + [ptsb], writes=[obb])
                    E("pe", lambda h: h.matmul(lb[:, 0:64], lhsT=onesb[0:np_, :], rhs=pts[0:np_, :], start=(pg == 0), stop=new),
                      reads=[ptsb] + CONSTS, writes=[lbb])

                NP1 = NPG + 1
                GA = 4
                for step in range(NP1 + 2 + GA):
                    if step < NP1:
                        st_g(step)
                    if GA <= step < NP1 + GA:
                        st_a(step - GA)
                    if GA + 1 <= step <= NP1 + GA:
                        st_b(step - GA - 1)
                    if step >= GA + 2:
                        st_c(step - GA - 2)
                E("act", lambda h, lb=lb: h.activation(out=RLS, in_=lb[:, 0:64], func=AF.Ln), reads=[lbb], writes=[bf("RLS")])
                E("act", lambda h: h.activation(out=RLS, in_=RLS, func=AF.Exp, scale=-1.0), writes=[bf("RLS")])
                E("dve", lambda h, ob=ob, s_=s_: h.tensor_tensor(out=OT[:, :, TB + s_ * TS:TB + (s_ + 1) * TS],
                                                                in0=ob[:, 0:64].rearrange("p (a q) -> p a q", q=TS),
                                                                in1=RLS.rearrange("p (a q) -> p a q", q=TS), op=ALU.mult),
                  reads=[obb, bf("RLS")], writes=[bf("OT")])
        elif last:
            E("dve", lambda h: h.memset(OT[:, :, TB:NTB], 0.0), writes=[bf("OT")])

        if last:
            P.barrier()
        RA_P1 = [bf(n) for n in ("MASKT", "ISC", "MB", "ROPE", "RTMP0", "RTMP1")]
        RC_P1 = [bf(n) for n in ("IQT", "QB0", "QB1", "PTT0", "PTT1", "PTT2", "PTT3", "LNL0", "LNL1", "VTB", "IKD0", "IKD1", "IKTOK0", "IKTOK1")]
        ck(9)
        asrc = lambda k, c0, n: AT[:, k, c0:c0 + n]
        osrc = lambda k, c0, n: OT[:, k, c0:c0 + n]
        for j in range(KC):
            def epi_ga(si, c0, n, bank, bb):
                E("act", lambda h: h.activation(out=TB1[:, c0:c0 + n], in_=bank[:, 0:n], func=AF.Sigmoid), reads=[bb], writes=[bf("TB1")])
            proj(w_in_d[TI_GA + j], KC, 128, xsrc, XB, segs, epi_ga)

            def epi_ya(si, c0, n, bank, bb):
                E("dve", lambda h: h.tensor_tensor(out=TB1[:, c0:c0 + n], in0=bank[:, 0:n], in1=TB1[:, c0:c0 + n], op=ALU.mult), reads=[bb], writes=[bf("TB1")])
            proj(w_a_d[j], 8, 128, asrc, [bf("AT")], segs, epi_ya)

            def epi_gb(si, c0, n, bank, bb):
                E("act", lambda h: h.activation(out=TB2[:, c0:c0 + n], in_=bank[:, 0:n], func=AF.Sigmoid), reads=[bb], writes=[bf("TB2")])
            proj(w_in_d[TI_GB + j], KC, 128, xsrc, XB, segs, epi_gb)

            def epi_yb(si, c0, n, bank, bb, j=j):
                E("dve", lambda h: h.tensor_tensor(out=TB2[:, c0:c0 + n], in0=bank[:, 0:n], in1=TB2[:, c0:c0 + n], op=ALU.mult), reads=[bb], writes=[bf("TB2")])
                E("dve", lambda h: h.tensor_tensor(out=MT[:, j, c0:c0 + n], in0=TB1[:, c0:c0 + n], in1=TB2[:, c0:c0 + n], op=ALU.add),
                  reads=[bf("TB1"), bf("TB2")], writes=[bf("MT"), bf("ACTT")] + RC_P1)
            proj(w_at_d[j], KC, 128, osrc, [bf("OT")], segs, epi_yb)

        pass

        def proj_to_tok(wsel, kc, srcf, src_bufs, first):
            for jg in range(4):
                for jj in range(4):
                    j = jg * 4 + jj
                    def epi_s(si, c0, n, bank, bb, jj=jj):
                        E("act", lambda h: h.copy(out=STG4[:, jj, c0:c0 + n], in_=bank[:, 0:n]), reads=[bb], writes=[bf(f"STG{jj}")])
                    proj(wsel(j), kc, 128, srcf, src_bufs, segs, epi_s)
                for tt in range(ntiles):
                    _, nr = tile_rows(tt)
                    lc0 = tt * 128
                    bank, bb = next_bank()
                    for jj in range(4):
                        E("pe", lambda h, bank=bank, jj=jj, nr=nr, lc0=lc0: h.transpose(bank[0:nr, jj * 128:(jj + 1) * 128], STG4[:, jj, lc0:lc0 + nr], identf),
                          reads=[bf(f"STG{jj}")] + CONSTS, writes=[bb])
                    tv = TOK[0:nr, tt, jg * 512:(jg + 1) * 512]
                    if first:
                        E("dve", lambda h, bank=bank, nr=nr, tv=tv: h.scalar_tensor_tensor(out=tv, in0=tv, scalar=ALPHA, in1=bank[0:nr, 0:512], op0=ALU.mult, op1=ALU.add),
                          reads=[bb], writes=[bf(f"TOK{tt}")])
                    else:
                        E("dve", lambda h, bank=bank, nr=nr, tv=tv: h.tensor_tensor(out=tv, in0=tv, in1=bank[0:nr, 0:512], op=ALU.add),
                          reads=[bb], writes=[bf(f"TOK{tt}")])

        def ln_tok(which):
            E("sp", lambda h: h.dma_start(out=LNB[:, 0, :], in_=ln_d[2 * which].partition_broadcast(128)), writes=[bf("OT")], dma_key="lnb")
            E("sp", lambda h: h.dma_start(out=LNB[:, 1, :], in_=ln_d[2 * which + 1].partition_broadcast(128)), writes=[bf("OT")], dma_key="lnb")
            for tt in range(ntiles):
                _, nr = tile_rows(tt)
                tb_ = bf(f"TOK{tt}")
                tv = TOK[:, tt, :]
                MV, RSv = layer_norm_rows(tv, nr, D, tb_, "ln")
                E("dve", lambda h, nr=nr, tv=tv: h.tensor_scalar(out=tv[0:nr, :], in0=tv[0:nr, :], scalar1=MV[0:nr, 0:1], scalar2=RSv[0:nr, 0:1], op0=ALU.subtract, op1=ALU.mult),
                  reads=[bf("MV"), bf("RSv")], writes=[tb_])
                E("dve", lambda h, nr=nr, tv=tv: h.tensor_tensor(out=tv[0:nr, :], in0=tv[0:nr, :], in1=LNB[0:nr, 0, :], op=ALU.mult), reads=[bf("OT")], writes=[tb_])
                E("dve", lambda h, nr=nr, tv=tv: h.tensor_tensor(out=tv[0:nr, :], in0=tv[0:nr, :], in1=LNB[0:nr, 1, :], op=ALU.add), reads=[bf("OT")], writes=[tb_])

        ck(10)
        for tt in range(ntiles):
            r0, nr = tile_rows(tt)
            E("sp", lambda h, tt=tt, r0=r0, nr=nr: h.dma_start(out=TOK[0:nr, tt, :], in_=xtok_d[r0:r0 + nr, :]), writes=[bf(f"TOK{tt}")] + RA_P1, dma_key=f"tok{tt}", acc=True)
        msrc = lambda k, c0, n: MT[:, k, c0:c0 + n]
        proj_to_tok(lambda j: w_mix_d[j], KC, msrc, [bf("MT")], True)
        ln_tok(0)
        HBv = RD[:, 0:1024].bitcast(BF16)
        for tt in range(ntiles):
            _, nr = tile_rows(tt)
            lc0 = tt * 128
            E("act", lambda h, nr=nr, tt=tt: h.copy(out=HBv[0:nr, :], in_=TOK[0:nr, tt, :]), reads=[bf(f"TOK{tt}")],
              writes=[bf("STG0"), bf("STG1"), bf("STG2"), bf("STG3")])
            for jg in range(4):
                tbk, tbb = next_tb()
                for jj in range(4):
                    j = jg * 4 + jj
                    E("pe", lambda h, tbk=tbk, jj=jj, j=j, nr=nr: h.matmul(tbk[:, jj * 128:jj * 128 + nr], lhsT=HBv[0:nr, j * 128:(j + 1) * 128], rhs=identb[0:nr, 0:nr], start=True, stop=True),
                      reads=[bf("STG0")] + CONSTS, writes=[tbb])
                E("act", lambda h, tbk=tbk, jg=jg, nr=nr, lc0=lc0: h.copy(out=XT[:, jg * 4:jg * 4 + 4, lc0:lc0 + nr],
                                                                        in_=tbk[:, 0:512].rearrange("p (a t) -> p a t", t=128)[:, :, 0:nr]),
                  reads=[tbb], writes=[bf("XT")])
        pass

        ck(11)
        UB = TA[:, 0:TB + 2]
        UBS = TA[:, 520:544].rearrange("p (s t) -> p s t", t=6)
        hsrc = lambda k, c0, n: XT[:, k, c0:c0 + n]
        for hf in range(2):
            for c in range(22):
                cc = hf * 22 + c
                E("act", lambda h, cc=cc: h.copy(out=UB[:, 0:2], in_=UH[:, cc, :]), reads=[bf("UH")], writes=[bf("TA")])
                if last:
                    E("act", lambda h, cc=cc: h.copy(out=UBS[:, :, 0:2], in_=SCF[:, cc, :, :]), reads=[bf("SCF")], writes=[bf("TA")])

                def epi_u(si, c0, n, bank, bb):
                    if si == 0:
                        E("act", lambda h: h.copy(out=UB[:, 2:2 + TB], in_=bank[:, 0:TB]), reads=[bb], writes=[bf("TA")])
                    else:
                        E("act", lambda h: h.copy(out=UBS[:, :, 2:6], in_=bank[:, 0:NS].rearrange("p (s t) -> p s t", t=TS)), reads=[bb], writes=[bf("TA")])
                proj(w_up_d[cc], KC, 128, hsrc, XB, segs, epi_u)
                E("dve", lambda h, cc=cc: h.tensor_scalar(out=TB1[:, 0:TB], in0=UB[:, 2:2 + TB], scalar1=CFW[:, cc, 2:3], scalar2=CFB[:, cc:cc + 1], op0=ALU.mult, op1=ALU.add),
                  reads=[bf("TA"), bf("SM")], writes=[bf("TB1")])
                E("dve", lambda h, cc=cc: h.scalar_tensor_tensor(out=TB1[:, 0:TB], in0=UB[:, 1:1 + TB], scalar=CFW[:, cc, 1:2], in1=TB1[:, 0:TB], op0=ALU.mult, op1=ALU.add),
                  reads=[bf("TA"), bf("SM")], writes=[bf("TB1")])
                E("dve", lambda h, cc=cc: h.scalar_tensor_tensor(out=TB1[:, 0:TB], in0=UB[:, 0:TB], scalar=CFW[:, cc, 0:1], in1=TB1[:, 0:TB], op0=ALU.mult, op1=ALU.add),
                  reads=[bf("TA"), bf("SM")], writes=[bf("TB1")])
                if last:
                    E("dve", lambda h, cc=cc: h.tensor_scalar(out=ACCS, in0=UBS[:, :, 2:6], scalar1=CFW[:, cc, 2:3], scalar2=CFB[:, cc:cc + 1], op0=ALU.mult, op1=ALU.add),
                      reads=[bf("TA"), bf("SM")], writes=[bf("TB1")])
                    E("dve", lambda h, cc=cc: h.scalar_tensor_tensor(out=ACCS, in0=UBS[:, :, 1:5], scalar=CFW[:, cc, 1:2], in1=ACCS, op0=ALU.mult, op1=ALU.add),
                      reads=[bf("TA"), bf("SM")], writes=[bf("TB1")])
                    E("dve", lambda h, cc=cc: h.scalar_tensor_tensor(out=ACCS, in0=UBS[:, :, 0:4], scalar=CFW[:, cc, 0:1], in1=ACCS, op0=ALU.mult, op1=ALU.add),
                      reads=[bf("TA"), bf("SM")], writes=[bf("TB1")])
                    E("act", lambda h, cc=cc: h.copy(out=CFS[:, cc, :, :], in_=UBS[:, :, 4:6]), reads=[bf("TA")], writes=[bf("CFS")])
                E("act", lambda h, cc=cc: h.copy(out=UH[:, cc, :], in_=UB[:, TB:TB + 2]), reads=[bf("TA")], writes=[bf("UH")])
                E("act", lambda h: h.activation(out=TB2[:, 0:ntok], in_=TB1[:, 0:ntok], func=AF.Gelu_apprx_tanh), reads=[bf("TB1")], writes=[bf("TB2")])

                def epi_g(si, c0, n, bank, bb, c=c):
                    E("dve", lambda h: h.tensor_tensor(out=ACTT[:, c, c0:c0 + n], in0=bank[:, 0:n], in1=TB2[:, c0:c0 + n], op=ALU.mult),
                      reads=[bb, bf("TB2")], writes=[bf("ACTT"), bf("MT")])
                proj(w_gate_d[cc], KC, 128, hsrc, XB, segs, epi_g)
            fsrc = lambda k, c0, n: ACTT[:, k, c0:c0 + n]
            proj_to_tok(lambda j, hf=hf: w_dn_d[j, hf], 22, fsrc, [bf("ACTT")], hf == 0)
        ln_tok(1)
        for tt in range(ntiles):
            r0, nr = tile_rows(tt)
            E("sp", lambda h, tt=tt, r0=r0, nr=nr: h.dma_start(out=y_o[r0:r0 + nr, :], in_=TOK[0:nr, tt, :]), reads=[bf(f"TOK{tt}")], dma_key=f"tok{tt}")
        P.barrier()

    try:
        for b_ in range(nblk_run):
            do_block(b_)
    except _Stop:
        pass
    E("sp", lambda h: h.dma_start(out=ca_o, in_=AH[:, :, :].rearrange("p a b -> p (a b)")), reads=[bf("AH")], dma_key="o_ca")
    E("sp", lambda h: h.dma_start(out=cf_o, in_=UH[:, :, :].rearrange("p a b -> p (a b)")), reads=[bf("UH")], dma_key="o_cf")
    E("sp", lambda h: h.dma_start(out=cas_o, in_=CAS[:, :, :, :].rearrange("p a b c -> p (a b c)")), reads=[bf("CAS")], dma_key="o_cas")
    E("sp", lambda h: h.dma_start(out=cfs_o, in_=CFS[:, :, :, :].rearrange("p a b c -> p (a b c)")), reads=[bf("CFS")], dma_key="o_cfs")

    P.finalize(st)
    st.close()
    return nc


def _tile_w(w, kc):
    K, N = w.shape
    assert K == kc * 128 and N % 128 == 0
    return np.ascontiguousarray(w.reshape(kc, 128, N // 128, 128).transpose(2, 1, 0, 3).reshape(N // 128, 128, kc * 128))


def _rope_tables():
    pos = np.concatenate([np.arange(T, dtype=np.float32), PAST + np.arange(NS, dtype=np.float32) % TS])
    pos = pos.astype(np.float32)

    def cs(half):
        inv = (np.float32(500000.0) ** (-np.arange(half, dtype=np.float32) / np.float32(half))).astype(np.float32)
        ang = (pos[:, None] * inv[None, :]).astype(np.float32)
        return np.cos(ang).astype(np.float32), np.sin(ang).astype(np.float32)

    c16, s16 = cs(16)
    c8, s8 = cs(8)
    fm = np.zeros((128, 4, NTOT), np.float32)
    fm[:, 0, :] = 1.0
    fm[:, 2, :] = 1.0
    fm[0:16, 0, :] = c16.T
    fm[16:32, 0, :] = c16.T
    fm[0:16, 1, :] = -s16.T
    fm[16:32, 1, :] = s16.T
    for o in (0, 64):
        fm[o:o + 8, 2, :] = c8.T
        fm[o + 8:o + 16, 2, :] = c8.T
        fm[o:o + 8, 3, :] = -s8.T
        fm[o + 8:o + 16, 3, :] = s8.T
    tok = np.concatenate([c8, s8], axis=1).astype(np.float32)
    return fm, tok


def _consts():
    cbf = np.zeros((128, 4, 128), np.float32)
    cbf[:, 0, :] = np.eye(128, dtype=np.float32)
    for m in range(16):
        cbf[m + 16, 1, m] = 1.0
        cbf[m, 1, m + 16] = 1.0
    for o in (0, 64):
        for m in range(8):
            cbf[o + m + 8, 2, o + m] = 1.0
            cbf[o + m, 2, o + m + 8] = 1.0
    cbf[:, 3, :] = 1.0
    cf = np.zeros((128, 3, 128), np.float32)
    cf[:, 0, :] = np.eye(128, dtype=np.float32)
    t = np.arange(128)[:, None]
    s = np.arange(128)[None, :]
    cf[:, 1, :] = np.where(s <= t, 0.0, NEGBIG).astype(np.float32)
    cf[:, 2, 0] = np.arange(128, dtype=np.float32)
    j = np.arange(4)[:, None]
    q = np.arange(4)[None, :]
    cf[0:4, 2, 1:5] = np.where(j <= q, 0.0, NEG).astype(np.float32)
    for tok in range(NS):
        cf[tok, 2, 8 + tok] = 1.0
    cf[:, 2, 32:64] = 1.0
    cf[:, 2, 64:96] = (2.0 ** -(np.arange(32, dtype=np.float64) + 1.0)).astype(np.float32)[None, :]
    return cbf.reshape(128, 512), cf.reshape(128, 384)


_PROGRAM = None


def _prepare_shared(inp):
    w_in = np.asarray(inp["w_in"][0], np.float32)
    offs = np.cumsum([0, 1024, 1024, 1024, 2048, 512, 512, 1024, 64, 16, 2048, 2048])
    oB, oC, oH, oQ, oK, oV, oIQ, oIK, oIW, oGA, oGB = offs[:11]
    cols = []
    for c in range(8):
        for base in (oC, oH, oB):
            cols.append(np.arange(base + c * 128, base + (c + 1) * 128))
    for base, n in ((oQ, 16), (oK, 4), (oV, 4), (oIQ, 8), (oGA, 16), (oGB, 16)):
        for i in range(n):
            cols.append(np.arange(base + i * 128, base + (i + 1) * 128))
    cols = np.concatenate(cols)
    assert cols.size == N_WIN_TILES * 128
    w_in_t = _tile_w(w_in[:, cols], KC)
    w_ikw = np.ascontiguousarray(w_in[:, oIK:oIK + 80].reshape(KC, 128, 80).transpose(1, 0, 2).reshape(128, KC * 80))
    w_a_t = _tile_w(np.asarray(inp["w_a_out"][0], np.float32), 8)
    w_attn_t = _tile_w(np.asarray(inp["w_attn_out"][0], np.float32), KC)
    w_mix_t = _tile_w(np.asarray(inp["w_mix_out"][0], np.float32), KC)
    w_up_t = _tile_w(np.asarray(inp["w_up"][0], np.float32), KC)
    w_gate_t = _tile_w(np.asarray(inp["w_gate"][0], np.float32), KC)
    wd = _tile_w(np.asarray(inp["w_down"][0], np.float32), NFC)
    w_down_t = np.ascontiguousarray(wd.reshape(16, 128, 2, 22 * 128).transpose(0, 2, 1, 3))
    caw = np.asarray(inp["conv_a_w"][0], np.float32)
    cfw = np.asarray(inp["conv_ffn_w"][0], np.float32)
    cfb = np.asarray(inp["conv_ffn_b"][0], np.float32)
    small = np.concatenate([
        caw.T.reshape(8, 128, 3).transpose(1, 0, 2).reshape(128, 24),
        cfw.T.reshape(NFC, 128, 3).transpose(1, 0, 2).reshape(128, NFC * 3),
        cfb.reshape(NFC, 128).T], axis=1).astype(np.float32)
    ikgb = np.stack([np.asarray(inp["idx_k_norm_g"][0], np.float32), np.asarray(inp["idx_k_norm_b"][0], np.float32)])
    lnp = np.stack([np.asarray(inp[k][0], np.float32) for k in ("ln1_g", "ln1_b", "ln2_g", "ln2_b")])
    rope_fm, rope_tok = _rope_tables()
    cbf, cf = _consts()
    d = {}
    o = 0
    for i, n in enumerate(WIN_SPLIT):
        d[f"w_in_t{i}"] = np.ascontiguousarray(w_in_t[o:o + n])
        o += n
    for i in range(2):
        d[f"w_up_t{i}"] = np.ascontiguousarray(w_up_t[22 * i:22 * (i + 1)])
        d[f"w_gate_t{i}"] = np.ascontiguousarray(w_gate_t[22 * i:22 * (i + 1)])
        d[f"w_down_t{i}"] = np.ascontiguousarray(w_down_t[:, i])
    d.update(dict(w_ikw=w_ikw, w_a_t=w_a_t, w_attn_t=w_attn_t, w_mix_t=w_mix_t, small=np.ascontiguousarray(small), ikgb=ikgb, lnp=lnp,
                rope_fm=rope_fm, rope_tok=rope_tok, cbf=cbf, cf=cf,
                cache_kv=np.concatenate([np.asarray(inp["cache_k"], np.float32).reshape(NPHYS * 128, NKV * HD),
                                         np.asarray(inp["cache_v"], np.float32).reshape(NPHYS * 128, NKV * HD)], axis=1),
                cache_ik=np.asarray(inp["cache_idx_k"], np.float32).reshape(NPHYS * 128, IDIM)))
    return d


def kernel(**inp):
    global _PROGRAM
    if _PROGRAM is None:
        _PROGRAM = build_program()
    nc = _PROGRAM
    shared = _prepare_shared(inp)
    x_prompt = np.asarray(inp["x_prompt"], np.float32)
    x_sample = np.asarray(inp["x_sample"], np.float32)
    sca_all = np.asarray(inp["state_conv_a"][0], np.float32)
    scf_all = np.asarray(inp["state_conv_ffn"][0], np.float32)
    pt_all = np.asarray(inp["page_table"], np.int32)
    in_maps = []
    for c in range(8):
        xs = x_sample[NSEQ * c:NSEQ * (c + 1)].reshape(NS, D)
        xtok = np.concatenate([x_prompt[c], xs], axis=0)
        m = dict(shared)
        m["xtok"] = np.ascontiguousarray(xtok)
        m["xT"] = np.ascontiguousarray(xtok.T)
        a = sca_all[NSEQ * c:NSEQ * (c + 1)]
        m["sca"] = np.ascontiguousarray(a.reshape(NSEQ, 2, 8, 128).transpose(3, 2, 0, 1).reshape(128, 8 * NSEQ * 2))
        f = scf_all[NSEQ * c:NSEQ * (c + 1)]
        m["scf"] = np.ascontiguousarray(f.reshape(NSEQ, 2, NFC, 128).transpose(3, 2, 0, 1).reshape(128, NFC * NSEQ * 2))
        m["pt"] = np.ascontiguousarray(pt_all[NSEQ * c:NSEQ * (c + 1)].reshape(NSEQ * NPG))
        in_maps.append(m)
    res = run_bass_kernel_spmd(nc, in_maps, core_ids=list(range(8)))
    R = res.results
    y_p = np.stack([R[c]["y_o"][:T] for c in range(8)])
    y_s = np.concatenate([R[c]["y_o"][T:].reshape(NSEQ, TS, D) for c in range(8)])

    def tok_major(name, c):
        return R[c][name].transpose(2, 0, 1)

    k_p = np.stack([tok_major("kT_o", c)[:T] for c in range(8)])[None]
    v_p = np.stack([tok_major("vT_o", c)[:T] for c in range(8)])[None]
    k_s = np.concatenate([tok_major("kT_o", c)[T:].reshape(NSEQ, TS, NKV, HD) for c in range(8)])[None]
    v_s = np.concatenate([tok_major("vT_o", c)[T:].reshape(NSEQ, TS, NKV, HD) for c in range(8)])[None]
    ik_p = np.stack([R[c]["ik_o"][:T] for c in range(8)])[None]
    ik_s = np.concatenate([R[c]["ik_o"][T:].reshape(NSEQ, TS, IDIM) for c in range(8)])[None]
    ca_p = np.stack([R[c]["ca_o"].reshape(128, 8, 2).transpose(2, 1, 0).reshape(2, DCONV) for c in range(8)])[None]
    cf_p = np.stack([R[c]["cf_o"].reshape(128, NFC, 2).transpose(2, 1, 0).reshape(2, DFF) for c in range(8)])[None]
    ca_s = np.concatenate([R[c]["cas_o"].reshape(128, 8, NSEQ, 2).transpose(2, 3, 1, 0).reshape(NSEQ, 2, DCONV) for c in range(8)])[None]
    cf_s = np.concatenate([R[c]["cfs_o"].reshape(128, NFC, NSEQ, 2).transpose(2, 3, 1, 0).reshape(NSEQ, 2, DFF) for c in range(8)])[None]
    outs = (y_p, y_s, k_p, v_p, ik_p, ca_p, cf_p, k_s, v_s, ik_s, ca_s, cf_s)
    return tuple(np.ascontiguousarray(o, dtype=np.float32) for o in outs)
```
